# Optimizing a Trainium2 kernel written in Bass

```python
import math
import jax, jax.numpy as jnp
from jax import lax
import numpy as np

D_MODEL = 2048
BATCH = 16
SEQ = 2048
DEPTH = 4

N_BRANCH = 4
BRANCH_WIDTH = D_MODEL // 4
ROPE_THETA = 500000.0
Q_BLOCK = 128
NORM_EPS = 1e-5

DIFF_HEADS = 4
DIFF_HEAD_DIM = BRANCH_WIDTH // (2 * DIFF_HEADS)
DIFF_V_DIM = 2 * DIFF_HEAD_DIM
DIFF_ROT = DIFF_HEAD_DIM // 4

GLA_HEADS = 4
GLA_DK = BRANCH_WIDTH // 2 // GLA_HEADS
GLA_DV = BRANCH_WIDTH // GLA_HEADS
GLA_GATE_RANK = 16
GLA_TAU = 16.0
GLA_CHUNK = 64

MLA_HEADS = 4
MLA_NOPE = 128
MLA_ROPE = 64
MLA_V = BRANCH_WIDTH // MLA_HEADS
MLA_Q_RANK = 384
MLA_KV_RANK = 128

SSD_HEADDIM = 64
SSD_INNER = BRANCH_WIDTH
SSD_HEADS = SSD_INNER // SSD_HEADDIM
SSD_GROUPS = 2
SSD_STATE = 128
SSD_CONV = 4
SSD_CONV_DIM = SSD_INNER + 2 * SSD_GROUPS * SSD_STATE
SSD_CHUNK = 128

D_FF = 256 * ((8 * D_MODEL // 3 + 255) // 256)

ALPHA = (2 * DEPTH) ** 0.25
BETA = (8 * DEPTH) ** -0.25

IN_SPLITS = [
    DIFF_HEADS * 2 * DIFF_HEAD_DIM, DIFF_HEADS * 2 * DIFF_HEAD_DIM, DIFF_HEADS * DIFF_V_DIM,
    GLA_HEADS * GLA_DK, GLA_HEADS * GLA_DK, GLA_HEADS * GLA_DV, GLA_GATE_RANK, BRANCH_WIDTH,
    MLA_Q_RANK, MLA_KV_RANK, MLA_ROPE,
    SSD_INNER, SSD_CONV_DIM, SSD_HEADS,
    N_BRANCH * D_MODEL,
]
N_IN = sum(IN_SPLITS)

kernel_name = "hybrid_diff_gla_mla_ssd_macaron_deepnorm"


def split_cols(t, sizes):
    return jnp.split(t, [int(i) for i in np.cumsum(sizes)[:-1]], axis=-1)


def layer_norm(x, g, b):
    xf = x.astype(jnp.float32)
    mu = jnp.mean(xf, axis=-1, keepdims=True)
    xc = xf - mu
    var = jnp.mean(xc * xc, axis=-1, keepdims=True)
    return (xc * lax.rsqrt(var + NORM_EPS) * g + b).astype(x.dtype)


def rms_norm(x, g):
    xf = x.astype(jnp.float32)
    return (xf * lax.rsqrt(jnp.mean(xf * xf, axis=-1, keepdims=True) + NORM_EPS) * g).astype(x.dtype)


def swiglu(h, w_gu, w_down):
    gate, up = jnp.split(h @ w_gu, 2, axis=-1)
    return (jax.nn.silu(gate) * up) @ w_down


def rope_tables(positions, rot_dim):
    inv = ROPE_THETA ** (-jnp.arange(0, rot_dim, 2, dtype=jnp.float32) / rot_dim)
    ang = positions.astype(jnp.float32)[..., None] * inv
    return jnp.cos(ang), jnp.sin(ang)


def apply_rope(x, cos, sin):
    x1, x2 = jnp.split(x, 2, axis=-1)
    return jnp.concatenate([x1 * cos - x2 * sin, x1 * sin + x2 * cos], axis=-1).astype(x.dtype)


def partial_rope(x, cos, sin, rot):
    return jnp.concatenate([apply_rope(x[..., :rot], cos, sin), x[..., rot:]], axis=-1)


def causal_softmax_attend(q, k, v, scale):
    bsz, m, nh, s, dk = q.shape
    nb = s // Q_BLOCK
    q_blocks = jnp.moveaxis(q.reshape(bsz, m, nh, nb, Q_BLOCK, dk), 3, 0)
    key_pos = jnp.arange(s)

    def one_block(args):
        qb, i = args
        sc = jnp.einsum('bmhqd,bmhkd->bmhqk', qb, k).astype(jnp.float32) * scale
        q_pos = i * Q_BLOCK + jnp.arange(Q_BLOCK)
        sc = jnp.where(q_pos[:, None] >= key_pos[None, :], sc, -jnp.inf)
        p = jax.nn.softmax(sc, axis=-1).astype(v.dtype)
        return jnp.einsum('bmhqk,bhkd->bmhqd', p, v)

    o = lax.map(one_block, (q_blocks, jnp.arange(nb)))
    return jnp.moveaxis(o, 0, 3).reshape(bsz, m, nh, s, v.shape[-1])


def chunk_scan(decay, inc):
    def step(state, xs):
        d, u = xs
        return (d * state + u).astype(state.dtype), state
    _, prev = lax.scan(step, jnp.zeros_like(inc[0]), (decay, inc))
    return prev


def diff_attention_branch(q, k, v, lam_params, subln_g, cos, sin, layer):
    bsz, s, _ = q.shape
    q = q.reshape(bsz, s, DIFF_HEADS, 2, DIFF_HEAD_DIM).transpose(0, 3, 2, 1, 4)
    k = k.reshape(bsz, s, DIFF_HEADS, 2, DIFF_HEAD_DIM).transpose(0, 3, 2, 1, 4)
    c, sn = cos[:, None, None], sin[:, None, None]
    q = partial_rope(q, c, sn, DIFF_ROT)
    k = partial_rope(k, c, sn, DIFF_ROT)
    v = v.reshape(bsz, s, DIFF_HEADS, DIFF_V_DIM).transpose(0, 2, 1, 3)
    o = causal_softmax_attend(q, k, v, DIFF_HEAD_DIM ** -0.5)
    lam_init = 0.8 - 0.6 * math.exp(-0.3 * layer)
    lp = lam_params.astype(jnp.float32)
    lam = jnp.exp(jnp.sum(lp[0] * lp[1])) - jnp.exp(jnp.sum(lp[2] * lp[3])) + lam_init
    o = o[:, 0] - lam.astype(o.dtype) * o[:, 1]
    o = rms_norm(o, subln_g) * (1.0 - lam_init)
    return o.transpose(0, 2, 1, 3).reshape(bsz, s, BRANCH_WIDTH)


def gla_branch(q, k, v, g_low, r, w_gate2, b_gate, norm_g):
    bsz, s, _ = q.shape
    nc = s // GLA_CHUNK

    def heads(t, d):
        return t.reshape(bsz, s, GLA_HEADS, d).transpose(0, 2, 1, 3).reshape(bsz, GLA_HEADS, nc, GLA_CHUNK, d)

    g = jax.nn.log_sigmoid((g_low @ w_gate2 + b_gate).astype(jnp.float32)) / GLA_TAU
    qc = heads(q, GLA_DK) * GLA_DK ** -0.5
    kc = heads(k, GLA_DK)
    vc = heads(v, GLA_DV)
    b = jnp.cumsum(heads(g, GLA_DK), axis=3)
    b_last = b[:, :, :, -1:, :]
    q_t = qc * jnp.exp(b)
    k_t = kc * jnp.exp(-b)
    mask = jnp.tril(jnp.ones((GLA_CHUNK, GLA_CHUNK), dtype=bool))
    att = jnp.where(mask, jnp.einsum('bhnid,bhnjd->bhnij', q_t, k_t), 0.0)
    o_intra = jnp.einsum('bhnij,bhnje->bhnie', att, vc)
    inc = jnp.einsum('bhnjd,bhnje->bhnde', kc * jnp.exp(b_last - b), vc)
    decay = jnp.swapaxes(jnp.exp(b_last), -1, -2)
    prev = jnp.moveaxis(chunk_scan(jnp.moveaxis(decay, 2, 0), jnp.moveaxis(inc, 2, 0)), 0, 2)
    o = o_intra + jnp.einsum('bhnid,bhnde->bhnie', q_t, prev)
    o = rms_norm(o, norm_g).reshape(bsz, GLA_HEADS, s, GLA_DV).transpose(0, 2, 1, 3).reshape(bsz, s, BRANCH_WIDTH)
    return (o * jax.nn.silu(r)).astype(q.dtype)


def mla_branch(c_q, c_kv, k_r, cos, sin, q_norm_g, w_uq, kv_norm_g, w_ukv):
    bsz, s, _ = c_q.shape
    q = (rms_norm(c_q, q_norm_g) @ w_uq).reshape(bsz, s, MLA_HEADS, MLA_NOPE + MLA_ROPE)
    q = jnp.concatenate([q[..., :MLA_NOPE], apply_rope(q[..., MLA_NOPE:], cos[:, :, None], sin[:, :, None])], axis=-1)
    kv = (rms_norm(c_kv, kv_norm_g) @ w_ukv).reshape(bsz, s, MLA_HEADS, MLA_NOPE + MLA_V)
    k_nope, v = kv[..., :MLA_NOPE], kv[..., MLA_NOPE:]
    k_rope = jnp.broadcast_to(apply_rope(k_r, cos, sin)[:, :, None, :], (bsz, s, MLA_HEADS, MLA_ROPE))
    k = jnp.concatenate([k_nope, k_rope], axis=-1)
    o = causal_softmax_attend(q.transpose(0, 2, 1, 3)[:, None], k.transpose(0, 2, 1, 3)[:, None],
                              v.transpose(0, 2, 1, 3), (MLA_NOPE + MLA_ROPE) ** -0.5)[:, 0]
    return o.transpose(0, 2, 1, 3).reshape(bsz, s, BRANCH_WIDTH)


def causal_depthwise_conv(x, w, b):
    kw, ch = w.shape
    out = lax.conv_general_dilated(x, w[:, None, :].astype(x.dtype), window_strides=(1,), padding=[(kw - 1, 0)],
                                   dimension_numbers=('NWC', 'WIO', 'NWC'), feature_group_count=ch)
    return out + b


def ssd_chunked(x, dt, a, bm, cm):
    bsz, s, nh, p = x.shape
    g, n = bm.shape[-2:]
    hg = nh // g
    nc = s // SSD_CHUNK
    xdt = (x * dt[..., None]).reshape(bsz, nc, SSD_CHUNK, g, hg, p)
    a_cum = jnp.cumsum((dt * a).reshape(bsz, nc, SSD_CHUNK, g, hg), axis=2)
    bc = bm.reshape(bsz, nc, SSD_CHUNK, g, n)
    cc = cm.reshape(bsz, nc, SSD_CHUNK, g, n)
    mask = jnp.tril(jnp.ones((SSD_CHUNK, SSD_CHUNK), dtype=bool))[:, :, None, None]
    seg = a_cum[:, :, :, None] - a_cum[:, :, None, :]
    lmat = jnp.exp(jnp.where(mask, seg, -jnp.inf))
    cb = jnp.einsum('bnigs,bnjgs->bnijg', cc, bc)
    y_diag = jnp.einsum('bnijg,bnijgh,bnjghp->bnighp', cb, lmat, xdt)
    decay_states = jnp.exp(a_cum[:, :, -1:] - a_cum)
    inc = jnp.einsum('bnjgs,bnjgh,bnjghp->bnghps', bc, decay_states, xdt)
    chunk_decay = jnp.exp(a_cum[:, :, -1])[..., None, None]
    prev = jnp.moveaxis(chunk_scan(jnp.moveaxis(chunk_decay, 1, 0), jnp.moveaxis(inc, 1, 0)), 0, 1)
    y_off = jnp.einsum('bnigs,bnghps,bnigh->bnighp', cc, prev, jnp.exp(a_cum))
    return (y_diag + y_off).reshape(bsz, s, nh, p)


def ssd_branch(z, xbc, dt, conv_w, conv_b, dt_bias, a_log, d_skip, norm_g):
    bsz, s, _ = z.shape
    xbc = jax.nn.silu(causal_depthwise_conv(xbc, conv_w, conv_b))
    xs, bm, cm = split_cols(xbc, [SSD_INNER, SSD_GROUPS * SSD_STATE, SSD_GROUPS * SSD_STATE])
    xs = xs.reshape(bsz, s, SSD_HEADS, SSD_HEADDIM)
    dt = jax.nn.softplus((dt + dt_bias).astype(jnp.float32))
    a = -jnp.exp(a_log.astype(jnp.float32))
    y = ssd_chunked(xs, dt, a, bm.reshape(bsz, s, SSD_GROUPS, SSD_STATE), cm.reshape(bsz, s, SSD_GROUPS, SSD_STATE))
    y = y + d_skip[:, None] * xs
    y = (y.reshape(bsz, s, SSD_INNER) * jax.nn.silu(z)).reshape(bsz, s, SSD_GROUPS, SSD_INNER // SSD_GROUPS)
    y = rms_norm(y, norm_g.reshape(SSD_GROUPS, SSD_INNER // SSD_GROUPS)).reshape(bsz, s, SSD_INNER)
    return y.astype(z.dtype)


def hybrid_mixer(h, layer, cos_d, sin_d, cos_m, sin_m, w_in, b_gate, diff_lambda, diff_subln_g,
                 gla_w_gate2, gla_b_gate, gla_norm_g, mla_q_norm_g, mla_w_uq, mla_kv_norm_g, mla_w_ukv,
                 ssd_conv_w, ssd_conv_b, ssd_dt_bias, ssd_a_log, ssd_d, ssd_norm_g, w_branch, w_out):
    bsz, s, _ = h.shape
    (a_q, a_k, a_v, b_q, b_k, b_v, b_glow, b_r, c_q, c_kv, c_kr,
     d_z, d_xbc, d_dt, gate_logits) = split_cols(h @ w_in, IN_SPLITS)
    o_a = diff_attention_branch(a_q, a_k, a_v, diff_lambda, diff_subln_g, cos_d, sin_d, layer)
    o_b = gla_branch(b_q, b_k, b_v, b_glow, b_r, gla_w_gate2, gla_b_gate, gla_norm_g)
    o_c = mla_branch(c_q, c_kv, c_kr, cos_m, sin_m, mla_q_norm_g, mla_w_uq, mla_kv_norm_g, mla_w_ukv)
    o_d = ssd_branch(d_z, d_xbc, d_dt, ssd_conv_w, ssd_conv_b, ssd_dt_bias, ssd_a_log, ssd_d, ssd_norm_g)
    gates = jax.nn.sigmoid((gate_logits + b_gate).astype(jnp.float32)).astype(h.dtype)
    gates = gates.reshape(bsz, s, N_BRANCH, D_MODEL)
    outs = (o_a, o_b, o_c, o_d)
    merged = gates[:, :, 0] * (o_a @ w_branch[0])
    for i in range(1, N_BRANCH):
        merged = merged + gates[:, :, i] * (outs[i] @ w_branch[i])
    return merged @ w_out


def setup_inputs(seed: int = 0) -> dict:
    key = jax.random.key(seed)
    ks = jax.random.split(key, 32)
    f32 = jnp.float32

    def nrm(k, shape, fan_in, gain=1.0):
        return jax.random.normal(k, shape, f32) * (gain * fan_in ** -0.5)

    def gain(k, shape):
        return 1.0 + 0.02 * jax.random.normal(k, shape, f32)

    def small(k, shape):
        return 0.02 * jax.random.normal(k, shape, f32)

    x = jax.random.normal(ks[0], (BATCH, SEQ, D_MODEL), f32)
    positions = (jax.random.randint(ks[1], (BATCH, 1), 0, 4096) + jnp.arange(SEQ)[None, :]).astype(jnp.int32)
    dt0 = jnp.exp(jax.random.uniform(ks[22], (DEPTH, SSD_HEADS), f32, math.log(1e-3), math.log(1e-1)))
    return {
        "x": x,
        "positions": positions,
        "ln_g": gain(ks[2], (DEPTH, 3, D_MODEL)),
        "ln_b": small(ks[3], (DEPTH, 3, D_MODEL)),
        "ffn1_w_gu": nrm(ks[4], (DEPTH, D_MODEL, 2 * D_FF), D_MODEL),
        "ffn1_w_down": nrm(ks[5], (DEPTH, D_FF, D_MODEL), D_FF, BETA),
        "ffn2_w_gu": nrm(ks[6], (DEPTH, D_MODEL, 2 * D_FF), D_MODEL),
        "ffn2_w_down": nrm(ks[7], (DEPTH, D_FF, D_MODEL), D_FF, BETA),
        "w_in": nrm(ks[8], (DEPTH, D_MODEL, N_IN), D_MODEL),
        "b_gate": small(ks[9], (DEPTH, N_BRANCH * D_MODEL)),
        "diff_lambda": 0.1 * jax.random.normal(ks[10], (DEPTH, 4, DIFF_HEAD_DIM), f32),
        "diff_subln_g": gain(ks[11], (DEPTH, DIFF_V_DIM)),
        "gla_w_gate2": nrm(ks[12], (DEPTH, GLA_GATE_RANK, GLA_HEADS * GLA_DK), GLA_GATE_RANK),
        "gla_b_gate": small(ks[13], (DEPTH, GLA_HEADS * GLA_DK)),
        "gla_norm_g": gain(ks[14], (DEPTH, GLA_DV)),
        "mla_q_norm_g": gain(ks[15], (DEPTH, MLA_Q_RANK)),
        "mla_w_uq": nrm(ks[16], (DEPTH, MLA_Q_RANK, MLA_HEADS * (MLA_NOPE + MLA_ROPE)), MLA_Q_RANK),
        "mla_kv_norm_g": gain(ks[17], (DEPTH, MLA_KV_RANK)),
        "mla_w_ukv": nrm(ks[18], (DEPTH, MLA_KV_RANK, MLA_HEADS * (MLA_NOPE + MLA_V)), MLA_KV_RANK),
        "ssd_conv_w": jax.random.uniform(ks[19], (DEPTH, SSD_CONV, SSD_CONV_DIM), f32, -SSD_CONV ** -0.5, SSD_CONV ** -0.5),
        "ssd_conv_b": small(ks[20], (DEPTH, SSD_CONV_DIM)),
        "ssd_dt_bias": dt0 + jnp.log(-jnp.expm1(-dt0)),
        "ssd_a_log": jnp.log(jax.random.uniform(ks[23], (DEPTH, SSD_HEADS), f32, 1.0, 16.0)),
        "ssd_d": gain(ks[24], (DEPTH, SSD_HEADS)),
        "ssd_norm_g": gain(ks[25], (DEPTH, SSD_INNER)),
        "w_branch": nrm(ks[26], (DEPTH, N_BRANCH, BRANCH_WIDTH, D_MODEL), BRANCH_WIDTH),
        "w_out": nrm(ks[27], (DEPTH, D_MODEL, D_MODEL), D_MODEL, BETA),
    }


def reference(x, positions, ln_g, ln_b, ffn1_w_gu, ffn1_w_down, ffn2_w_gu, ffn2_w_down, w_in, b_gate,
              diff_lambda, diff_subln_g, gla_w_gate2, gla_b_gate, gla_norm_g, mla_q_norm_g, mla_w_uq,
              mla_kv_norm_g, mla_w_ukv, ssd_conv_w, ssd_conv_b, ssd_dt_bias, ssd_a_log, ssd_d, ssd_norm_g,
              w_branch, w_out):
    cos_d, sin_d = rope_tables(positions, DIFF_ROT)
    cos_m, sin_m = rope_tables(positions, MLA_ROPE)
    h = x
    for l in range(DEPTH):
        h = layer_norm(ALPHA * h + 0.5 * swiglu(h, ffn1_w_gu[l], ffn1_w_down[l]), ln_g[l, 0], ln_b[l, 0])
        mix = hybrid_mixer(h, l, cos_d, sin_d, cos_m, sin_m, w_in[l], b_gate[l], diff_lambda[l], diff_subln_g[l],
                           gla_w_gate2[l], gla_b_gate[l], gla_norm_g[l], mla_q_norm_g[l], mla_w_uq[l],
                           mla_kv_norm_g[l], mla_w_ukv[l], ssd_conv_w[l], ssd_conv_b[l], ssd_dt_bias[l],
                           ssd_a_log[l], ssd_d[l], ssd_norm_g[l], w_branch[l], w_out[l])
        h = layer_norm(ALPHA * h + mix, ln_g[l, 1], ln_b[l, 1])
        h = layer_norm(ALPHA * h + 0.5 * swiglu(h, ffn2_w_gu[l], ffn2_w_down[l]), ln_g[l, 2], ln_b[l, 2])
    return h
```

```python
import math
import contextlib
import numpy as np
import concourse.bass as bass
import concourse.mybir as mybir
from concourse.bass_utils import run_bass_kernel_spmd

F32 = mybir.dt.float32
BF16 = mybir.dt.bfloat16
I32 = mybir.dt.int32
AF = mybir.ActivationFunctionType
ALU = mybir.AluOpType

D = 2048
DFF = 5632
KC = D // 128
FCH = DFF // 128
TB = 512
DEPTH = 4
ALPHA = (2 * DEPTH) ** 0.25
EPS = 1e-5
NIN = 13400
GATE0 = 5208
ROPE_THETA = 500000.0

ENGS = ("pe", "act", "dve", "pool", "sp")
N_DSEM = 12
N_WSEM = 4


class Buf:
    __slots__ = ("name", "w", "r")

    def __init__(self, name=""):
        self.name = name
        self.w = None
        self.r = []


class Prog:
    def __init__(self, nc):
        self.nc = nc
        self.eng = {"pe": nc.tensor, "act": nc.scalar, "dve": nc.vector, "pool": nc.gpsimd, "sp": nc.sync}
        self.seq = {e: 0 for e in ENGS}
        self.waited = {e: {} for e in ENGS}
        self.sems = {}
        for e in ENGS:
            self.sems["c" + e] = nc.alloc_semaphore("c_" + e)
        self.dtot = {}
        for i in range(N_DSEM):
            self.sems["d%d" % i] = nc.alloc_semaphore("dma%d" % i)
            self.dtot["d%d" % i] = 0
        for i in range(N_WSEM):
            self.sems["w%d" % i] = nc.alloc_semaphore("wdma%d" % i)
            self.dtot["w%d" % i] = 0
        self.rr = {"d": 0, "w": 0}
        self.n_ins = 0

    def _need(self, eng, tok, waits):
        if tok is None:
            return
        key, val, src = tok
        if src == eng and eng == "pe":
            return
        if self.waited[eng].get(key, 0) >= val:
            return
        self.waited[eng][key] = val
        waits[key] = max(waits.get(key, 0), val)

    def _deps(self, eng, reads, writes):
        waits = {}
        for b in reads:
            self._need(eng, b.w, waits)
        for b in writes:
            self._need(eng, b.w, waits)
            for t in b.r:
                self._need(eng, t, waits)
        return waits

    def _mark(self, tok, reads, writes):
        for b in reads:
            b.r.append(tok)
            if len(b.r) > 16:
                best = {}
                for t in b.r:
                    if t[0] not in best or best[t[0]][1] < t[1]:
                        best[t[0]] = t
                b.r = list(best.values())
        for b in writes:
            b.w = tok
            b.r = []

    def _emit_waits(self, eng, waits):
        e = self.eng[eng]
        for k, v in waits.items():
            e.wait_ge(self.sems[k], v)

    def op(self, eng, fn, reads=(), writes=()):
        waits = self._deps(eng, reads, writes)
        self._emit_waits(eng, waits)
        ins = fn(self.eng[eng])
        self.seq[eng] += 1
        ins.then_inc(self.sems["c" + eng], 1)
        tok = ("c" + eng, self.seq[eng], eng)
        self._mark(tok, reads, writes)
        return tok

    def dma(self, eng, fn, reads=(), writes=(), grp="d"):
        n = N_DSEM if grp == "d" else N_WSEM
        s = self.rr[grp]
        self.rr[grp] = (s + 1) % n
        key = "%s%d" % (grp, s)
        waits = self._deps(eng, reads, writes)
        if self.dtot[key] > 0:
            self._need(eng, (key, self.dtot[key], None), waits)
        self._emit_waits(eng, waits)
        ins = fn(self.eng[eng])
        self.dtot[key] += 16
        ins.then_inc(self.sems[key], 16)
        tok = (key, self.dtot[key], None)
        self._mark(tok, reads, writes)
        return tok

    def barrier(self, engs=("pe", "act", "dve", "sp")):
        for e in engs:
            waits = {}
            for x in engs:
                if x != e and self.seq[x] > 0:
                    self._need(e, ("c" + x, self.seq[x], x), waits)
            for i in range(N_DSEM):
                k = "d%d" % i
                if self.dtot[k] > 0:
                    self._need(e, (k, self.dtot[k], None), waits)
            self._emit_waits(e, waits)

    def wait_bufs(self, eng, bufs):
        waits = {}
        for b in bufs:
            self._need(eng, b.w, waits)
        self._emit_waits(eng, waits)


W_IN_CHUNKS = []


def _mk_chunks():
    def add(name, c0, n):
        i = 0
        while n > 0:
            m = min(128, n)
            W_IN_CHUNKS.append((name + str(i), c0, m))
            c0 += m
            n -= m
            i += 1
    add("aq", 0, 512)
    add("ak", 512, 512)
    add("av", 1024, 512)
    add("bq", 1536, 256)
    add("bk", 1792, 256)
    add("bv", 2048, 512)
    add("bg", 2560, 16)
    add("br", 2576, 512)
    add("cq", 3088, 384)
    add("ckv", 3472, 128)
    add("ckr", 3600, 64)
    add("dz", 3664, 512)
    add("dx", 4176, 1024)
    add("dt", 5200, 8)


_mk_chunks()
CH_IDX = {c[0]: i for i, c in enumerate(W_IN_CHUNKS)}
NCH = len(W_IN_CHUNKS)
W_IN_GROUPS = [["aq0", "aq1", "aq2", "aq3"], ["ak0", "ak1", "ak2", "ak3"], ["av0", "av1", "av2", "av3"],
               ["bq0", "bq1", "bk0", "bk1"], ["bv0", "bv1", "bv2", "bv3"], ["bg0", "br0", "br1", "br2"],
               ["br3", "cq0", "cq1", "cq2"], ["ckv0", "ckr0", "dz0", "dz1"], ["dz2", "dz3", "dx0", "dx1"],
               ["dx2", "dx3", "dx4", "dx5"], ["dx6", "dx7", "dt0"]]


def pj_row(name):
    return CH_IDX[name] * 128


class Ctx:
    pass


_UID = [0]


def uname(name):
    _UID[0] += 1
    return "%s_u%d" % (name, _UID[0])


def build_program(S, NSEQ, depth, dbg=None, phases=("ffn1", "mix", "ffn2"), mix_sub=("proj", "diff", "gla", "mla", "ssd", "merge")):
    T = S * NSEQ
    NB = T // TB
    nc = bass.Bass("TRN2", target_bir_lowering=False)
    P = Prog(nc)
    C = Ctx()
    C.nc, C.P, C.S, C.NSEQ, C.T, C.NB = nc, P, S, NSEQ, T, NB
    C.mix_sub = mix_sub
    C.dump = (dbg == 'dump')

    def din(name, shape, dt=F32):
        return nc.dram_tensor(name, list(shape), dt, kind="ExternalInput").ap()

    def dscr(name, shape, dt=F32):
        return nc.dram_tensor(name, list(shape), dt, kind="Internal").ap()

    x_in = din("x", [T, D])
    pos_in = din("positions", [NSEQ, S], I32)
    w_gu = [din("ffn1_w_gu", [depth, D, 2 * DFF]), din("ffn2_w_gu", [depth, D, 2 * DFF])]
    w_dn = [din("ffn1_w_down", [depth, DFF, D]), din("ffn2_w_down", [depth, DFF, D])]
    w_in = din("w_in", [depth, D, NIN])
    w_br = din("w_branch", [depth, 4 * 512, D])
    w_out = din("w_out", [depth, D, D])
    w_uq = din("mla_w_uq", [depth, 384, 768])
    w_ukv = din("mla_w_ukv", [depth, 128, 1024])
    w_g2 = din("gla_w_gate2", [depth, 16, 256])
    NPV = depth * PV_PER_LAYER
    pvec_in = din("pvec", [128, NPV])
    cst_in = din("cst", [128, CST_N])
    y_out = nc.dram_tensor("y", [T, D], F32, kind="ExternalOutput").ap()

    HT = dscr("HT", [D, T])
    C.HT = HT
    if dbg:
        C.PJ = nc.dram_tensor("PJ", [NCH * 128, T], F32, kind="ExternalOutput").ap()
        C.OM = nc.dram_tensor("OM", [D, T], BF16, kind="ExternalOutput").ap()
    else:
        C.PJ = dscr("PJ", [NCH * 128, T])
        C.OM = dscr("OM", [D, T], BF16)
    C.ROPE = dscr("ROPE", [NSEQ, 4, 128, S])
    C.pjb = [[Buf() for _ in range(NB)] for _ in range(NCH)]
    C.omb = [[Buf() for _ in range(NB)] for _ in range(16)]
    C.ropeb = [Buf() for _ in range(NSEQ)]
    hbuf = [Buf("HT%d" % i) for i in range(NB)]
    C.hbuf = hbuf
    wb_gu = [[dscr("wgu%d_%d" % (i, l), [D, 2 * DFF], BF16) for l in range(depth)] for i in range(2)]
    wb_dn = [[dscr("wdn%d_%d" % (i, l), [DFF, D], BF16) for l in range(depth)] for i in range(2)]
    wb_in = [dscr("win_%d" % l, [D, NIN], BF16) for l in range(depth)]
    wb_br = [dscr("wbr_%d" % l, [4 * 512, D], BF16) for l in range(depth)]
    wb_out = [dscr("wout_%d" % l, [D, D], BF16) for l in range(depth)]
    wb_uq = [dscr("wuq_%d" % l, [384, 768], BF16) for l in range(depth)]
    wb_ukv = [dscr("wukv_%d" % l, [128, 1024], BF16) for l in range(depth)]
    wbuf = {}

    def prep(dst, src, rows, cols, key):
        b = wbuf.setdefault(key, Buf(key))
        for r0 in range(0, rows, 2048):
            r1 = min(rows, r0 + 2048)
            for c0 in range(0, cols, 2048):
                c1 = min(cols, c0 + 2048)
                P.dma("pool", lambda e, r0=r0, r1=r1, c0=c0, c1=c1: e.dma_start(out=dst[r0:r1, c0:c1], in_=src[r0:r1, c0:c1]),
                      writes=[b], grp="w")
        return b

    def prep_layer(l, which):
        if which == "ffn1":
            prep(wb_gu[0][l], w_gu[0][l], D, 2 * DFF, "gu0_%d" % l)
            prep(wb_dn[0][l], w_dn[0][l], DFF, D, "dn0_%d" % l)
        elif which == "mix":
            prep(wb_in[l], w_in[l], D, NIN, "in_%d" % l)
            prep(wb_br[l], w_br[l], 2048, D, "br_%d" % l)
            prep(wb_out[l], w_out[l], D, D, "out_%d" % l)
            prep(wb_uq[l], w_uq[l], 384, 768, "uq_%d" % l)
            prep(wb_ukv[l], w_ukv[l], 128, 1024, "ukv_%d" % l)
        else:
            prep(wb_gu[1][l], w_gu[1][l], D, 2 * DFF, "gu1_%d" % l)
            prep(wb_dn[1][l], w_dn[1][l], DFF, D, "dn1_%d" % l)

    with contextlib.ExitStack() as root:
        def sbp(name, shape, dt):
            return root.enter_context(nc.sbuf_tensor(uname(name), list(shape), dt))

        cst = sbp("cst_sb", [128, CST_N], F32)
        pvec = sbp("pvec_sb", [128, NPV], F32)
        cstb = sbp("cstb", [128, CSTB_N], BF16)
        ps = root.enter_context(nc.psum_tensor("ps", [128, 8, 512], F32))
        C.ps = ps
        C.pb = [Buf("bank%d" % i) for i in range(8)]
        b_cst, b_pvec, b_cstb = Buf("cst"), Buf("pvec"), Buf("cstb")
        C.cst, C.pvec, C.cstb, C.b_cst, C.b_pvec, C.b_cstb = cst, pvec, cstb, b_cst, b_pvec, b_cstb
        P.dma("sp", lambda e: e.dma_start(out=cst[:], in_=cst_in[:]), writes=[b_cst])
        P.dma("sp", lambda e: e.dma_start(out=pvec[:], in_=pvec_in[:]), writes=[b_pvec])
        P.op("dve", lambda e: e.tensor_copy(out=cstb[:], in_=cst[:, 0:CSTB_N]), reads=[b_cst], writes=[b_cstb])
        C.ident = cst[:, CO["ident"]:CO["ident"] + 128]
        C.ones = cst[:, CO["ones"]:CO["ones"] + 128]
        C.identb = cstb[:, CO["ident"]:CO["ident"] + 128]
        C.onesb = cstb[:, CO["ones"]:CO["ones"] + 128]

        C.aux = sbp("aux_sb", [128, depth * AUX_N], F32)
        C.b_aux = Buf("aux")
        order = [(l, ph) for l in range(depth) for ph in ("ffn1", "mix", "ffn2") if ph in phases]
        for (l, ph) in order[:2]:
            prep_layer(l, ph)
        nxt = 2

        if "mix" in phases:
            setup_phase(C, pos_in, depth)
        prologue(C, x_in)
        for oi, (l, ph) in enumerate(order):
            if ph == "ffn1":
                ffn_phase(C, wb_gu[0][l], wb_dn[0][l], wbuf["gu0_%d" % l], wbuf["dn0_%d" % l], l, 0)
            elif ph == "ffn2":
                ffn_phase(C, wb_gu[1][l], wb_dn[1][l], wbuf["gu1_%d" % l], wbuf["dn1_%d" % l], l, 2)
            else:
                sub = C.mix_sub
                if "proj" in sub:
                    proj_phase(C, l, wb_in[l], wbuf["in_%d" % l])
                if "diff" in sub:
                    diff_phase(C, l)
                if "gla" in sub:
                    gla_phase(C, l, w_g2[l])
                if "mla" in sub:
                    mla_phase(C, l, wb_uq[l], wbuf["uq_%d" % l], wb_ukv[l], wbuf["ukv_%d" % l])
                if "ssd" in sub:
                    ssd_phase(C, l)
                if "merge" in sub:
                    merge_phase(C, l, wb_in[l], wbuf["in_%d" % l], wb_br[l], wbuf["br_%d" % l], wb_out[l], wbuf["out_%d" % l])
            if nxt < len(order):
                prep_layer(*order[nxt])
                nxt += 1
        epilogue(C, y_out)
    return nc


def _build_consts():
    tabs = []
    off = {}

    def add(name, arr):
        arr = np.asarray(arr, np.float32)
        assert arr.shape[0] == 128
        off[name] = sum(t.shape[1] for t in tabs)
        tabs.append(arr)
    p = np.arange(128)[:, None]
    f = np.arange(512)[None, :]
    i128 = np.arange(128)[None, :]
    add("ident", np.eye(128))
    add("ones", np.ones((128, 128)))
    add("maskA", np.concatenate([(f - j * 128 - p >= 0) for j in range(4)], axis=1))
    add("maskG", ((p // 64) == (i128 // 64)) & (p <= i128))
    add("maskS", (p <= i128))
    pd = np.zeros((128, 128))
    for m in range(128):
        d = m % 64
        if d < 16:
            k = m + 8 if d < 8 else m - 8
            pd[k, m] = 1
    add("permD", pd)
    pm = np.zeros((128, 128))
    for m in range(64):
        k = m + 32 if m < 32 else m - 32
        pm[k, m] = 1
    add("permM", pm)
    nb = sum(t.shape[1] for t in tabs)
    invd = np.zeros((128, 1), np.float32)
    sgnd = np.zeros((128, 1), np.float32)
    fr_d = (np.float32(ROPE_THETA) ** (-np.arange(0, 16, 2, dtype=np.float32) / np.float32(16))).astype(np.float32)
    for m in range(128):
        d = m % 64
        if d < 16:
            invd[m, 0] = fr_d[d % 8]
            sgnd[m, 0] = -1.0 if d < 8 else 1.0
    invm = np.zeros((128, 1), np.float32)
    sgnm = np.zeros((128, 1), np.float32)
    fr_m = (np.float32(ROPE_THETA) ** (-np.arange(0, 64, 2, dtype=np.float32) / np.float32(64))).astype(np.float32)
    for m in range(64):
        invm[m, 0] = fr_m[m % 32]
        sgnm[m, 0] = -1.0 if m < 32 else 1.0
    add("invd", invd)
    add("sgnd", sgnd)
    add("invm", invm)
    add("sgnm", sgnm)
    selE = np.zeros((128, 4, 128))
    for pr in range(4):
        for m in range(128):
            selE[2 * pr + m // 64, pr, m] = 1
    add("selE", selE.reshape(128, 512))
    selH = np.zeros((128, 8, 128))
    for h in range(8):
        selH[h, h, :] = 1
    add("selH", selH.reshape(128, 1024))
    return np.concatenate(tabs, axis=1), off, nb


CST, CO, CSTB_N = _build_consts()
CST_N = CST.shape[1]

PV = {}
_o = 0
for _n, _w in [("ln_g", 48), ("ln_b", 48), ("b_gate", 64), ("subln_g", 1), ("gla_bg", 2), ("gla_ng", 1), ("mla_qg", 3),
               ("mla_kvg", 1), ("conv_w", 32), ("conv_b", 8), ("dt_bias", 1), ("a_log", 1), ("ssd_d", 4), ("ssd_ng", 4),
               ("lam", 256)]:
    PV[_n] = _o
    _o += _w
PV_PER_LAYER = _o


def pv_col(l, name, i=0):
    return l * PV_PER_LAYER + PV[name] + i


def prologue(C, x_in):
    nc, P, ps, pb = C.nc, C.P, C.ps, C.pb
    with contextlib.ExitStack() as st:
        xt = [st.enter_context(nc.sbuf_tensor("pro_x%d" % i, [128, D], F32)) for i in range(2)]
        bx = [Buf(), Buf()]
        stg = st.enter_context(nc.sbuf_tensor("pro_stage", [128, KC, TB], F32))
        bstg = Buf()
        n = 0
        for blk in range(C.NB):
            for sub in range(TB // 128):
                t0 = blk * TB + sub * 128
                s = n % 2
                P.dma("sp", lambda e, s=s, t0=t0: e.dma_start(out=xt[s][:], in_=x_in[t0:t0 + 128, :]), writes=[bx[s]])
                for g in range(4):
                    bank = (n * 4 + g) % 8

                    def tr(e, s=s, g=g, bank=bank):
                        for j in range(4):
                            k = g * 4 + j
                            ins = e.transpose(out=ps[:, bank, j * 128:(j + 1) * 128], in_=xt[s][:, k * 128:(k + 1) * 128],
                                              identity=C.ident)
                        return ins
                    P.op("pe", tr, reads=[bx[s], C.b_cst], writes=[pb[bank]])
                    eng = "dve" if g % 2 == 0 else "act"
                    src = ps[:, bank, :].rearrange("p (j t) -> p j t", j=4)
                    dst = stg[:, g * 4:(g + 1) * 4, sub * 128:(sub + 1) * 128]
                    if eng == "dve":
                        P.op("dve", lambda e, src=src, dst=dst: e.tensor_copy(out=dst, in_=src), reads=[pb[bank]], writes=[bstg])
                    else:
                        P.op("act", lambda e, src=src, dst=dst: e.copy(out=dst, in_=src), reads=[pb[bank]], writes=[bstg])
                n += 1
            P.dma("sp", lambda e, blk=blk: e.dma_start(
                out=C.HT[:, blk * TB:(blk + 1) * TB].rearrange("(k p) t -> p k t", p=128), in_=stg[:]),
                reads=[bstg], writes=[C.hbuf[blk]])
        P.barrier()


def epilogue(C, y_out):
    nc, P, ps, pb = C.nc, C.P, C.ps, C.pb
    by = Buf("y")
    with contextlib.ExitStack() as st:
        hf = st.enter_context(nc.sbuf_tensor("epi_h", [128, KC, TB], F32))
        bh = Buf()
        yt = [st.enter_context(nc.sbuf_tensor("epi_y%d" % i, [128, D], F32)) for i in range(2)]
        byt = [Buf(), Buf()]
        n = 0
        for blk in range(C.NB):
            P.dma("sp", lambda e, blk=blk: e.dma_start(
                out=hf[:], in_=C.HT[:, blk * TB:(blk + 1) * TB].rearrange("(k p) t -> p k t", p=128)),
                reads=[C.hbuf[blk]], writes=[bh])
            for sub in range(TB // 128):
                t0 = blk * TB + sub * 128
                s = n % 2
                for g in range(4):
                    bank = (n * 4 + g) % 8

                    def tr(e, g=g, bank=bank, sub=sub):
                        for j in range(4):
                            k = g * 4 + j
                            ins = e.transpose(out=ps[:, bank, j * 128:(j + 1) * 128], in_=hf[:, k, sub * 128:(sub + 1) * 128],
                                              identity=C.ident)
                        return ins
                    P.op("pe", tr, reads=[bh, C.b_cst], writes=[pb[bank]])
                    dst = yt[s][:, g * 512:(g + 1) * 512]
                    src = ps[:, bank, :]
                    if g % 2 == 0:
                        P.op("dve", lambda e, src=src, dst=dst: e.tensor_copy(out=dst, in_=src), reads=[pb[bank]], writes=[byt[s]])
                    else:
                        P.op("act", lambda e, src=src, dst=dst: e.copy(out=dst, in_=src), reads=[pb[bank]], writes=[byt[s]])
                P.dma("sp", lambda e, s=s, t0=t0: e.dma_start(out=y_out[t0:t0 + 128, :], in_=yt[s][:]), reads=[byt[s]], writes=[by])
                n += 1
        P.wait_bufs("sp", [by])
        P.barrier()


def load_w(C, slot, bslot, w_bf, bw, r0, nk, c0, ncols):
    C.P.dma("sp", lambda e: e.dma_start(out=slot[:, 0:nk, 0:ncols],
                                        in_=w_bf[r0:r0 + nk * 128, c0:c0 + ncols].rearrange("(k p) c -> p k c", p=128)),
            reads=[bw], writes=[bslot])


def ln_block(C, r, br, l, idx, blk, tmp):
    nc, P, ps, pb = C.nc, C.P, C.ps, C.pb
    sq, bsq, mean, msq, rstd, bst = tmp
    def msum(e):
        for c in range(KC):
            ins = e.matmul(ps[:, 0, :], lhsT=C.ones, rhs=r[:, c, :], start=(c == 0), stop=(c == KC - 1))
        return ins
    P.op("pe", msum, reads=[br, C.b_cst], writes=[pb[0]])
    for c in range(KC):
        s = c % 2
        P.op("act", lambda e, c=c, s=s: e.activation(out=sq[s][:], in_=r[:, c, :], func=AF.Square), reads=[br], writes=[bsq[s]])
        P.op("pe", lambda e, c=c, s=s: e.matmul(ps[:, 1, :], lhsT=C.ones, rhs=sq[s][:], start=(c == 0), stop=(c == KC - 1)),
             reads=[bsq[s], C.b_cst], writes=[pb[1]])
    P.op("dve", lambda e: e.tensor_scalar(out=mean[:], in0=ps[:, 0, :], scalar1=1.0 / D, scalar2=None, op0=ALU.mult),
         reads=[pb[0]], writes=[bst])
    P.op("dve", lambda e: e.tensor_tensor(out=msq[:], in0=mean[:], in1=mean[:], op=ALU.mult), reads=[bst], writes=[bst])
    P.op("dve", lambda e: e.scalar_tensor_tensor(out=msq[:], in0=ps[:, 1, :], scalar=1.0 / D, in1=msq[:], op0=ALU.mult, op1=ALU.subtract),
         reads=[pb[1], bst], writes=[bst])
    P.op("dve", lambda e: e.tensor_scalar(out=msq[:], in0=msq[:], scalar1=EPS, scalar2=None, op0=ALU.add), reads=[bst], writes=[bst])
    P.op("act", lambda e: e.activation(out=msq[:], in_=msq[:], func=AF.Sqrt), reads=[bst], writes=[bst])
    P.op("dve", lambda e: e.reciprocal(out=rstd[:], in_=msq[:]), reads=[bst], writes=[bst])
    for c in range(KC):
        P.op("dve", lambda e, c=c: e.tensor_tensor(out=r[:, c, :], in0=r[:, c, :], in1=mean[:], op=ALU.subtract), reads=[br, bst], writes=[br])
        P.op("dve", lambda e, c=c: e.tensor_tensor(out=r[:, c, :], in0=r[:, c, :], in1=rstd[:], op=ALU.mult), reads=[br, bst], writes=[br])
        gcol = pv_col(l, "ln_g", idx * 16 + c)
        bcol = pv_col(l, "ln_b", idx * 16 + c)
        P.op("act", lambda e, c=c, gcol=gcol, bcol=bcol: e.activation(
            out=r[:, c, :], in_=r[:, c, :], func=AF.Identity, scale=C.pvec[:, gcol:gcol + 1], bias=C.pvec[:, bcol:bcol + 1]),
            reads=[br, C.b_pvec], writes=[br])
    P.dma("sp", lambda e: e.dma_start(out=C.HT[:, blk * TB:(blk + 1) * TB].rearrange("(k p) t -> p k t", p=128), in_=r[:]),
          reads=[br], writes=[C.hbuf[blk]])


def ffn_phase(C, wgu, wdn, bgu, bdn, l, idx):
    nc, P, ps, pb = C.nc, C.P, C.ps, C.pb
    with contextlib.ExitStack() as st:
        def sb(name, shape, dt):
            return st.enter_context(nc.sbuf_tensor(uname(name), list(shape), dt))
        hf = sb("f_hf", [128, KC, TB], F32)
        hb = sb("f_hb", [128, KC, TB], BF16)
        act = sb("f_act", [128, FCH, TB], BF16)
        sg = sb("f_sg", [128, 4, TB], BF16)
        NSL = 3
        slots = [sb("f_w%d" % i, [128, 16, 512], BF16) for i in range(NSL)]
        bsl = [Buf() for _ in range(NSL)]
        sq = [sb("f_sq%d" % i, [128, TB], F32) for i in range(2)]
        bsq = [Buf(), Buf()]
        mean = sb("f_mean", [128, TB], F32)
        msq = sb("f_msq", [128, TB], F32)
        rstd = sb("f_rstd", [128, TB], F32)
        bhf, bhb, bact, bsg, bst = Buf(), Buf(), Buf(), [Buf() for _ in range(4)], Buf()
        bactc = [Buf() for _ in range(FCH)]
        si = 0
        for blk in range(C.NB):
            P.dma("sp", lambda e, blk=blk: e.dma_start(
                out=hf[:], in_=C.HT[:, blk * TB:(blk + 1) * TB].rearrange("(k p) t -> p k t", p=128)),
                reads=[C.hbuf[blk]], writes=[bhf])
            for h in range(4):
                eng = "dve" if h % 2 == 0 else "act"
                if eng == "dve":
                    P.op("dve", lambda e, h=h: e.tensor_copy(out=hb[:, h * 4:(h + 1) * 4, :], in_=hf[:, h * 4:(h + 1) * 4, :]),
                         reads=[bhf], writes=[bhb])
                else:
                    P.op("act", lambda e, h=h: e.copy(out=hb[:, h * 4:(h + 1) * 4, :], in_=hf[:, h * 4:(h + 1) * 4, :]),
                         reads=[bhf], writes=[bhb])
            for g in range(FCH // 4):
                for part in range(2):
                    s = si % NSL
                    si += 1
                    load_w(C, slots[s], bsl[s], wgu, bgu, 0, 16, part * DFF + g * 512, 512)
                    banks = [part * 4 + j for j in range(4)]

                    def mm(e, s=s, banks=banks):
                        for k in range(KC):
                            for j in range(4):
                                ins = e.matmul(ps[:, banks[j], :], lhsT=slots[s][:, k, j * 128:(j + 1) * 128], rhs=hb[:, k, :],
                                               start=(k == 0), stop=(k == KC - 1))
                        return ins
                    P.op("pe", mm, reads=[bsl[s], bhb], writes=[pb[b] for b in banks])
                for j in range(4):
                    P.op("act", lambda e, j=j: e.activation(out=sg[:, j, :], in_=ps[:, j, :], func=AF.Silu), reads=[pb[j]], writes=[bsg[j]])
                    P.op("dve", lambda e, j=j, g=g: e.scalar_tensor_tensor(
                        out=act[:, g * 4 + j, :], in0=ps[:, 4 + j, :], scalar=0.5, in1=sg[:, j, :], op0=ALU.mult, op1=ALU.mult),
                        reads=[pb[4 + j], bsg[j]], writes=[bactc[g * 4 + j]])
            for cg in range(4):
                banks = [(cg % 2) * 4 + j for j in range(4)]
                for q in range(4):
                    s = si % NSL
                    si += 1
                    load_w(C, slots[s], bsl[s], wdn, bdn, q * 11 * 128, 11, cg * 512, 512)

                    def mm(e, s=s, banks=banks, q=q):
                        for kk in range(11):
                            k = q * 11 + kk
                            for j in range(4):
                                ins = e.matmul(ps[:, banks[j], :], lhsT=slots[s][:, kk, j * 128:(j + 1) * 128], rhs=act[:, k, :],
                                               start=(k == 0), stop=(k == FCH - 1))
                        return ins
                    P.op("pe", mm, reads=[bsl[s]] + bactc[q * 11:(q + 1) * 11], writes=[pb[b] for b in banks])
                for j in range(4):
                    c = cg * 4 + j
                    P.op("dve", lambda e, c=c, b=banks[j]: e.scalar_tensor_tensor(
                        out=hf[:, c, :], in0=hf[:, c, :], scalar=ALPHA, in1=ps[:, b, :], op0=ALU.mult, op1=ALU.add),
                        reads=[pb[banks[j]], bhf], writes=[bhf])
            ln_block(C, hf, bhf, l, idx, blk, (sq, bsq, mean, msq, rstd, bst))
        P.barrier()


def _pvec_host(inputs, depth):
    pv = np.zeros((128, depth * PV_PER_LAYER), np.float32)

    def put(l, name, arr):
        arr = np.asarray(arr, np.float32)
        c = pv_col(l, name)
        pv[:arr.shape[0], c:c + arr.shape[1]] = arr
    for l in range(depth):
        put(l, "ln_g", inputs["ln_g"][l].reshape(3, 16, 128).transpose(2, 0, 1).reshape(128, 48))
        put(l, "ln_b", inputs["ln_b"][l].reshape(3, 16, 128).transpose(2, 0, 1).reshape(128, 48))
        put(l, "b_gate", inputs["b_gate"][l].reshape(64, 128).T)
        put(l, "subln_g", inputs["diff_subln_g"][l].reshape(128, 1))
        put(l, "gla_bg", inputs["gla_b_gate"][l].reshape(2, 128).T)
        put(l, "gla_ng", inputs["gla_norm_g"][l].reshape(128, 1))
        put(l, "mla_qg", inputs["mla_q_norm_g"][l].reshape(3, 128).T)
        put(l, "mla_kvg", inputs["mla_kv_norm_g"][l].reshape(128, 1))
        put(l, "conv_w", inputs["ssd_conv_w"][l].reshape(4, 8, 128).transpose(2, 0, 1).reshape(128, 32))
        put(l, "conv_b", inputs["ssd_conv_b"][l].reshape(8, 128).T)
        put(l, "dt_bias", inputs["ssd_dt_bias"][l].reshape(8, 1))
        put(l, "a_log", inputs["ssd_a_log"][l].reshape(8, 1))
        put(l, "ssd_d", np.repeat(inputs["ssd_d"][l], 64).reshape(4, 128).T)
        put(l, "ssd_ng", inputs["ssd_norm_g"][l].reshape(4, 128).T)
        put(l, "lam", np.broadcast_to(inputs["diff_lambda"][l].reshape(1, 256), (128, 256)))
    return pv


_NC_CACHE = {}


def run(inputs, S, NSEQ, depth, n_cores, phases=("ffn1", "mix", "ffn2"), trace=False, dbg=None,
        mix_sub=("proj", "diff", "gla", "mla", "ssd", "merge")):
    key = (S, NSEQ, depth, tuple(phases), dbg, tuple(mix_sub))
    if key not in _NC_CACHE:
        _NC_CACHE[key] = build_program(S, NSEQ, depth, phases=phases, dbg=dbg, mix_sub=mix_sub)
    nc = _NC_CACHE[key]
    pv = _pvec_host(inputs, depth)
    x = np.ascontiguousarray(inputs["x"], dtype=np.float32)
    pos = np.ascontiguousarray(inputs["positions"], dtype=np.int32)
    shared = {
        "ffn1_w_gu": inputs["ffn1_w_gu"], "ffn2_w_gu": inputs["ffn2_w_gu"],
        "ffn1_w_down": inputs["ffn1_w_down"], "ffn2_w_down": inputs["ffn2_w_down"],
        "w_in": inputs["w_in"], "w_branch": np.asarray(inputs["w_branch"]).reshape(depth, 2048, D), "w_out": inputs["w_out"],
        "mla_w_uq": inputs["mla_w_uq"], "mla_w_ukv": inputs["mla_w_ukv"], "gla_w_gate2": inputs["gla_w_gate2"],
        "pvec": pv, "cst": CST,
    }
    shared = {k: np.ascontiguousarray(v, dtype=np.float32) for k, v in shared.items()}
    in_maps = []
    for c in range(n_cores):
        m = dict(shared)
        m["x"] = x[c * NSEQ:(c + 1) * NSEQ].reshape(NSEQ * S, D)
        m["positions"] = pos[c * NSEQ:(c + 1) * NSEQ]
        in_maps.append(m)
    res = run_bass_kernel_spmd(nc, in_maps, core_ids=list(range(n_cores)), trace=trace)
    out = np.concatenate([r["y"].reshape(NSEQ, S, D) for r in res.results], axis=0)
    return out, res


def kernel(**inputs):
    out, _ = run(inputs, 2048, 2, DEPTH, 8)
    return out.astype(np.float32)


import os
GLA_STOP = int(os.environ.get('GLA_STOP', '9'))
AUX_N = 4
TWO_PI = 2.0 * math.pi
CW1 = 6.28125
CW2 = TWO_PI - CW1


def setup_phase(C, pos_in, depth):
    nc, P = C.nc, C.P
    S = C.S
    aux, baux = C.aux, C.b_aux
    pv = C.pvec
    with contextlib.ExitStack() as st:
        def sb(name, shape, dt):
            return st.enter_context(nc.sbuf_tensor(uname(name), list(shape), dt))
        t64 = sb("su_t64", [128, 64], F32)
        s1 = sb("su_s1", [128, 2], F32)
        bt = Buf()
        for l in range(depth):
            lc = pv_col(l, "lam")
            lam_init = 0.8 - 0.6 * math.exp(-0.3 * l)
            for i in range(2):
                P.op("dve", lambda e, i=i, lc=lc: e.tensor_tensor(out=t64[:], in0=pv[:, lc + i * 128:lc + i * 128 + 64],
                                                               in1=pv[:, lc + i * 128 + 64:lc + i * 128 + 128], op=ALU.mult),
                     reads=[C.b_pvec], writes=[bt])
                P.op("dve", lambda e, i=i: e.tensor_reduce(out=s1[:, i:i + 1], in_=t64[:], axis=mybir.AxisListType.X, op=ALU.add),
                     reads=[bt], writes=[bt])
            P.op("act", lambda e: e.activation(out=s1[:], in_=s1[:], func=AF.Exp), reads=[bt], writes=[bt])
            a0 = l * AUX_N
            P.op("dve", lambda e, a0=a0: e.tensor_tensor(out=aux[:, a0:a0 + 1], in0=s1[:, 1:2], in1=s1[:, 0:1], op=ALU.subtract),
                 reads=[bt], writes=[baux])
            P.op("dve", lambda e, a0=a0, li=lam_init: e.tensor_scalar(out=aux[:, a0:a0 + 1], in0=aux[:, a0:a0 + 1], scalar1=-li, scalar2=None, op0=ALU.add),
                 reads=[baux], writes=[baux])
            gc = pv_col(l, "gla_bg")
            P.op("dve", lambda e, a0=a0, gc=gc: e.tensor_scalar(out=aux[:, a0 + 1:a0 + 3], in0=pv[:, gc:gc + 2], scalar1=-1.0, scalar2=None, op0=ALU.mult),
                 reads=[C.b_pvec], writes=[baux])
            ac = pv_col(l, "a_log")
            P.op("act", lambda e, a0=a0, ac=ac: e.activation(out=aux[:, a0 + 3:a0 + 4], in_=pv[:, ac:ac + 1], func=AF.Exp),
                 reads=[C.b_pvec, baux], writes=[baux])
            P.op("dve", lambda e, a0=a0: e.tensor_scalar(out=aux[:, a0 + 3:a0 + 4], in0=aux[:, a0 + 3:a0 + 4], scalar1=-1.0, scalar2=None, op0=ALU.mult),
                 reads=[baux], writes=[baux])
        pi = sb("su_pi", [128, 1, S], I32)
        pf = sb("su_pf", [128, S], F32)
        ang = sb("su_ang", [128, S], F32)
        a = sb("su_a", [128, S], F32)
        ni = sb("su_ni", [128, S], I32)
        nf = sb("su_nf", [128, S], F32)
        m = sb("su_m", [128, S], F32)
        bw = Buf()
        for s in range(C.NSEQ):
            P.dma("sp", lambda e, s=s: e.dma_start(out=pi[:], in_=pos_in[s:s + 1, :].partition_broadcast(128)), writes=[bw])
            P.op("dve", lambda e: e.tensor_copy(out=pf[:], in_=pi[:, 0, :]), reads=[bw], writes=[bw])
            for ti, (inv, sgn) in enumerate((("invd", "sgnd"), ("invm", "sgnm"))):
                P.op("dve", lambda e, inv=inv: e.tensor_scalar(out=ang[:], in0=pf[:], scalar1=C.cst[:, CO[inv]:CO[inv] + 1], scalar2=None, op0=ALU.mult),
                     reads=[bw, C.b_cst], writes=[bw])
                for ki in range(2):
                    V = lambda fn: P.op("dve", fn, reads=[bw], writes=[bw])
                    V(lambda e, ki=ki: e.tensor_scalar(out=a[:], in0=ang[:], scalar1=(math.pi / 2 if ki == 0 else 0.0), scalar2=None, op0=ALU.add))
                    V(lambda e: e.tensor_scalar(out=ni[:], in0=a[:], scalar1=1.0 / TWO_PI, scalar2=None, op0=ALU.mult))
                    V(lambda e: e.tensor_copy(out=nf[:], in_=ni[:]))
                    V(lambda e: e.scalar_tensor_tensor(out=a[:], in0=nf[:], scalar=-CW1, in1=a[:], op0=ALU.mult, op1=ALU.add))
                    V(lambda e: e.scalar_tensor_tensor(out=a[:], in0=nf[:], scalar=-CW2, in1=a[:], op0=ALU.mult, op1=ALU.add))
                    V(lambda e: e.tensor_scalar(out=m[:], in0=a[:], scalar1=math.pi, scalar2=TWO_PI, op0=ALU.is_gt, op1=ALU.mult))
                    V(lambda e: e.tensor_tensor(out=a[:], in0=a[:], in1=m[:], op=ALU.subtract))
                    V(lambda e: e.tensor_scalar(out=m[:], in0=a[:], scalar1=-math.pi, scalar2=TWO_PI, op0=ALU.is_lt, op1=ALU.mult))
                    V(lambda e: e.tensor_tensor(out=a[:], in0=a[:], in1=m[:], op=ALU.add))
                    V(lambda e: e.tensor_scalar(out=a[:], in0=a[:], scalar1=3.1415925, scalar2=-3.1415925, op0=ALU.min, op1=ALU.max))
                    P.op("act", lambda e: e.activation(out=a[:], in_=a[:], func=AF.Sin), reads=[bw], writes=[bw])
                    if ki == 1:
                        V(lambda e, sgn=sgn: e.tensor_scalar(out=a[:], in0=a[:], scalar1=C.cst[:, CO[sgn]:CO[sgn] + 1], scalar2=None, op0=ALU.mult))
                    P.dma("sp", lambda e, s=s, ti=ti, ki=ki: e.dma_start(out=C.ROPE[s, ti * 2 + ki], in_=a[:]), reads=[bw], writes=[C.ropeb[s]])
        P.barrier()


def seq_blocks(C, s):
    n = C.S // TB
    return list(range(s * n, (s + 1) * n))


def proj_phase(C, l, wbin, bwin):
    nc, P, ps, pb = C.nc, C.P, C.ps, C.pb
    with contextlib.ExitStack() as st:
        def sb(name, shape, dt):
            return st.enter_context(nc.sbuf_tensor(uname(name), list(shape), dt))
        hf = sb("p_hf", [128, KC, TB], F32)
        hb = sb("p_hb", [128, KC, TB], BF16)
        slots = [sb("p_w%d" % i, [128, 16, 512], BF16) for i in range(3)]
        bsl = [Buf() for _ in range(3)]
        stg = [sb("p_st%d" % i, [128, 4, TB], F32) for i in range(2)]
        bstg = [Buf(), Buf()]
        bhf, bhb = Buf(), Buf()
        si = 0
        for blk in range(C.NB):
            P.dma("sp", lambda e, blk=blk: e.dma_start(
                out=hf[:], in_=C.HT[:, blk * TB:(blk + 1) * TB].rearrange("(k p) t -> p k t", p=128)),
                reads=[C.hbuf[blk]], writes=[bhf])
            for h in range(4):
                if h % 2 == 0:
                    P.op("dve", lambda e, h=h: e.tensor_copy(out=hb[:, h * 4:(h + 1) * 4, :], in_=hf[:, h * 4:(h + 1) * 4, :]), reads=[bhf], writes=[bhb])
                else:
                    P.op("act", lambda e, h=h: e.copy(out=hb[:, h * 4:(h + 1) * 4, :], in_=hf[:, h * 4:(h + 1) * 4, :]), reads=[bhf], writes=[bhb])
            for gi, grp in enumerate(W_IN_GROUPS):
                chs = [W_IN_CHUNKS[CH_IDX[n]] for n in grp]
                c0 = chs[0][1]
                ncols = chs[-1][1] + chs[-1][2] - c0
                s = si % 3
                si += 1
                load_w(C, slots[s], bsl[s], wbin, bwin, 0, 16, c0, ncols)
                banks = [(gi % 2) * 4 + j for j in range(len(chs))]

                def mm(e, s=s, chs=chs, banks=banks, c0=c0):
                    for k in range(KC):
                        for j, ch in enumerate(chs):
                            o = ch[1] - c0
                            ins = e.matmul(ps[0:ch[2], banks[j], :], lhsT=slots[s][:, k, o:o + ch[2]], rhs=hb[:, k, :],
                                           start=(k == 0), stop=(k == KC - 1))
                    return ins
                P.op("pe", mm, reads=[bsl[s], bhb], writes=[pb[b] for b in banks])
                sg = gi % 2
                for j, ch in enumerate(chs):
                    w = ch[2]
                    if j % 2 == 0:
                        P.op("dve", lambda e, j=j, w=w, sg=sg, b=banks[j]: e.tensor_copy(out=stg[sg][0:w, j, :], in_=ps[0:w, b, :]),
                             reads=[pb[banks[j]]], writes=[bstg[sg]])
                    else:
                        P.op("act", lambda e, j=j, w=w, sg=sg, b=banks[j]: e.copy(out=stg[sg][0:w, j, :], in_=ps[0:w, b, :]),
                             reads=[pb[banks[j]]], writes=[bstg[sg]])
                for j, ch in enumerate(chs):
                    w = ch[2]
                    ci = CH_IDX[ch[0]]
                    P.dma("sp", lambda e, j=j, w=w, sg=sg, ci=ci, blk=blk: e.dma_start(
                        out=C.PJ[ci * 128:ci * 128 + w, blk * TB:(blk + 1) * TB], in_=stg[sg][0:w, j, :]),
                        reads=[bstg[sg]], writes=[C.pjb[ci][blk]])
        P.barrier()


def pj_load(C, dst, bdst, name, s, rows=128, t0=0, nt=None):
    S = C.S
    nt = S if nt is None else nt
    ci = CH_IDX[name]
    g0 = s * S + t0
    blks = sorted(set(range(g0 // TB, (g0 + nt - 1) // TB + 1)))
    C.P.dma("sp", lambda e: e.dma_start(out=dst, in_=C.PJ[ci * 128:ci * 128 + rows, g0:g0 + nt]),
            reads=[C.pjb[ci][b] for b in blks], writes=[bdst])


def om_store(C, src, bsrc, row0, s, t0, nt, rows=128):
    g0 = s * C.S + t0
    blk = g0 // TB
    C.P.dma("sp", lambda e: e.dma_start(out=C.OM[row0:row0 + rows, g0:g0 + nt], in_=src),
            reads=[bsrc], writes=[C.omb[row0 // 128][blk]])


def attn_core(C, T_, qk_parts, v_tok, bv, scale, qb, rl, brl, obank, lbank, reads):
    P, ps, pb = C.P, C.ps, C.pb
    QB = min(512, C.S)
    nkc = (qb + 1) * QB // 128
    ndiag = QB // 128
    pT, bpT = T_["pT"], T_["bpT"]
    for kc in range(nkc):
        sb_ = kc % 2
        sl = kc % 3
        def qk(e, kc=kc, sb_=sb_):
            for i, (qT, kT) in enumerate(qk_parts):
                ins = e.matmul(ps[:, sb_, 0:QB], lhsT=kT[:, kc * 128:(kc + 1) * 128], rhs=qT[:, qb * QB:(qb + 1) * QB],
                               start=(i == 0), stop=(i == len(qk_parts) - 1))
            return ins
        P.op("pe", qk, reads=reads, writes=[pb[sb_]])
        P.op("act", lambda e, sb_=sb_, sl=sl: e.activation(out=pT[sl][:, 0:QB], in_=ps[:, sb_, 0:QB], func=AF.Exp, scale=scale),
             reads=[pb[sb_]], writes=[bpT[sl]])
        dj = kc - (nkc - ndiag)
        if dj >= 0:
            mo = CO["maskA"] + dj * 512
            P.op("dve", lambda e, sl=sl, mo=mo: e.tensor_tensor(out=pT[sl][:, 0:QB], in0=pT[sl][:, 0:QB], in1=C.cstb[:, mo:mo + QB], op=ALU.mult),
                 reads=[bpT[sl], C.b_cstb], writes=[bpT[sl]])
        def pv_(e, kc=kc, sl=sl):
            e.matmul(ps[:, obank, 0:QB], lhsT=v_tok[:, kc, :], rhs=pT[sl][:, 0:QB], start=(kc == 0), stop=(kc == nkc - 1))
            return e.matmul(ps[:, lbank, 0:QB], lhsT=C.onesb, rhs=pT[sl][:, 0:QB], start=(kc == 0), stop=(kc == nkc - 1))
        if getattr(C, "dump", False) and not getattr(C, "dumped", False) and kc == 1:
            C.dumped = True
            P.dma("sp", lambda e, sl=sl: e.dma_start(out=C.OM[896:1024, 0:QB], in_=pT[sl][:, 0:QB]), reads=[bpT[sl]], writes=[C.omb[7][0]])
        P.op("pe", pv_, reads=[bpT[sl], bv, C.b_cstb], writes=[pb[obank], pb[lbank]])
    P.op("dve", lambda e: e.reciprocal(out=rl[:, 0:QB], in_=ps[:, lbank, 0:QB]), reads=[pb[lbank]], writes=[brl])


def transpose_to_tok(C, srcT, bsrc, dst, bdst, S, bank0=2):
    P, ps, pb = C.P, C.ps, C.pb
    n = S // 128
    for g in range(0, n, 4):
        bank = bank0 + (g // 4) % 2
        cnt = min(4, n - g)
        def tr(e, g=g, bank=bank, cnt=cnt):
            for j in range(cnt):
                ins = e.transpose(out=ps[:, bank, j * 128:(j + 1) * 128], in_=srcT[:, (g + j) * 128:(g + j + 1) * 128], identity=C.ident)
            return ins
        P.op("pe", tr, reads=[bsrc, C.b_cst], writes=[pb[bank]])
        P.op("act", lambda e, g=g, bank=bank, cnt=cnt: e.copy(out=dst[:, g:g + cnt, :], in_=ps[:, bank, 0:cnt * 128].rearrange("p (j t) -> p j t", j=cnt)),
             reads=[pb[bank]], writes=[bdst])


def rope_fm(C, x, bx, xb, bxb, cosT, sinT, brope, perm, kp, out, bout, tmp, btmp, S, bank=3):
    P, ps, pb = C.P, C.ps, C.pb
    P.op("act", lambda e: e.copy(out=xb[0:kp, :], in_=x[0:kp, :]), reads=[bx], writes=[bxb])
    for t0 in range(0, S, 512):
        n = min(512, S - t0)
        P.op("pe", lambda e, t0=t0, n=n: e.matmul(ps[0:kp, bank, 0:n], lhsT=perm[0:kp, 0:kp], rhs=xb[0:kp, t0:t0 + n], start=True, stop=True),
             reads=[bxb, C.b_cstb], writes=[pb[bank]])
        P.op("dve", lambda e, t0=t0, n=n: e.tensor_tensor(out=tmp[0:kp, 0:n], in0=ps[0:kp, bank, 0:n], in1=sinT[0:kp, t0:t0 + n], op=ALU.mult),
             reads=[pb[bank], brope], writes=[btmp])
        P.op("dve", lambda e, t0=t0, n=n: e.tensor_tensor(out=x[0:kp, t0:t0 + n], in0=x[0:kp, t0:t0 + n], in1=cosT[0:kp, t0:t0 + n], op=ALU.mult),
             reads=[bx, brope], writes=[bx])
        P.op("dve", lambda e, t0=t0, n=n: e.tensor_tensor(out=out[0:kp, t0:t0 + n], in0=x[0:kp, t0:t0 + n], in1=tmp[0:kp, 0:n], op=ALU.add),
             reads=[bx, btmp], writes=[bout])


def colnorm_rstd(C, sq_aps, bsq, n_feat, rstd, brstd, n, bank=7):
    P, ps, pb = C.P, C.ps, C.pb
    def mm(e):
        for i, a in enumerate(sq_aps):
            kp = a.shape[0]
            ins = e.matmul(ps[:, bank, 0:n], lhsT=C.ones[0:kp, :], rhs=a, start=(i == 0), stop=(i == len(sq_aps) - 1))
        return ins
    P.op("pe", mm, reads=[bsq, C.b_cst], writes=[pb[bank]])
    P.op("dve", lambda e: e.tensor_scalar(out=rstd[:, 0:n], in0=ps[:, bank, 0:n], scalar1=1.0 / n_feat, scalar2=EPS, op0=ALU.mult, op1=ALU.add),
         reads=[pb[bank]], writes=[brstd])
    P.op("act", lambda e: e.activation(out=rstd[:, 0:n], in_=rstd[:, 0:n], func=AF.Sqrt), reads=[brstd], writes=[brstd])
    P.op("dve", lambda e: e.reciprocal(out=rstd[:, 0:n], in_=rstd[:, 0:n]), reads=[brstd], writes=[brstd])


def diff_phase(C, l):
    nc, P, ps, pb = C.nc, C.P, C.ps, C.pb
    S = C.S
    QB = min(512, S)
    lam_init = 0.8 - 0.6 * math.exp(-0.3 * l)
    with contextlib.ExitStack() as st:
        def sb(name, shape, dt):
            return st.enter_context(nc.sbuf_tensor(uname(name), list(shape), dt))
        cosT = sb("a_cos", [128, S], F32)
        sinT = sb("a_sin", [128, S], F32)
        brope = Buf()
        qf, kf, vf = sb("a_qf", [128, S], F32), sb("a_kf", [128, S], F32), sb("a_vf", [128, S], F32)
        xb = sb("a_xb", [128, S], BF16)
        qb_, kb_ = sb("a_qb", [128, S], BF16), sb("a_kb", [128, S], BF16)
        kbm = [sb("a_kbm%d" % i, [128, S], BF16) for i in range(2)]
        bkbm = [Buf(), Buf()]
        for i in range(2):
            P.op("dve", lambda e, i=i: e.memset(kbm[i][:], 0.0), writes=[bkbm[i]])
        v_tok = sb("a_vt", [128, S // 128, 128], BF16)
        tmp = sb("a_tmp", [128, 512], F32)
        T_ = {"pT": [sb("a_pT%d" % i, [128, 512], BF16) for i in range(3)], "bpT": [Buf() for _ in range(3)]}
        rl = [sb("a_rl%d" % i, [128, 512], F32) for i in range(2)]
        t0_, t1_ = sb("a_t0", [128, 512], F32), sb("a_t1", [128, 512], F32)
        sq = sb("a_sq", [128, 512], F32)
        rstd = sb("a_rstd", [128, 512], F32)
        ob = sb("a_ob", [128, 512], BF16)
        bq, bk, bvf, bxb, bqb, bkb, bvt, btmp = [Buf() for _ in range(8)]
        brl, bt0, bt1, bsq, brstd, bob = [Buf(), Buf()], Buf(), Buf(), Buf(), Buf(), Buf()
        for s in range(C.NSEQ):
            P.dma("sp", lambda e, s=s: e.dma_start(out=cosT[:], in_=C.ROPE[s, 0]), reads=[C.ropeb[s]], writes=[brope])
            P.dma("sp", lambda e, s=s: e.dma_start(out=sinT[:], in_=C.ROPE[s, 1]), reads=[C.ropeb[s]], writes=[brope])
            for h in range(4):
                pj_load(C, qf[:], bq, "aq%d" % h, s)
                pj_load(C, kf[:], bk, "ak%d" % h, s)
                pj_load(C, vf[:], bvf, "av%d" % h, s)
                rope_fm(C, qf, bq, xb, bxb, cosT, sinT, brope, C.cstb[:, CO["permD"]:CO["permD"] + 128], 128, qb_, bqb, tmp, btmp, S)
                rope_fm(C, kf, bk, xb, bxb, cosT, sinT, brope, C.cstb[:, CO["permD"]:CO["permD"] + 128], 128, kb_, bkb, tmp, btmp, S)
                transpose_to_tok(C, vf, bvf, v_tok, bvt, S)
                P.op("act", lambda e: e.copy(out=kbm[0][0:64, :], in_=kb_[0:64, :]), reads=[bkb], writes=[bkbm[0]])
                P.op("dve", lambda e: e.tensor_copy(out=kbm[1][64:128, :], in_=kb_[64:128, :]), reads=[bkb], writes=[bkbm[1]])
                if getattr(C, "dump", False) and s == 0 and h == 0:
                    P.dma("sp", lambda e: e.dma_start(out=C.OM[512:640, 0:S], in_=qb_[:]), reads=[bqb], writes=[C.omb[4][0]])
                    P.dma("sp", lambda e: e.dma_start(out=C.OM[640:768, 0:S], in_=kb_[:]), reads=[bkb], writes=[C.omb[5][0]])
                    P.dma("sp", lambda e: e.dma_start(out=C.OM[768:896, 0:S].rearrange("p (c f) -> p c f", f=128), in_=v_tok[:]), reads=[bvt], writes=[C.omb[6][0]])
                for qb in range(S // QB):
                    for c in range(2):
                        attn_core(C, T_, [(qb_[:, :], kbm[c][:, :])], v_tok, bvt, 0.125, qb,
                                  rl[c], brl[c], 4 + c, 6 + c, [bqb, bkbm[c]])
                    P.op("dve", lambda e: e.tensor_tensor(out=t0_[:, 0:QB], in0=ps[:, 4, 0:QB], in1=rl[0][:, 0:QB], op=ALU.mult),
                         reads=[pb[4], brl[0]], writes=[bt0])
                    P.op("dve", lambda e: e.tensor_tensor(out=t1_[:, 0:QB], in0=ps[:, 5, 0:QB], in1=rl[1][:, 0:QB], op=ALU.mult),
                         reads=[pb[5], brl[1]], writes=[bt1])
                    if getattr(C, "dump", False) and s == 0 and h == 0 and qb == 0:
                        dbt = [st.enter_context(nc.sbuf_tensor("a_dbt%d" % i, [128, 512], BF16)) for i in range(4)]
                        bdb = Buf()
                        P.op("dve", lambda e: e.tensor_copy(out=dbt[0][:, 0:QB], in_=t0_[:, 0:QB]), reads=[bt0], writes=[bdb])
                        P.op("dve", lambda e: e.tensor_copy(out=dbt[1][:, 0:QB], in_=t1_[:, 0:QB]), reads=[bt1], writes=[bdb])
                        P.op("dve", lambda e: e.tensor_copy(out=dbt[2][:, 0:QB], in_=rl[0][:, 0:QB]), reads=[brl[0]], writes=[bdb])
                        P.op("dve", lambda e: e.tensor_copy(out=dbt[3][:, 0:QB], in_=ps[:, 4, 0:QB]), reads=[pb[4]], writes=[bdb])
                        for i in range(4):
                            P.dma("sp", lambda e, i=i: e.dma_start(out=C.OM[1024 + i * 128:1152 + i * 128, 0:QB], in_=dbt[i][:, 0:QB]), reads=[bdb], writes=[C.omb[8 + i][0]])
                    nl = l * AUX_N
                    P.op("dve", lambda e, nl=nl: e.scalar_tensor_tensor(out=t0_[:, 0:QB], in0=t1_[:, 0:QB], scalar=C.aux[:, nl:nl + 1], in1=t0_[:, 0:QB],
                                                                      op0=ALU.mult, op1=ALU.add), reads=[bt0, bt1, C.b_aux], writes=[bt0])
                    P.op("act", lambda e: e.activation(out=sq[:, 0:QB], in_=t0_[:, 0:QB], func=AF.Square), reads=[bt0], writes=[bsq])
                    colnorm_rstd(C, [sq[:, 0:QB]], bsq, 128.0, rstd, brstd, QB)
                    P.op("dve", lambda e: e.tensor_scalar(out=rstd[:, 0:QB], in0=rstd[:, 0:QB], scalar1=1.0 - lam_init, scalar2=None, op0=ALU.mult), reads=[brstd], writes=[brstd])
                    gc = pv_col(l, "subln_g")
                    P.op("dve", lambda e, gc=gc: e.scalar_tensor_tensor(out=ob[:, 0:QB], in0=t0_[:, 0:QB], scalar=C.pvec[:, gc:gc + 1], in1=rstd[:, 0:QB],
                                                                      op0=ALU.mult, op1=ALU.mult), reads=[bt0, brstd, C.b_pvec], writes=[bob])
                    om_store(C, ob[:, 0:QB], bob, h * 128, s, qb * QB, QB)
        P.barrier()


def mla_phase(C, l, wbuq, bwuq, wbukv, bwukv):
    nc, P, ps, pb = C.nc, C.P, C.ps, C.pb
    S = C.S
    QB = min(512, S)
    NT = S // 128
    with contextlib.ExitStack() as st:
        def sb(name, shape, dt):
            return st.enter_context(nc.sbuf_tensor(uname(name), list(shape), dt))
        cosT, sinT = sb("c_cos", [128, S], F32), sb("c_sin", [128, S], F32)
        brope = Buf()
        wq = sb("c_wq", [128, 3, 768], BF16)
        wkv = sb("c_wkv", [128, 1024], BF16)
        bwq, bwkv = Buf(), Buf()
        cq = sb("c_cq", [128, 3, S], F32)
        ckv = sb("c_ckv", [128, S], F32)
        ckr = sb("c_ckr", [128, S], F32)
        cqn = sb("c_cqn", [128, 3, S], BF16)
        ckvn = sb("c_ckvn", [128, S], BF16)
        krb = sb("c_krb", [128, S], BF16)
        xb = sb("c_xb", [128, S], BF16)
        sq3 = sb("c_sq3", [128, 3, 512], F32)
        rstd = sb("c_rstd", [128, 512], F32)
        tmp = sb("c_tmp", [128, 512], F32)
        vt = sb("c_vt", [128, NT, 4, 128], BF16)
        knb, qnb, qrb = sb("c_knb", [128, S], BF16), sb("c_qnb", [128, S], BF16), sb("c_qrb", [128, S], BF16)
        qrf = sb("c_qrf", [128, S], F32)
        T_ = {"pT": [sb("c_pT%d" % i, [128, 512], BF16) for i in range(3)], "bpT": [Buf() for _ in range(3)]}
        rl = sb("c_rl", [128, 512], F32)
        ob = sb("c_ob", [128, 512], BF16)
        bcq, bckv, bckr, bcqn, bckvn, bkrb, bxb, bsq, brstd, btmp, bvt, bknb, bqnb, bqrb, bqrf, brl, bob = [Buf() for _ in range(17)]
        P.dma("sp", lambda e: e.dma_start(out=wq[:], in_=wbuq.rearrange("(k p) c -> p k c", p=128)), reads=[bwuq], writes=[bwq])
        P.dma("sp", lambda e: e.dma_start(out=wkv[:], in_=wbukv), reads=[bwukv], writes=[bwkv])
        permM = C.cstb[:, CO["permM"]:CO["permM"] + 128]
        P.op("dve", lambda e: e.memset(krb[:], 0.0), writes=[bkrb])
        P.op("dve", lambda e: e.memset(qrb[:], 0.0), writes=[bqrb])
        for s in range(C.NSEQ):
            P.dma("sp", lambda e, s=s: e.dma_start(out=cosT[:], in_=C.ROPE[s, 2]), reads=[C.ropeb[s]], writes=[brope])
            P.dma("sp", lambda e, s=s: e.dma_start(out=sinT[:], in_=C.ROPE[s, 3]), reads=[C.ropeb[s]], writes=[brope])
            for c in range(3):
                pj_load(C, cq[:, c, :], bcq, "cq%d" % c, s)
            pj_load(C, ckv[:], bckv, "ckv0", s)
            pj_load(C, ckr[0:64, :], bckr, "ckr0", s, rows=64)
            for t0 in range(0, S, 512):
                n = min(512, S - t0)
                for c in range(3):
                    P.op("act", lambda e, c=c, t0=t0, n=n: e.activation(out=sq3[:, c, 0:n], in_=cq[:, c, t0:t0 + n], func=AF.Square), reads=[bcq], writes=[bsq])
                colnorm_rstd(C, [sq3[:, c, 0:n] for c in range(3)], bsq, 384.0, rstd, brstd, n)
                for c in range(3):
                    gc = pv_col(l, "mla_qg", c)
                    P.op("dve", lambda e, c=c, t0=t0, n=n: e.tensor_tensor(out=tmp[:, 0:n], in0=cq[:, c, t0:t0 + n], in1=rstd[:, 0:n], op=ALU.mult),
                         reads=[bcq, brstd], writes=[btmp])
                    P.op("dve", lambda e, c=c, t0=t0, n=n, gc=gc: e.tensor_scalar(out=cqn[:, c, t0:t0 + n], in0=tmp[:, 0:n], scalar1=C.pvec[:, gc:gc + 1], scalar2=None, op0=ALU.mult),
                         reads=[btmp, C.b_pvec], writes=[bcqn])
                P.op("act", lambda e, t0=t0, n=n: e.activation(out=sq3[:, 0, 0:n], in_=ckv[:, t0:t0 + n], func=AF.Square), reads=[bckv], writes=[bsq])
                colnorm_rstd(C, [sq3[:, 0, 0:n]], bsq, 128.0, rstd, brstd, n)
                gc = pv_col(l, "mla_kvg")
                P.op("dve", lambda e, t0=t0, n=n: e.tensor_tensor(out=tmp[:, 0:n], in0=ckv[:, t0:t0 + n], in1=rstd[:, 0:n], op=ALU.mult), reads=[bckv, brstd], writes=[btmp])
                P.op("dve", lambda e, t0=t0, n=n, gc=gc: e.tensor_scalar(out=ckvn[:, t0:t0 + n], in0=tmp[:, 0:n], scalar1=C.pvec[:, gc:gc + 1], scalar2=None, op0=ALU.mult),
                     reads=[btmp, C.b_pvec], writes=[bckvn])
            rope_fm(C, ckr, bckr, xb, bxb, cosT, sinT, brope, permM, 64, krb, bkrb, tmp, btmp, S)
            for tc in range(NT):
                bank = 2 + tc % 2
                def vm(e, tc=tc, bank=bank):
                    for h in range(4):
                        ins = e.matmul(ps[:, bank, h * 128:(h + 1) * 128], lhsT=ckvn[:, tc * 128:(tc + 1) * 128], rhs=wkv[:, h * 256 + 128:h * 256 + 256], start=True, stop=True)
                    return ins
                P.op("pe", vm, reads=[bckvn, bwkv], writes=[pb[bank]])
                P.op("act", lambda e, tc=tc, bank=bank: e.copy(out=vt[:, tc, :, :], in_=ps[:, bank, :].rearrange("p (h d) -> p h d", h=4)), reads=[pb[bank]], writes=[bvt])
            for h in range(4):
                for t0 in range(0, S, 512):
                    n = min(512, S - t0)
                    P.op("pe", lambda e, h=h, t0=t0, n=n: e.matmul(ps[:, 2, 0:n], lhsT=wkv[:, h * 256:h * 256 + 128], rhs=ckvn[:, t0:t0 + n], start=True, stop=True),
                         reads=[bckvn, bwkv], writes=[pb[2]])
                    P.op("act", lambda e, t0=t0, n=n: e.copy(out=knb[:, t0:t0 + n], in_=ps[:, 2, 0:n]), reads=[pb[2]], writes=[bknb])
                    def qn(e, h=h, t0=t0, n=n):
                        for c in range(3):
                            ins = e.matmul(ps[:, 3, 0:n], lhsT=wq[:, c, h * 192:h * 192 + 128], rhs=cqn[:, c, t0:t0 + n], start=(c == 0), stop=(c == 2))
                        return ins
                    P.op("pe", qn, reads=[bcqn, bwq], writes=[pb[3]])
                    P.op("dve", lambda e, t0=t0, n=n: e.tensor_copy(out=qnb[:, t0:t0 + n], in_=ps[:, 3, 0:n]), reads=[pb[3]], writes=[bqnb])
                    def qr(e, h=h, t0=t0, n=n):
                        for c in range(3):
                            ins = e.matmul(ps[0:64, 2, 0:n], lhsT=wq[:, c, h * 192 + 128:h * 192 + 192], rhs=cqn[:, c, t0:t0 + n], start=(c == 0), stop=(c == 2))
                        return ins
                    P.op("pe", qr, reads=[bcqn, bwq], writes=[pb[2]])
                    P.op("act", lambda e, t0=t0, n=n: e.copy(out=qrf[0:64, t0:t0 + n], in_=ps[0:64, 2, 0:n]), reads=[pb[2]], writes=[bqrf])
                rope_fm(C, qrf, bqrf, xb, bxb, cosT, sinT, brope, permM, 64, qrb, bqrb, tmp, btmp, S)
                for qb in range(S // QB):
                    attn_core(C, T_, [(qnb[:, :], knb[:, :]), (qrb[:, :], krb[:, :])], vt[:, :, h, :], bvt, 192.0 ** -0.5, qb,
                              rl, brl, 4, 6, [bqnb, bknb, bqrb, bkrb])
                    P.op("dve", lambda e: e.tensor_tensor(out=ob[:, 0:QB], in0=ps[:, 4, 0:QB], in1=rl[:, 0:QB], op=ALU.mult), reads=[pb[4], brl], writes=[bob])
                    om_store(C, ob[:, 0:QB], bob, 1024 + h * 128, s, qb * QB, QB)
        P.barrier()


def gla_phase(C, l, w_g2l):
    nc, P, ps, pb = C.nc, C.P, C.ps, C.pb
    S = C.S
    NCk = S // 64
    NBk = S // 128
    with contextlib.ExitStack() as st:
        def sb(name, shape, dt):
            return st.enter_context(nc.sbuf_tensor(uname(name), list(shape), dt))
        rm = sb("g_rm", [128, S], F32)
        wg2f, wg2 = sb("g_w2f", [16, 256], F32), sb("g_w2", [16, 256], BF16)
        glf, glb = sb("g_glf", [16, S], F32), sb("g_glb", [16, S], BF16)
        qf, kf = sb("g_qf", [128, S], F32), sb("g_kf", [128, S], F32)
        g_, b_ = sb("g_g", [128, S], F32), sb("g_b", [128, S], F32)
        eb, enb = sb("g_eb", [128, S], F32), sb("g_enb", [128, S], F32)
        kd = sb("g_kd", [128, S], F32)
        qt, kt = sb("g_qt", [128, S], BF16), sb("g_kt", [128, S], BF16)
        kd_tok = sb("g_kdt", [128, NBk, 128], BF16)
        vf = sb("g_vf", [128, S], F32)
        vtok = [sb("g_vt%d" % i, [128, NBk, 128], BF16) for i in range(2)]
        vpar = [[sb("g_vp%d%d" % (i, j), [128, NBk, 128], BF16) for j in range(2)] for i in range(2)]
        qtm = [sb("g_qtm%d" % i, [128, S], BF16) for i in range(2)]
        bqtm = [Buf(), Buf()]
        bvpar = [Buf(), Buf()]
        rf = sb("g_rf", [128, S], F32)
        Sall = sb("g_Sall", [128, NCk, 128], F32)
        Sbf = sb("g_Sbf", [128, NCk, 128], BF16)
        att = [sb("g_att%d" % i, [128, 4, 128], BF16) for i in range(2)]
        osb, sq, rstd, sr = sb("g_osb", [128, 512], F32), sb("g_sq", [128, 512], F32), sb("g_rstd", [128, 512], F32), sb("g_sr", [128, 512], F32)
        ob = sb("g_ob", [128, 512], BF16)
        brm, bw2, bgl, bq, bk, bg, bb, beb, benb, bkd, bqt, bkt, bkdt, bvf, brf, bS, bSb, bosb, bsq, brstd, bsr, bob = [Buf() for _ in range(22)]
        bvt = [Buf(), Buf()]
        batt = [Buf(), Buf()]
        P.op("dve", lambda e: e.memset(rm[:], 1.0), writes=[brm])
        for i in range(2):
            P.op("dve", lambda e, i=i: e.memset(qtm[i][:], 0.0), writes=[bqtm[i]])
            for j in range(2):
                P.op("dve", lambda e, i=i, j=j: e.memset(vpar[i][j][:], 0.0), writes=[bvpar[i]])
        P.op("dve", lambda e: e.memset(rm[:].rearrange("p (n c) -> p n c", c=64)[:, :, 0:1], 0.0), writes=[brm])
        P.dma("sp", lambda e: e.dma_start(out=wg2f[:], in_=w_g2l), writes=[bw2])
        P.op("dve", lambda e: e.tensor_copy(out=wg2[:], in_=wg2f[:]), reads=[bw2], writes=[bw2])
        onecol = C.cst[:, CO["ones"]:CO["ones"] + 1]
        maskG = C.cstb[:, CO["maskG"]:CO["maskG"] + 128]
        ai = 0
        for s in range(C.NSEQ):
            pj_load(C, glf[0:16, :], bgl, "bg0", s, rows=16)
            P.op("dve", lambda e: e.tensor_copy(out=glb[:], in_=glf[:]), reads=[bgl], writes=[bgl])
            for hp in range(2):
                pj_load(C, qf[:], bq, "bq%d" % hp, s)
                pj_load(C, kf[:], bk, "bk%d" % hp, s)
                nb = l * AUX_N + 1 + hp
                for t0 in range(0, S, 512):
                    n = min(512, S - t0)
                    P.op("pe", lambda e, hp=hp, t0=t0, n=n: e.matmul(ps[:, 0, 0:n], lhsT=wg2[0:16, hp * 128:(hp + 1) * 128], rhs=glb[0:16, t0:t0 + n], start=True, stop=True),
                         reads=[bw2, bgl], writes=[pb[0]])
                    P.op("act", lambda e, t0=t0, n=n, nb=nb: e.activation(out=g_[:, t0:t0 + n], in_=ps[:, 0, 0:n], func=AF.Exp, scale=-1.0, bias=C.aux[:, nb:nb + 1]),
                         reads=[pb[0], C.b_aux], writes=[bg])
                P.op("act", lambda e: e.activation(out=g_[:], in_=g_[:], func=AF.Ln, bias=onecol, scale=1.0), reads=[bg, C.b_cst], writes=[bg])
                P.op("dve", lambda e: e.tensor_scalar(out=g_[:], in0=g_[:], scalar1=-1.0 / 16.0, scalar2=None, op0=ALU.mult), reads=[bg], writes=[bg])
                P.op("dve", lambda e: e.tensor_tensor_scan(out=b_[:], data0=rm[:], data1=g_[:], initial=0.0, op0=ALU.mult, op1=ALU.add), reads=[bg, brm], writes=[bb])
                P.op("act", lambda e: e.activation(out=eb[:], in_=b_[:], func=AF.Exp), reads=[bb], writes=[beb])
                P.op("act", lambda e: e.activation(out=enb[:], in_=b_[:], func=AF.Exp, scale=-1.0), reads=[bb], writes=[benb])
                P.op("dve", lambda e: e.scalar_tensor_tensor(out=qt[:], in0=qf[:], scalar=0.125, in1=eb[:], op0=ALU.mult, op1=ALU.mult), reads=[bq, beb], writes=[bqt])
                P.op("dve", lambda e: e.tensor_tensor(out=kt[:], in0=kf[:], in1=enb[:], op=ALU.mult), reads=[bk, benb], writes=[bkt])
                P.op("act", lambda e: e.copy(out=qtm[0][0:64, :], in_=qt[0:64, :]), reads=[bqt], writes=[bqtm[0]])
                P.op("act", lambda e: e.copy(out=qtm[1][64:128, :], in_=qt[64:128, :]), reads=[bqt], writes=[bqtm[1]])
                b3 = b_[:].rearrange("p (n c) -> p n c", c=64)
                P.op("dve", lambda e, b3=b3: e.tensor_tensor(out=kd[:].rearrange("p (n c) -> p n c", c=64), in0=b3[:, :, 63:64].broadcast_to([128, NCk, 64]), in1=b3, op=ALU.subtract),
                     reads=[bb], writes=[bkd])
                P.op("act", lambda e: e.activation(out=kd[:], in_=kd[:], func=AF.Exp), reads=[bkd], writes=[bkd])
                P.op("dve", lambda e: e.tensor_tensor(out=kd[:], in0=kd[:], in1=kf[:], op=ALU.mult), reads=[bkd, bk], writes=[bkd])
                if GLA_STOP <= 1:
                    continue
                transpose_to_tok(C, kd, bkd, kd_tok, bkdt, S)
                for hh in range(2):
                    pj_load(C, vf[:], bvf, "bv%d" % (hp * 2 + hh), s)
                    transpose_to_tok(C, vf, bvf, vtok[hh], bvt[hh], S)
                    P.op("dve", lambda e, hh=hh: e.tensor_copy(out=vpar[hh][0][0:64, :, :], in_=vtok[hh][0:64, :, :]), reads=[bvt[hh]], writes=[bvpar[hh]])
                    P.op("dve", lambda e, hh=hh: e.tensor_copy(out=vpar[hh][1][64:128, :, :], in_=vtok[hh][64:128, :, :]), reads=[bvt[hh]], writes=[bvpar[hh]])
                if GLA_STOP <= 2:
                    continue
                for cg in range(NCk // 4):
                    bank = 4 + cg % 2
                    def inc(e, cg=cg, bank=bank):
                        for c in range(4):
                            n = cg * 4 + c
                            m, par = n // 2, n % 2
                            for hh in range(2):
                                ins = e.matmul(ps[hh * 64:(hh + 1) * 64, bank, c * 128:(c + 1) * 128], lhsT=kd_tok[:, m, hh * 64:(hh + 1) * 64],
                                               rhs=vpar[hh][par][:, m, :], start=True, stop=True)
                        return ins
                    P.op("pe", inc, reads=[bkdt, bvpar[0], bvpar[1]], writes=[pb[bank]])
                    for c in range(4):
                        n = cg * 4 + c
                        if n == 0:
                            P.op("dve", lambda e, bank=bank: e.tensor_copy(out=Sall[:, 0, :], in_=ps[:, bank, 0:128]), reads=[pb[bank]], writes=[bS])
                        else:
                            P.op("dve", lambda e, n=n, c=c, bank=bank: e.scalar_tensor_tensor(
                                out=Sall[:, n, :], in0=Sall[:, n - 1, :], scalar=eb[:, n * 64 + 63:n * 64 + 64], in1=ps[:, bank, c * 128:(c + 1) * 128],
                                op0=ALU.mult, op1=ALU.add), reads=[pb[bank], bS, beb], writes=[bS])
                    P.op("act", lambda e, cg=cg: e.copy(out=Sbf[:, cg * 4:(cg + 1) * 4, :], in_=Sall[:, cg * 4:(cg + 1) * 4, :]), reads=[bS], writes=[bSb])
                if GLA_STOP <= 3:
                    continue
                for hh in range(2):
                    A = hp * 2 + hh
                    base = hh * 64
                    pj_load(C, rf[:], brf, "br%d" % A, s)
                    for g4 in range(0, NBk, 4):
                        cnt = min(4, NBk - g4)
                        W = cnt * 128
                        sl = ai % 2
                        ai += 1
                        def attm(e, g4=g4, cnt=cnt, hh=hh):
                            for j in range(cnt):
                                m = g4 + j
                                ins = e.matmul(ps[:, 0, j * 128:(j + 1) * 128], lhsT=kt[:, m * 128:(m + 1) * 128], rhs=qtm[hh][:, m * 128:(m + 1) * 128],
                                               start=True, stop=True)
                            return ins
                        P.op("pe", attm, reads=[bkt, bqtm[hh]], writes=[pb[0]])
                        P.op("dve", lambda e, sl=sl, cnt=cnt: e.tensor_tensor(out=att[sl][:, 0:cnt, :], in0=ps[:, 0, 0:cnt * 128].rearrange("p (j t) -> p j t", j=cnt),
                                                                           in1=maskG.unsqueeze(1).broadcast_to([128, cnt, 128]), op=ALU.mult),
                             reads=[pb[0], C.b_cstb], writes=[batt[sl]])
                        def om(e, g4=g4, cnt=cnt, base=base, hh=hh, sl=sl):
                            for j in range(cnt):
                                m = g4 + j
                                ins = e.matmul(ps[:, 1, j * 128:(j + 1) * 128], lhsT=vtok[hh][:, m, :], rhs=att[sl][:, j, :], start=True, stop=(m == 0))
                                for par in range(2):
                                    n = 2 * m + par
                                    if n > 0:
                                        ins = e.matmul(ps[:, 1, j * 128 + par * 64:j * 128 + par * 64 + 64], lhsT=Sbf[:, n - 1, :],
                                                       rhs=qtm[hh][:, n * 64:(n + 1) * 64], start=False, stop=True)
                            return ins
                        P.op("pe", om, reads=[bvt[hh], batt[sl], bSb, bqtm[hh]], writes=[pb[1]])
                        t0 = g4 * 128
                        P.op("act", lambda e, W=W: e.copy(out=osb[:, 0:W], in_=ps[:, 1, 0:W]), reads=[pb[1]], writes=[bosb])
                        P.op("act", lambda e, W=W: e.activation(out=sq[:, 0:W], in_=ps[:, 1, 0:W], func=AF.Square), reads=[pb[1]], writes=[bsq])
                        colnorm_rstd(C, [sq[:, 0:W]], bsq, 128.0, rstd, brstd, W)
                        P.op("dve", lambda e, W=W: e.tensor_tensor(out=osb[:, 0:W], in0=osb[:, 0:W], in1=rstd[:, 0:W], op=ALU.mult), reads=[bosb, brstd], writes=[bosb])
                        P.op("act", lambda e, W=W, t0=t0: e.activation(out=sr[:, 0:W], in_=rf[:, t0:t0 + W], func=AF.Silu), reads=[brf], writes=[bsr])
                        gc = pv_col(l, "gla_ng")
                        P.op("dve", lambda e, W=W, gc=gc: e.scalar_tensor_tensor(out=ob[:, 0:W], in0=osb[:, 0:W], scalar=C.pvec[:, gc:gc + 1], in1=sr[:, 0:W],
                                                                              op0=ALU.mult, op1=ALU.mult), reads=[bosb, bsr, C.b_pvec], writes=[bob])
                        om_store(C, ob[:, 0:W], bob, 512 + A * 128, s, t0, W)
        P.barrier()


def ssd_phase(C, l):
    nc, P, ps, pb = C.nc, C.P, C.ps, C.pb
    S = C.S
    NCk = S // 128
    cst = C.cst
    onecol = cst[:, CO["ones"]:CO["ones"] + 1]
    maskS = cst[:, CO["maskS"]:CO["maskS"] + 128]
    with contextlib.ExitStack() as st:
        def sb(name, shape, dt):
            return st.enter_context(nc.sbuf_tensor(uname(name), list(shape), dt))
        xs = [sb("d_xs%d" % i, [128, S], F32) for i in range(4)]
        Bb = [sb("d_Bb%d" % i, [128, S], BF16) for i in range(2)]
        Cb = [sb("d_Cb%d" % i, [128, S], BF16) for i in range(2)]
        Btok = sb("d_Btok", [128, NCk, 256], BF16)
        dt8, dA8, ac8, ec8, rm8 = [sb("d_s%d" % i, [8, S], F32) for i in range(5)]
        raw = [sb("d_raw%d" % i, [128, S + 3], F32) for i in range(2)]
        acc = sb("d_acc", [128, S], F32)
        xdt_tok = sb("d_xdtt", [128, NCk, 512], BF16)
        xdd = [sb("d_xdd%d" % i, [128, 512], BF16) for i in range(2)]
        acol, dcol = sb("d_acol", [128, NCk, 8], F32), sb("d_dcol", [128, NCk, 8], F32)
        R8 = sb("d_R8", [8, NCk, 8], F32)
        cdrow = sb("d_cdrow", [128, NCk, 8], F32)
        stt = [sb("d_st%d" % i, [128, 512], F32) for i in range(2)]
        prevb = sb("d_prevb", [128, NCk, 512], BF16)
        zt = [sb("d_zt%d" % i, [128, 4, 128], F32) for i in range(2)]
        cbm = [sb("d_cbm%d" % i, [128, 128], F32) for i in range(2)]
        Dm = [sb("d_Dm%d" % i, [128, 128], F32) for i in range(2)]
        Mh = [sb("d_Mh%d" % i, [128, 128], BF16) for i in range(2)]
        ecb = [sb("d_ecb%d" % i, [128, 128], F32) for i in range(2)]
        yo = sb("d_yo", [128, 4, 128], F32)
        ysq = sb("d_ysq", [128, 4, 128], F32)
        rstd = sb("d_rstd", [128, 128], F32)
        ob = sb("d_ob", [128, 4, 128], BF16)
        bxs = [Buf() for _ in range(4)]
        bBb, bCb = [Buf(), Buf()], [Buf(), Buf()]
        bBtok, b8, braw, bacc, bxdtt, bcol, bR8, bcdrow, bprev, byo, bysq, brstd, bob = [Buf() for _ in range(13)]
        braw = [Buf(), Buf()]
        bxdd, bst, bzt, bcbm, bDm, bMh, becb = [[Buf(), Buf()] for _ in range(7)]
        P.op("dve", lambda e: e.memset(rm8[:], 1.0), writes=[b8])
        P.op("dve", lambda e: e.memset(rm8[:].rearrange("p (n c) -> p n c", c=128)[:, :, 0:1], 0.0), writes=[b8])
        for i in range(2):
            P.op("dve", lambda e, i=i: e.memset(raw[i][:, 0:3], 0.0), writes=[braw[i]])
        nega = C.aux[0:8, l * AUX_N + 3:l * AUX_N + 4]
        dtb = C.pvec[0:8, pv_col(l, "dt_bias"):pv_col(l, "dt_bias") + 1]
        ri = 0
        for s in range(C.NSEQ):
            for ti in range(8):
                r = ri % 2
                ri += 1
                pj_load(C, raw[r][:, 3:3 + S], braw[r], "dx%d" % ti, s)
                cw = pv_col(l, "conv_w")
                P.op("dve", lambda e, r=r, ti=ti, cw=cw: e.tensor_scalar(out=acc[:], in0=raw[r][:, 0:S], scalar1=C.pvec[:, cw + ti:cw + ti + 1], scalar2=None, op0=ALU.mult),
                     reads=[braw[r], C.b_pvec], writes=[bacc])
                for k in range(1, 4):
                    P.op("dve", lambda e, r=r, ti=ti, cw=cw, k=k: e.scalar_tensor_tensor(out=acc[:], in0=raw[r][:, k:k + S], scalar=C.pvec[:, cw + k * 8 + ti:cw + k * 8 + ti + 1],
                                                                                   in1=acc[:], op0=ALU.mult, op1=ALU.add), reads=[braw[r], bacc, C.b_pvec], writes=[bacc])
                cb_ = pv_col(l, "conv_b", ti)
                if ti < 4:
                    P.op("act", lambda e, ti=ti, cb_=cb_: e.activation(out=xs[ti][:], in_=acc[:], func=AF.Silu, bias=C.pvec[:, cb_:cb_ + 1], scale=1.0),
                         reads=[bacc, C.b_pvec], writes=[bxs[ti]])
                else:
                    P.op("act", lambda e, cb_=cb_: e.activation(out=acc[:], in_=acc[:], func=AF.Silu, bias=C.pvec[:, cb_:cb_ + 1], scale=1.0),
                         reads=[bacc, C.b_pvec], writes=[bacc])
                    g = (ti - 4) % 2
                    if ti < 6:
                        P.op("dve", lambda e, g=g: e.tensor_copy(out=Bb[g][:], in_=acc[:]), reads=[bacc], writes=[bBb[g]])
                        for c4 in range(0, NCk, 4):
                            cnt = min(4, NCk - c4)
                            bank = 2 + (c4 // 4) % 2
                            def tr(e, c4=c4, cnt=cnt, bank=bank):
                                for j in range(cnt):
                                    ins = e.transpose(out=ps[:, bank, j * 128:(j + 1) * 128], in_=acc[:, (c4 + j) * 128:(c4 + j + 1) * 128], identity=C.ident)
                                return ins
                            P.op("pe", tr, reads=[bacc, C.b_cst], writes=[pb[bank]])
                            P.op("act", lambda e, c4=c4, cnt=cnt, bank=bank, g=g: e.copy(out=Btok[:, c4:c4 + cnt, g * 128:(g + 1) * 128],
                                                                                     in_=ps[:, bank, 0:cnt * 128].rearrange("p (j t) -> p j t", j=cnt)),
                                 reads=[pb[bank]], writes=[bBtok])
                    else:
                        P.op("dve", lambda e, g=g: e.tensor_copy(out=Cb[g][:], in_=acc[:]), reads=[bacc], writes=[bCb[g]])
            pj_load(C, dt8[:], b8, "dt0", s, rows=8)
            P.op("act", lambda e: e.activation(out=dt8[:], in_=dt8[:], func=AF.Exp, bias=dtb, scale=1.0), reads=[b8, C.b_pvec], writes=[b8])
            P.op("act", lambda e: e.activation(out=dt8[:], in_=dt8[:], func=AF.Ln, bias=onecol[0:8, :], scale=1.0), reads=[b8, C.b_cst], writes=[b8])
            P.op("dve", lambda e: e.tensor_scalar(out=dA8[:], in0=dt8[:], scalar1=nega, scalar2=None, op0=ALU.mult), reads=[b8, C.b_aux], writes=[b8])
            P.op("dve", lambda e: e.tensor_tensor_scan(out=ac8[:], data0=rm8[:], data1=dA8[:], initial=0.0, op0=ALU.mult, op1=ALU.add), reads=[b8], writes=[b8])
            P.op("act", lambda e: e.activation(out=ec8[:], in_=ac8[:], func=AF.Exp), reads=[b8], writes=[b8])
            a3 = ac8[:].rearrange("p (n c) -> p n c", c=128)
            P.op("dve", lambda e, a3=a3: e.tensor_tensor(out=dA8[:].rearrange("p (n c) -> p n c", c=128), in0=a3[:, :, 127:128].broadcast_to([8, NCk, 128]), in1=a3, op=ALU.subtract),
                 reads=[b8], writes=[b8])
            P.op("act", lambda e: e.activation(out=dA8[:], in_=dA8[:], func=AF.Exp), reads=[b8], writes=[b8])
            for (src, dst) in ((ac8, acol), (dA8, dcol)):
                for c4 in range(0, NCk, 4):
                    cnt = min(4, NCk - c4)
                    def tr(e, src=src, c4=c4, cnt=cnt):
                        for j in range(cnt):
                            ins = e.transpose(out=ps[:, 2, j * 8:(j + 1) * 8], in_=src[0:8, (c4 + j) * 128:(c4 + j + 1) * 128], identity=C.ident[0:8, 0:8])
                        return ins
                    P.op("pe", tr, reads=[b8, C.b_cst], writes=[pb[2]])
                    P.op("dve", lambda e, dst=dst, c4=c4, cnt=cnt: e.tensor_copy(out=dst[:, c4:c4 + cnt, :], in_=ps[:, 2, 0:cnt * 8].rearrange("p (j h) -> p j h", j=cnt)),
                         reads=[pb[2]], writes=[bcol])
            e3 = ec8[:].rearrange("p (n c) -> p n c", c=128)
            P.op("dve", lambda e, e3=e3: e.tensor_tensor(out=R8[:], in0=C.ident[0:8, 0:8].unsqueeze(1).broadcast_to([8, NCk, 8]), in1=e3[:, :, 127:128].broadcast_to([8, NCk, 8]), op=ALU.mult),
                 reads=[b8, C.b_cst], writes=[bR8])
            P.op("pe", lambda e: e.matmul(ps[:, 3, 0:NCk * 8], lhsT=C.ones[0:8, :], rhs=R8[:].rearrange("p n h -> p (n h)"), start=True, stop=True), reads=[bR8, C.b_cst], writes=[pb[3]])
            P.op("dve", lambda e: e.tensor_copy(out=cdrow[:].rearrange("p n h -> p (n h)"), in_=ps[:, 3, 0:NCk * 8]), reads=[pb[3]], writes=[bcdrow])
            for pr in range(4):
                for t0 in range(0, S, 512):
                    n = min(512, S - t0)
                    so = CO["selE"] + pr * 128
                    P.op("pe", lambda e, so=so, t0=t0, n=n: e.matmul(ps[:, 3, 0:n], lhsT=cst[0:8, so:so + 128], rhs=dt8[:, t0:t0 + n], start=True, stop=True),
                         reads=[b8, C.b_cst], writes=[pb[3]])
                    P.op("dve", lambda e, pr=pr, t0=t0, n=n: e.tensor_tensor(out=acc[:, t0:t0 + n], in0=xs[pr][:, t0:t0 + n], in1=ps[:, 3, 0:n], op=ALU.mult),
                         reads=[pb[3], bxs[pr]], writes=[bacc])
                for c4 in range(0, NCk, 4):
                    cnt = min(4, NCk - c4)
                    bank = 2 + (c4 // 4) % 2
                    def tr(e, c4=c4, cnt=cnt, bank=bank):
                        for j in range(cnt):
                            ins = e.transpose(out=ps[:, bank, j * 128:(j + 1) * 128], in_=acc[:, (c4 + j) * 128:(c4 + j + 1) * 128], identity=C.ident)
                        return ins
                    P.op("pe", tr, reads=[bacc, C.b_cst], writes=[pb[bank]])
                    P.op("act", lambda e, c4=c4, cnt=cnt, bank=bank, pr=pr: e.copy(out=xdt_tok[:, c4:c4 + cnt, pr * 128:(pr + 1) * 128],
                                                                              in_=ps[:, bank, 0:cnt * 128].rearrange("p (j t) -> p j t", j=cnt)),
                         reads=[pb[bank]], writes=[bxdtt])
            for n in range(NCk - 1):
                sl = n % 2
                P.op("dve", lambda e, n=n, sl=sl: e.tensor_tensor(out=xdd[sl][:].rearrange("p (h q) -> p h q", h=8), in0=xdt_tok[:, n, :].rearrange("p (h q) -> p h q", h=8),
                                                               in1=dcol[:, n, :].unsqueeze(2).broadcast_to([128, 8, 64]), op=ALU.mult), reads=[bxdtt, bcol], writes=[bxdd[sl]])
                bank = 4 + n % 2
                def incm(e, n=n, sl=sl, bank=bank):
                    for g in range(2):
                        ins = e.matmul(ps[:, bank, g * 256:(g + 1) * 256], lhsT=Btok[:, n, g * 128:(g + 1) * 128], rhs=xdd[sl][:, g * 256:(g + 1) * 256], start=True, stop=True)
                    return ins
                P.op("pe", incm, reads=[bBtok, bxdd[sl]], writes=[pb[bank]])
                cur, prv = stt[n % 2], stt[(n + 1) % 2]
                if n == 0:
                    P.op("dve", lambda e, cur=cur, bank=bank: e.tensor_copy(out=cur[:], in_=ps[:, bank, :]), reads=[pb[bank]], writes=[bst[n % 2]])
                else:
                    P.op("dve", lambda e, n=n, cur=cur, prv=prv: e.tensor_tensor(out=cur[:].rearrange("p (h q) -> p h q", h=8), in0=prv[:].rearrange("p (h q) -> p h q", h=8),
                                                                              in1=cdrow[:, n, :].unsqueeze(2).broadcast_to([128, 8, 64]), op=ALU.mult),
                         reads=[bst[(n + 1) % 2], bcdrow], writes=[bst[n % 2]])
                    P.op("dve", lambda e, cur=cur, bank=bank: e.tensor_tensor(out=cur[:], in0=cur[:], in1=ps[:, bank, :], op=ALU.add), reads=[pb[bank], bst[n % 2]], writes=[bst[n % 2]])
                P.op("act", lambda e, n=n, cur=cur: e.copy(out=prevb[:, n + 1, :], in_=cur[:]), reads=[bst[n % 2]], writes=[bprev])
            hi = 0
            for n in range(NCk):
                tsl = slice(n * 128, (n + 1) * 128)
                zs = n % 2
                for pr in range(4):
                    pj_load(C, zt[zs][:, pr, :], bzt[zs], "dz%d" % pr, s, t0=n * 128, nt=128)
                for g in range(2):
                    cs = (n * 2 + g) % 2
                    P.op("pe", lambda e, g=g, tsl=tsl: e.matmul(ps[:, 0, 0:128], lhsT=Bb[g][:, tsl], rhs=Cb[g][:, tsl], start=True, stop=True), reads=[bBb[g], bCb[g]], writes=[pb[0]])
                    P.op("dve", lambda e, cs=cs: e.tensor_tensor(out=cbm[cs][:], in0=ps[:, 0, 0:128], in1=maskS, op=ALU.mult), reads=[pb[0], C.b_cst], writes=[bcbm[cs]])
                    for hh in range(4):
                        h = g * 4 + hh
                        pr = h // 2
                        k = hi % 2
                        hi += 1
                        so = CO["selH"] + h * 128
                        P.op("pe", lambda e, so=so, tsl=tsl: e.matmul(ps[:, 1, 0:128], lhsT=cst[0:8, so:so + 128], rhs=ac8[:, tsl], start=True, stop=True), reads=[b8, C.b_cst], writes=[pb[1]])
                        P.op("dve", lambda e, k=k, n=n, h=h: e.tensor_scalar(out=Dm[k][:], in0=ps[:, 1, 0:128], scalar1=acol[:, n, h:h + 1], scalar2=None, op0=ALU.subtract),
                             reads=[pb[1], bcol], writes=[bDm[k]])
                        P.op("dve", lambda e, k=k: e.tensor_scalar(out=Dm[k][:], in0=Dm[k][:], scalar1=0.0, scalar2=None, op0=ALU.min), reads=[bDm[k]], writes=[bDm[k]])
                        P.op("act", lambda e, k=k: e.activation(out=Dm[k][:], in_=Dm[k][:], func=AF.Exp), reads=[bDm[k]], writes=[bDm[k]])
                        P.op("dve", lambda e, k=k, cs=cs: e.tensor_tensor(out=Mh[k][:], in0=Dm[k][:], in1=cbm[cs][:], op=ALU.mult), reads=[bDm[k], bcbm[cs]], writes=[bMh[k]])
                        P.op("pe", lambda e, k=k, n=n, h=h, pr=pr: e.matmul(ps[(h % 2) * 64:(h % 2) * 64 + 64, 6, pr * 128:(pr + 1) * 128], lhsT=xdt_tok[:, n, h * 64:(h + 1) * 64], rhs=Mh[k][:],
                                                                          start=True, stop=True), reads=[bxdtt, bMh[k]], writes=[pb[6]])
                for pr in range(4):
                    g = pr // 2
                    P.op("dve", lambda e, pr=pr: e.tensor_copy(out=yo[:, pr, :], in_=ps[:, 6, pr * 128:(pr + 1) * 128]), reads=[pb[6]], writes=[byo])
                    if n > 0:
                        k = pr % 2
                        so = CO["selE"] + pr * 128
                        P.op("pe", lambda e, so=so, tsl=tsl: e.matmul(ps[:, 1, 0:128], lhsT=cst[0:8, so:so + 128], rhs=ec8[:, tsl], start=True, stop=True), reads=[b8, C.b_cst], writes=[pb[1]])
                        P.op("act", lambda e, k=k: e.copy(out=ecb[k][:], in_=ps[:, 1, 0:128]), reads=[pb[1]], writes=[becb[k]])
                        P.op("pe", lambda e, n=n, pr=pr, g=g, tsl=tsl: e.matmul(ps[:, 7, 0:128], lhsT=prevb[:, n, pr * 128:(pr + 1) * 128], rhs=Cb[g][:, tsl], start=True, stop=True),
                             reads=[bprev, bCb[g]], writes=[pb[7]])
                        P.op("dve", lambda e, k=k: e.tensor_tensor(out=ecb[k][:], in0=ecb[k][:], in1=ps[:, 7, 0:128], op=ALU.mult), reads=[pb[7], becb[k]], writes=[becb[k]])
                        P.op("dve", lambda e, k=k, pr=pr: e.tensor_tensor(out=yo[:, pr, :], in0=yo[:, pr, :], in1=ecb[k][:], op=ALU.add), reads=[byo, becb[k]], writes=[byo])
                    dc = pv_col(l, "ssd_d", pr)
                    P.op("dve", lambda e, pr=pr, dc=dc, tsl=tsl: e.scalar_tensor_tensor(out=yo[:, pr, :], in0=xs[pr][:, tsl], scalar=C.pvec[:, dc:dc + 1], in1=yo[:, pr, :], op0=ALU.mult, op1=ALU.add),
                         reads=[byo, bxs[pr], C.b_pvec], writes=[byo])
                P.op("act", lambda e, zs=zs: e.activation(out=zt[zs][:], in_=zt[zs][:], func=AF.Silu), reads=[bzt[zs]], writes=[bzt[zs]])
                P.op("dve", lambda e, zs=zs: e.tensor_tensor(out=yo[:], in0=yo[:], in1=zt[zs][:], op=ALU.mult), reads=[byo, bzt[zs]], writes=[byo])
                P.op("act", lambda e: e.activation(out=ysq[:], in_=yo[:], func=AF.Square), reads=[byo], writes=[bysq])
                for g in range(2):
                    colnorm_rstd(C, [ysq[:, 2 * g, :], ysq[:, 2 * g + 1, :]], bysq, 256.0, rstd, brstd, 128)
                    for pr in (2 * g, 2 * g + 1):
                        gc = pv_col(l, "ssd_ng", pr)
                        P.op("dve", lambda e, pr=pr, gc=gc: e.scalar_tensor_tensor(out=ob[:, pr, :], in0=yo[:, pr, :], scalar=C.pvec[:, gc:gc + 1], in1=rstd[:, 0:128], op0=ALU.mult, op1=ALU.mult),
                             reads=[byo, brstd, C.b_pvec], writes=[bob])
                for pr in range(4):
                    om_store(C, ob[:, pr, :], bob, 1536 + pr * 128, s, n * 128, 128)
        P.barrier()


def merge_phase(C, l, wbin, bwin, wbbr, bwbr, wbout, bwout):
    nc, P, ps, pb = C.nc, C.P, C.ps, C.pb
    with contextlib.ExitStack() as st:
        def sb(name, shape, dt):
            return st.enter_context(nc.sbuf_tensor(uname(name), list(shape), dt))
        hf = sb("m_hf", [128, KC, TB], F32)
        hb = sb("m_hb", [128, KC, TB], BF16)
        om = sb("m_om", [128, 16, TB], BF16)
        mg = sb("m_mg", [128, 4, TB], F32)
        mgb = sb("m_mgb", [128, KC, TB], BF16)
        sgm = sb("m_sgm", [128, 4, TB], F32)
        slots = [sb("m_w%d" % i, [128, 16, 512], BF16) for i in range(3)]
        bsl = [Buf() for _ in range(3)]
        sq = [sb("m_sq%d" % i, [128, TB], F32) for i in range(2)]
        bsq = [Buf(), Buf()]
        mean, msq, rstd = sb("m_mean", [128, TB], F32), sb("m_msq", [128, TB], F32), sb("m_rstd", [128, TB], F32)
        bhf, bhb, bom, bmgb, bst = Buf(), Buf(), Buf(), Buf(), Buf()
        bmg = [Buf() for _ in range(4)]
        bsg = [Buf() for _ in range(4)]
        si = 0
        for blk in range(C.NB):
            P.dma("sp", lambda e, blk=blk: e.dma_start(
                out=hf[:], in_=C.HT[:, blk * TB:(blk + 1) * TB].rearrange("(k p) t -> p k t", p=128)), reads=[C.hbuf[blk]], writes=[bhf])
            P.dma("sp", lambda e, blk=blk: e.dma_start(
                out=om[:], in_=C.OM[:, blk * TB:(blk + 1) * TB].rearrange("(k p) t -> p k t", p=128)), reads=[C.omb[r][blk] for r in range(16)], writes=[bom])
            for h in range(4):
                if h % 2 == 0:
                    P.op("dve", lambda e, h=h: e.tensor_copy(out=hb[:, h * 4:(h + 1) * 4, :], in_=hf[:, h * 4:(h + 1) * 4, :]), reads=[bhf], writes=[bhb])
                else:
                    P.op("act", lambda e, h=h: e.copy(out=hb[:, h * 4:(h + 1) * 4, :], in_=hf[:, h * 4:(h + 1) * 4, :]), reads=[bhf], writes=[bhb])
            for fg in range(4):
                for br in range(4):
                    s = si % 3
                    si += 1
                    load_w(C, slots[s], bsl[s], wbin, bwin, 0, 16, GATE0 + br * 2048 + fg * 512, 512)
                    def gm(e, s=s):
                        for k in range(KC):
                            for j in range(4):
                                ins = e.matmul(ps[:, j, :], lhsT=slots[s][:, k, j * 128:(j + 1) * 128], rhs=hb[:, k, :], start=(k == 0), stop=(k == KC - 1))
                        return ins
                    P.op("pe", gm, reads=[bsl[s], bhb], writes=[pb[j] for j in range(4)])
                    s2 = si % 3
                    si += 1
                    load_w(C, slots[s2], bsl[s2], wbbr, bwbr, br * 512, 4, fg * 512, 512)
                    def bm(e, s2=s2, br=br):
                        for k in range(4):
                            for j in range(4):
                                ins = e.matmul(ps[:, 4 + j, :], lhsT=slots[s2][:, k, j * 128:(j + 1) * 128], rhs=om[:, br * 4 + k, :], start=(k == 0), stop=(k == 3))
                        return ins
                    P.op("pe", bm, reads=[bsl[s2], bom], writes=[pb[4 + j] for j in range(4)])
                    for j in range(4):
                        bc = pv_col(l, "b_gate", br * 16 + fg * 4 + j)
                        P.op("act", lambda e, j=j, bc=bc: e.activation(out=sgm[:, j, :], in_=ps[:, j, :], func=AF.Sigmoid, bias=C.pvec[:, bc:bc + 1], scale=1.0),
                             reads=[pb[j], C.b_pvec], writes=[bsg[j]])
                        if br == 0:
                            P.op("dve", lambda e, j=j: e.tensor_tensor(out=mg[:, j, :], in0=ps[:, 4 + j, :], in1=sgm[:, j, :], op=ALU.mult), reads=[pb[4 + j], bsg[j]], writes=[bmg[j]])
                        else:
                            P.op("dve", lambda e, j=j: e.tensor_tensor(out=sgm[:, j, :], in0=ps[:, 4 + j, :], in1=sgm[:, j, :], op=ALU.mult), reads=[pb[4 + j], bsg[j]], writes=[bsg[j]])
                            if br < 3:
                                P.op("dve", lambda e, j=j: e.tensor_tensor(out=mg[:, j, :], in0=mg[:, j, :], in1=sgm[:, j, :], op=ALU.add), reads=[bmg[j], bsg[j]], writes=[bmg[j]])
                            else:
                                P.op("dve", lambda e, j=j, fg=fg: e.tensor_tensor(out=mgb[:, fg * 4 + j, :], in0=mg[:, j, :], in1=sgm[:, j, :], op=ALU.add), reads=[bmg[j], bsg[j]], writes=[bmgb])
            for cg in range(4):
                banks = [(cg % 2) * 4 + j for j in range(4)]
                s = si % 3
                si += 1
                load_w(C, slots[s], bsl[s], wbout, bwout, 0, 16, cg * 512, 512)
                def omm(e, s=s, banks=banks):
                    for k in range(KC):
                        for j in range(4):
                            ins = e.matmul(ps[:, banks[j], :], lhsT=slots[s][:, k, j * 128:(j + 1) * 128], rhs=mgb[:, k, :], start=(k == 0), stop=(k == KC - 1))
                    return ins
                P.op("pe", omm, reads=[bsl[s], bmgb], writes=[pb[b] for b in banks])
                for j in range(4):
                    c = cg * 4 + j
                    P.op("dve", lambda e, c=c, b=banks[j]: e.scalar_tensor_tensor(out=hf[:, c, :], in0=hf[:, c, :], scalar=ALPHA, in1=ps[:, b, :], op0=ALU.mult, op1=ALU.add),
                         reads=[pb[banks[j]], bhf], writes=[bhf])
            ln_block(C, hf, bhf, l, 1, blk, (sq, bsq, mean, msq, rstd, bst))
        P.barrier()
```

```python
import math
import contextlib
import numpy as np
import concourse.bass as bass
import concourse.mybir as mybir
from concourse.bass_utils import run_bass_kernel_spmd

F32 = mybir.dt.float32
BF16 = mybir.dt.bfloat16
I32 = mybir.dt.int32
AF = mybir.ActivationFunctionType
ALU = mybir.AluOpType

D = 2048
DFF = 5632
KC = D // 128
FCH = DFF // 128
TB = 512
DEPTH = 4
ALPHA = (2 * DEPTH) ** 0.25
EPS = 1e-5
NIN = 13400
GATE0 = 5208
ROPE_THETA = 500000.0

ENGS = ("pe", "act", "dve", "pool", "sp")
N_DSEM = 12
N_WSEM = 4


class Buf:
    __slots__ = ("name", "w", "r")

    def __init__(self, name=""):
        self.name = name
        self.w = None
        self.r = []


class Prog:
    def __init__(self, nc):
        self.nc = nc
        self.eng = {"pe": nc.tensor, "act": nc.scalar, "dve": nc.vector, "pool": nc.gpsimd, "sp": nc.sync}
        self.seq = {e: 0 for e in ENGS}
        self.waited = {e: {} for e in ENGS}
        self.sems = {}
        for e in ENGS:
            self.sems["c" + e] = nc.alloc_semaphore("c_" + e)
        self.dtot = {}
        for i in range(N_DSEM):
            self.sems["d%d" % i] = nc.alloc_semaphore("dma%d" % i)
            self.dtot["d%d" % i] = 0
        for i in range(N_WSEM):
            self.sems["w%d" % i] = nc.alloc_semaphore("wdma%d" % i)
            self.dtot["w%d" % i] = 0
        self.rr = {"d": 0, "w": 0}
        self.n_ins = 0

    def _need(self, eng, tok, waits):
        if tok is None:
            return
        key, val, src = tok
        if src == eng and eng == "pe":
            return
        if self.waited[eng].get(key, 0) >= val:
            return
        self.waited[eng][key] = val
        waits[key] = max(waits.get(key, 0), val)

    def _deps(self, eng, reads, writes):
        waits = {}
        for b in reads:
            self._need(eng, b.w, waits)
        for b in writes:
            self._need(eng, b.w, waits)
            for t in b.r:
                self._need(eng, t, waits)
        return waits

    def _mark(self, tok, reads, writes):
        for b in reads:
            b.r.append(tok)
            if len(b.r) > 16:
                best = {}
                for t in b.r:
                    if t[0] not in best or best[t[0]][1] < t[1]:
                        best[t[0]] = t
                b.r = list(best.values())
        for b in writes:
            b.w = tok
            b.r = []

    def _emit_waits(self, eng, waits):
        e = self.eng[eng]
        for k, v in waits.items():
            e.wait_ge(self.sems[k], v)

    def op(self, eng, fn, reads=(), writes=()):
        waits = self._deps(eng, reads, writes)
        self._emit_waits(eng, waits)
        ins = fn(self.eng[eng])
        self.seq[eng] += 1
        ins.then_inc(self.sems["c" + eng], 1)
        tok = ("c" + eng, self.seq[eng], eng)
        self._mark(tok, reads, writes)
        return tok

    def dma(self, eng, fn, reads=(), writes=(), grp="d"):
        n = N_DSEM if grp == "d" else N_WSEM
        s = self.rr[grp]
        self.rr[grp] = (s + 1) % n
        key = "%s%d" % (grp, s)
        waits = self._deps(eng, reads, writes)
        if self.dtot[key] > 0:
            self._need(eng, (key, self.dtot[key], None), waits)
        self._emit_waits(eng, waits)
        ins = fn(self.eng[eng])
        self.dtot[key] += 16
        ins.then_inc(self.sems[key], 16)
        tok = (key, self.dtot[key], None)
        self._mark(tok, reads, writes)
        return tok

    def barrier(self, engs=("pe", "act", "dve", "sp")):
        for e in engs:
            waits = {}
            for x in engs:
                if x != e and self.seq[x] > 0:
                    self._need(e, ("c" + x, self.seq[x], x), waits)
            for i in range(N_DSEM):
                k = "d%d" % i
                if self.dtot[k] > 0:
                    self._need(e, (k, self.dtot[k], None), waits)
            self._emit_waits(e, waits)

    def wait_bufs(self, eng, bufs):
        waits = {}
        for b in bufs:
            self._need(eng, b.w, waits)
        self._emit_waits(eng, waits)


W_IN_CHUNKS = []


def _mk_chunks():
    def add(name, c0, n):
        i = 0
        while n > 0:
            m = min(128, n)
            W_IN_CHUNKS.append((name + str(i), c0, m))
            c0 += m
            n -= m
            i += 1
    add("aq", 0, 512)
    add("ak", 512, 512)
    add("av", 1024, 512)
    add("bq", 1536, 256)
    add("bk", 1792, 256)
    add("bv", 2048, 512)
    add("bg", 2560, 16)
    add("br", 2576, 512)
    add("cq", 3088, 384)
    add("ckv", 3472, 128)
    add("ckr", 3600, 64)
    add("dz", 3664, 512)
    add("dx", 4176, 1024)
    add("dt", 5200, 8)


_mk_chunks()
CH_IDX = {c[0]: i for i, c in enumerate(W_IN_CHUNKS)}
NCH = len(W_IN_CHUNKS)
W_IN_GROUPS = [["aq0", "aq1", "aq2", "aq3"], ["ak0", "ak1", "ak2", "ak3"], ["av0", "av1", "av2", "av3"],
               ["bq0", "bq1", "bk0", "bk1"], ["bv0", "bv1", "bv2", "bv3"], ["bg0", "br0", "br1", "br2"],
               ["br3", "cq0", "cq1", "cq2"], ["ckv0", "ckr0", "dz0", "dz1"], ["dz2", "dz3", "dx0", "dx1"],
               ["dx2", "dx3", "dx4", "dx5"], ["dx6", "dx7", "dt0"]]


def pj_row(name):
    return CH_IDX[name] * 128


class Ctx:
    pass


_UID = [0]


def uname(name):
    _UID[0] += 1
    return "%s_u%d" % (name, _UID[0])


def build_program(S, NSEQ, depth, dbg=None, phases=("ffn1", "mix", "ffn2"), mix_sub=("proj", "diff", "gla", "mla", "ssd", "merge")):
    T = S * NSEQ
    NB = T // TB
    nc = bass.Bass("TRN2", target_bir_lowering=False)
    P = Prog(nc)
    C = Ctx()
    C.nc, C.P, C.S, C.NSEQ, C.T, C.NB = nc, P, S, NSEQ, T, NB
    C.mix_sub = mix_sub
    C.dump = (dbg == 'dump')

    def din(name, shape, dt=F32):
        return nc.dram_tensor(name, list(shape), dt, kind="ExternalInput").ap()

    def dscr(name, shape, dt=F32):
        return nc.dram_tensor(name, list(shape), dt, kind="Internal").ap()

    x_in = din("x", [T, D])
    pos_in = din("positions", [NSEQ, S], I32)
    w_gu = [din("ffn1_w_gu", [depth, D, 2 * DFF]), din("ffn2_w_gu", [depth, D, 2 * DFF])]
    w_dn = [din("ffn1_w_down", [depth, DFF, D]), din("ffn2_w_down", [depth, DFF, D])]
    w_in = din("w_in", [depth, D, NIN])
    w_br = din("w_branch", [depth, 4 * 512, D])
    w_out = din("w_out", [depth, D, D])
    w_uq = din("mla_w_uq", [depth, 384, 768])
    w_ukv = din("mla_w_ukv", [depth, 128, 1024])
    w_g2 = din("gla_w_gate2", [depth, 16, 256])
    NPV = depth * PV_PER_LAYER
    pvec_in = din("pvec", [128, NPV])
    cst_in = din("cst", [128, CST_N])
    y_out = nc.dram_tensor("y", [T, D], F32, kind="ExternalOutput").ap()

    HT = dscr("HT", [D, T])
    C.HT = HT
    if dbg:
        C.PJ = nc.dram_tensor("PJ", [NCH * 128, T], F32, kind="ExternalOutput").ap()
        C.OM = nc.dram_tensor("OM", [D, T], BF16, kind="ExternalOutput").ap()
    else:
        C.PJ = dscr("PJ", [NCH * 128, T])
        C.OM = dscr("OM", [D, T], BF16)
    C.ROPE = dscr("ROPE", [NSEQ, 4, 128, S])
    C.pjb = [[Buf() for _ in range(NB)] for _ in range(NCH)]
    C.omb = [[Buf() for _ in range(NB)] for _ in range(16)]
    C.ropeb = [Buf() for _ in range(NSEQ)]
    hbuf = [Buf("HT%d" % i) for i in range(NB)]
    C.hbuf = hbuf
    wb_gu = [[dscr("wgu%d_%d" % (i, l), [D, 2 * DFF], BF16) for l in range(depth)] for i in range(2)]
    wb_dn = [[dscr("wdn%d_%d" % (i, l), [DFF, D], BF16) for l in range(depth)] for i in range(2)]
    wb_in = [dscr("win_%d" % l, [D, NIN], BF16) for l in range(depth)]
    wb_br = [dscr("wbr_%d" % l, [4 * 512, D], BF16) for l in range(depth)]
    wb_out = [dscr("wout_%d" % l, [D, D], BF16) for l in range(depth)]
    wb_uq = [dscr("wuq_%d" % l, [384, 768], BF16) for l in range(depth)]
    wb_ukv = [dscr("wukv_%d" % l, [128, 1024], BF16) for l in range(depth)]
    wbuf = {}

    class WB:
        def __init__(self, ncols):
            self.nb = (ncols + 2047) // 2048
            self.b = [Buf() for _ in range(self.nb)]

        def bufs(self, c0, ncols):
            return self.b[c0 // 2048:(c0 + ncols - 1) // 2048 + 1]

    def prep(dst, src, rows, cols, key, order=None):
        wb = wbuf.setdefault(key, WB(cols))
        cbs = order if order is not None else list(range(wb.nb))
        for cb in cbs:
            c0 = cb * 2048
            c1 = min(cols, c0 + 2048)
            for r0 in range(0, rows, 2048):
                r1 = min(rows, r0 + 2048)
                P.dma("pool", lambda e, r0=r0, r1=r1, c0=c0, c1=c1: e.dma_start(out=dst[r0:r1, c0:c1], in_=src[r0:r1, c0:c1]),
                      writes=[wb.b[cb]], grp="w")
        return wb

    def prep_layer(l, which):
        if which == "ffn1":
            prep(wb_gu[0][l], w_gu[0][l], D, 2 * DFF, "gu0_%d" % l, order=[0, 2, 3, 1, 4, 5])
            prep(wb_dn[0][l], w_dn[0][l], DFF, D, "dn0_%d" % l)
        elif which == "mix":
            prep(wb_in[l], w_in[l], D, NIN, "in_%d" % l)
            prep(wb_br[l], w_br[l], 2048, D, "br_%d" % l)
            prep(wb_out[l], w_out[l], D, D, "out_%d" % l)
            prep(wb_uq[l], w_uq[l], 384, 768, "uq_%d" % l)
            prep(wb_ukv[l], w_ukv[l], 128, 1024, "ukv_%d" % l)
        else:
            prep(wb_gu[1][l], w_gu[1][l], D, 2 * DFF, "gu1_%d" % l, order=[0, 2, 3, 1, 4, 5])
            prep(wb_dn[1][l], w_dn[1][l], DFF, D, "dn1_%d" % l)

    with contextlib.ExitStack() as root:
        def sbp(name, shape, dt):
            return root.enter_context(nc.sbuf_tensor(uname(name), list(shape), dt))

        cst = sbp("cst_sb", [128, CST_N], F32)
        pvec = sbp("pvec_sb", [128, NPV], F32)
        cstb = sbp("cstb", [128, CSTB_N], BF16)
        ps = root.enter_context(nc.psum_tensor("ps", [128, 8, 512], F32))
        C.ps = ps
        C.pb = [Buf("bank%d" % i) for i in range(8)]
        b_cst, b_pvec, b_cstb = Buf("cst"), Buf("pvec"), Buf("cstb")
        C.cst, C.pvec, C.cstb, C.b_cst, C.b_pvec, C.b_cstb = cst, pvec, cstb, b_cst, b_pvec, b_cstb
        P.dma("sp", lambda e: e.dma_start(out=cst[:], in_=cst_in[:]), writes=[b_cst])
        P.dma("sp", lambda e: e.dma_start(out=pvec[:], in_=pvec_in[:]), writes=[b_pvec])
        P.op("dve", lambda e: e.tensor_copy(out=cstb[:], in_=cst[:, 0:CSTB_N]), reads=[b_cst], writes=[b_cstb])
        C.ident = cst[:, CO["ident"]:CO["ident"] + 128]
        C.ones = cst[:, CO["ones"]:CO["ones"] + 128]
        C.identb = cstb[:, CO["ident"]:CO["ident"] + 128]
        C.onesb = cstb[:, CO["ones"]:CO["ones"] + 128]

        C.aux = sbp("aux_sb", [128, depth * AUX_N], F32)
        C.b_aux = Buf("aux")
        order = [(l, ph) for l in range(depth) for ph in ("ffn1", "mix", "ffn2") if ph in phases]
        for (l, ph) in order[:2]:
            prep_layer(l, ph)
        nxt = 2

        if "mix" in phases:
            setup_phase(C, pos_in, depth)
        prologue(C, x_in)
        for oi, (l, ph) in enumerate(order):
            if ph == "ffn1":
                ffn_phase(C, wb_gu[0][l], wb_dn[0][l], wbuf["gu0_%d" % l], wbuf["dn0_%d" % l], l, 0)
            elif ph == "ffn2":
                ffn_phase(C, wb_gu[1][l], wb_dn[1][l], wbuf["gu1_%d" % l], wbuf["dn1_%d" % l], l, 2)
            else:
                sub = C.mix_sub
                if "proj" in sub:
                    proj_phase(C, l, wb_in[l], wbuf["in_%d" % l])
                if "diff" in sub:
                    diff_phase(C, l)
                if "gla" in sub:
                    gla_phase(C, l, w_g2[l])
                if "mla" in sub:
                    mla_phase(C, l, wb_uq[l], wbuf["uq_%d" % l], wb_ukv[l], wbuf["ukv_%d" % l])
                if "ssd" in sub:
                    ssd_phase(C, l)
                if "merge" in sub:
                    merge_phase(C, l, wb_in[l], wbuf["in_%d" % l], wb_br[l], wbuf["br_%d" % l], wb_out[l], wbuf["out_%d" % l])
            if nxt < len(order):
                prep_layer(*order[nxt])
                nxt += 1
        epilogue(C, y_out)
    return nc


def _build_consts():
    tabs = []
    off = {}

    def add(name, arr):
        arr = np.asarray(arr, np.float32)
        assert arr.shape[0] == 128
        off[name] = sum(t.shape[1] for t in tabs)
        tabs.append(arr)
    p = np.arange(128)[:, None]
    f = np.arange(512)[None, :]
    i128 = np.arange(128)[None, :]
    add("ident", np.eye(128))
    add("ones", np.ones((128, 128)))
    add("maskA", np.concatenate([(f - j * 128 - p >= 0) for j in range(4)], axis=1))
    add("maskG", ((p // 64) == (i128 // 64)) & (p <= i128))
    add("maskS", (p <= i128))
    pd = np.zeros((128, 128))
    for m in range(128):
        d = m % 64
        if d < 16:
            k = m + 8 if d < 8 else m - 8
            pd[k, m] = 1
    add("permD", pd)
    pm = np.zeros((128, 128))
    for m in range(64):
        k = m + 32 if m < 32 else m - 32
        pm[k, m] = 1
    add("permM", pm)
    nb = sum(t.shape[1] for t in tabs)
    invd = np.zeros((128, 1), np.float32)
    sgnd = np.zeros((128, 1), np.float32)
    fr_d = (np.float32(ROPE_THETA) ** (-np.arange(0, 16, 2, dtype=np.float32) / np.float32(16))).astype(np.float32)
    for m in range(128):
        d = m % 64
        if d < 16:
            invd[m, 0] = fr_d[d % 8]
            sgnd[m, 0] = -1.0 if d < 8 else 1.0
    invm = np.zeros((128, 1), np.float32)
    sgnm = np.zeros((128, 1), np.float32)
    fr_m = (np.float32(ROPE_THETA) ** (-np.arange(0, 64, 2, dtype=np.float32) / np.float32(64))).astype(np.float32)
    for m in range(64):
        invm[m, 0] = fr_m[m % 32]
        sgnm[m, 0] = -1.0 if m < 32 else 1.0
    add("invd", invd)
    add("sgnd", sgnd)
    add("invm", invm)
    add("sgnm", sgnm)
    selE = np.zeros((128, 4, 128))
    for pr in range(4):
        for m in range(128):
            selE[2 * pr + m // 64, pr, m] = 1
    add("selE", selE.reshape(128, 512))
    selH = np.zeros((128, 8, 128))
    for h in range(8):
        selH[h, h, :] = 1
    add("selH", selH.reshape(128, 1024))
    return np.concatenate(tabs, axis=1), off, nb


CST, CO, CSTB_N = _build_consts()
CST_N = CST.shape[1]

PV = {}
_o = 0
for _n, _w in [("ln_g", 48), ("ln_b", 48), ("b_gate", 64), ("subln_g", 1), ("gla_bg", 2), ("gla_ng", 1), ("mla_qg", 3),
               ("mla_kvg", 1), ("conv_w", 32), ("conv_b", 8), ("dt_bias", 1), ("a_log", 1), ("ssd_d", 4), ("ssd_ng", 4),
               ("lam", 256)]:
    PV[_n] = _o
    _o += _w
PV_PER_LAYER = _o


def pv_col(l, name, i=0):
    return l * PV_PER_LAYER + PV[name] + i


def prologue(C, x_in):
    nc, P, ps, pb = C.nc, C.P, C.ps, C.pb
    with contextlib.ExitStack() as st:
        xt = [st.enter_context(nc.sbuf_tensor("pro_x%d" % i, [128, D], F32)) for i in range(2)]
        bx = [Buf(), Buf()]
        stg = st.enter_context(nc.sbuf_tensor("pro_stage", [128, KC, TB], F32))
        bstg = Buf()
        n = 0
        for blk in range(C.NB):
            for sub in range(TB // 128):
                t0 = blk * TB + sub * 128
                s = n % 2
                P.dma("sp", lambda e, s=s, t0=t0: e.dma_start(out=xt[s][:], in_=x_in[t0:t0 + 128, :]), writes=[bx[s]])
                for g in range(4):
                    bank = (n * 4 + g) % 8

                    def tr(e, s=s, g=g, bank=bank):
                        for j in range(4):
                            k = g * 4 + j
                            ins = e.transpose(out=ps[:, bank, j * 128:(j + 1) * 128], in_=xt[s][:, k * 128:(k + 1) * 128],
                                              identity=C.ident)
                        return ins
                    P.op("pe", tr, reads=[bx[s], C.b_cst], writes=[pb[bank]])
                    eng = "dve" if g % 2 == 0 else "act"
                    src = ps[:, bank, :].rearrange("p (j t) -> p j t", j=4)
                    dst = stg[:, g * 4:(g + 1) * 4, sub * 128:(sub + 1) * 128]
                    if eng == "dve":
                        P.op("dve", lambda e, src=src, dst=dst: e.tensor_copy(out=dst, in_=src), reads=[pb[bank]], writes=[bstg])
                    else:
                        P.op("act", lambda e, src=src, dst=dst: e.copy(out=dst, in_=src), reads=[pb[bank]], writes=[bstg])
                n += 1
            P.dma("sp", lambda e, blk=blk: e.dma_start(
                out=C.HT[:, blk * TB:(blk + 1) * TB].rearrange("(k p) t -> p k t", p=128), in_=stg[:]),
                reads=[bstg], writes=[C.hbuf[blk]])
        P.barrier()


def epilogue(C, y_out):
    nc, P, ps, pb = C.nc, C.P, C.ps, C.pb
    by = Buf("y")
    with contextlib.ExitStack() as st:
        hf = st.enter_context(nc.sbuf_tensor("epi_h", [128, KC, TB], F32))
        bh = Buf()
        yt = [st.enter_context(nc.sbuf_tensor("epi_y%d" % i, [128, D], F32)) for i in range(2)]
        byt = [Buf(), Buf()]
        n = 0
        for blk in range(C.NB):
            P.dma("sp", lambda e, blk=blk: e.dma_start(
                out=hf[:], in_=C.HT[:, blk * TB:(blk + 1) * TB].rearrange("(k p) t -> p k t", p=128)),
                reads=[C.hbuf[blk]], writes=[bh])
            for sub in range(TB // 128):
                t0 = blk * TB + sub * 128
                s = n % 2
                for g in range(4):
                    bank = (n * 4 + g) % 8

                    def tr(e, g=g, bank=bank, sub=sub):
                        for j in range(4):
                            k = g * 4 + j
                            ins = e.transpose(out=ps[:, bank, j * 128:(j + 1) * 128], in_=hf[:, k, sub * 128:(sub + 1) * 128],
                                              identity=C.ident)
                        return ins
                    P.op("pe", tr, reads=[bh, C.b_cst], writes=[pb[bank]])
                    dst = yt[s][:, g * 512:(g + 1) * 512]
                    src = ps[:, bank, :]
                    if g % 2 == 0:
                        P.op("dve", lambda e, src=src, dst=dst: e.tensor_copy(out=dst, in_=src), reads=[pb[bank]], writes=[byt[s]])
                    else:
                        P.op("act", lambda e, src=src, dst=dst: e.copy(out=dst, in_=src), reads=[pb[bank]], writes=[byt[s]])
                P.dma("sp", lambda e, s=s, t0=t0: e.dma_start(out=y_out[t0:t0 + 128, :], in_=yt[s][:]), reads=[byt[s]], writes=[by])
                n += 1
        P.wait_bufs("sp", [by])
        P.barrier()


def load_w(C, slot, bslot, w_bf, bw, r0, nk, c0, ncols):
    rd = bw.bufs(c0, ncols) if hasattr(bw, "bufs") else [bw]
    C.P.dma("sp", lambda e: e.dma_start(out=slot[:, 0:nk, 0:ncols],
                                        in_=w_bf[r0:r0 + nk * 128, c0:c0 + ncols].rearrange("(k p) c -> p k c", p=128)),
            reads=rd, writes=[bslot])


class WStream:
    def __init__(self, C, slots, bsl):
        self.C, self.slots, self.bsl = C, slots, bsl
        self.items = []

    def add(self, load_args, fn):
        self.items.append((load_args, fn))

    def run(self):
        n, nsl = len(self.items), len(self.slots)
        la = nsl - 1
        issued = 0
        for i in range(n):
            while issued < min(n, i + la + 1):
                sl = issued % nsl
                load_w(self.C, self.slots[sl], self.bsl[sl], *self.items[issued][0])
                issued += 1
            self.items[i][1](i % nsl)


def prefetch_hb(C, blk, hb, bhb, stage, bstage, cnt):
    P = C.P
    for h in range(8):
        sg = cnt[0] % 2
        cnt[0] += 1
        P.dma("sp", lambda e, h=h, sg=sg: e.dma_start(
            out=stage[sg][:], in_=C.HT[h * 256:(h + 1) * 256, blk * TB:(blk + 1) * TB].rearrange("(k p) t -> p k t", p=128)),
            reads=[C.hbuf[blk]], writes=[bstage[sg]])
        if h % 2 == 0:
            P.op("dve", lambda e, h=h, sg=sg: e.tensor_copy(out=hb[:, h * 2:(h + 1) * 2, :], in_=stage[sg][:]), reads=[bstage[sg]], writes=[bhb])
        else:
            P.op("act", lambda e, h=h, sg=sg: e.copy(out=hb[:, h * 2:(h + 1) * 2, :], in_=stage[sg][:]), reads=[bstage[sg]], writes=[bhb])


def ln_block(C, r, br, l, idx, blk, tmp):
    nc, P, ps, pb = C.nc, C.P, C.ps, C.pb
    sq, bsq, mean, msq, rstd, bst = tmp

    def msum(e):
        for c in range(KC):
            ins = e.matmul(ps[:, 6, :], lhsT=C.ones, rhs=r[:, c, :], start=(c == 0), stop=(c == KC - 1))
        return ins
    P.op("pe", msum, reads=[br, C.b_cst], writes=[pb[6]])
    for c in range(KC):
        s = c % 2
        P.op("act", lambda e, c=c, s=s: e.activation(out=sq[s][:], in_=r[:, c, :], func=AF.Square), reads=[br], writes=[bsq[s]])
        P.op("pe", lambda e, c=c, s=s: e.matmul(ps[:, 7, :], lhsT=C.onesb, rhs=sq[s][:], start=(c == 0), stop=(c == KC - 1)),
             reads=[bsq[s], C.b_cstb], writes=[pb[7]])
    P.op("dve", lambda e: e.tensor_scalar(out=mean[:], in0=ps[:, 6, :], scalar1=1.0 / D, scalar2=None, op0=ALU.mult),
         reads=[pb[6]], writes=[bst])
    P.op("dve", lambda e: e.tensor_tensor(out=msq[:], in0=mean[:], in1=mean[:], op=ALU.mult), reads=[bst], writes=[bst])
    P.op("dve", lambda e: e.scalar_tensor_tensor(out=msq[:], in0=ps[:, 7, :], scalar=1.0 / D, in1=msq[:], op0=ALU.mult, op1=ALU.subtract),
         reads=[pb[7], bst], writes=[bst])
    P.op("dve", lambda e: e.tensor_scalar(out=msq[:], in0=msq[:], scalar1=EPS, scalar2=None, op0=ALU.add), reads=[bst], writes=[bst])
    P.op("act", lambda e: e.activation(out=msq[:], in_=msq[:], func=AF.Sqrt), reads=[bst], writes=[bst])
    P.op("dve", lambda e: e.reciprocal(out=rstd[:], in_=msq[:]), reads=[bst], writes=[bst])
    for c in range(KC):
        P.op("dve", lambda e, c=c: e.tensor_tensor(out=r[:, c, :], in0=r[:, c, :], in1=mean[:], op=ALU.subtract), reads=[br, bst], writes=[br])
        P.op("dve", lambda e, c=c: e.tensor_tensor(out=r[:, c, :], in0=r[:, c, :], in1=rstd[:], op=ALU.mult), reads=[br, bst], writes=[br])
        gcol = pv_col(l, "ln_g", idx * 16 + c)
        bcol = pv_col(l, "ln_b", idx * 16 + c)
        P.op("act", lambda e, c=c, gcol=gcol, bcol=bcol: e.activation(
            out=r[:, c, :], in_=r[:, c, :], func=AF.Identity, scale=C.pvec[:, gcol:gcol + 1], bias=C.pvec[:, bcol:bcol + 1]),
            reads=[br, C.b_pvec], writes=[br])
    P.dma("sp", lambda e: e.dma_start(out=C.HT[:, blk * TB:(blk + 1) * TB].rearrange("(k p) t -> p k t", p=128), in_=r[:]),
          reads=[br], writes=[C.hbuf[blk]])


def ffn_phase(C, wgu, wdn, bgu, bdn, l, idx):
    nc, P, ps, pb = C.nc, C.P, C.ps, C.pb
    with contextlib.ExitStack() as st:
        def sb(name, shape, dt):
            return st.enter_context(nc.sbuf_tensor(uname(name), list(shape), dt))
        hf = sb("f_hf", [128, KC, TB], F32)
        hb = sb("f_hb", [128, KC, TB], BF16)
        stage = [sb("f_stg%d" % i, [128, 2, TB], F32) for i in range(2)]
        act = sb("f_act", [128, FCH, TB], BF16)
        sg = sb("f_sg", [128, 4, TB], BF16)
        NSL = 3
        slots = [sb("f_w%d" % i, [128, 16, 512], BF16) for i in range(NSL)]
        bsl = [Buf() for _ in range(NSL)]
        sq = [sb("f_sq%d" % i, [128, TB], BF16) for i in range(2)]
        bsq = [Buf(), Buf()]
        mean = sb("f_mean", [128, TB], F32)
        msq = sb("f_msq", [128, TB], F32)
        rstd = sb("f_rstd", [128, TB], F32)
        bhf, bst, bhb = Buf(), Buf(), Buf()
        bstage = [Buf(), Buf()]
        bsg = [Buf() for _ in range(4)]
        bactc = [Buf() for _ in range(FCH)]
        cnt = [0]
        W = WStream(C, slots, bsl)
        prefetch_hb(C, 0, hb, bhb, stage, bstage, cnt)
        for blk in range(C.NB):
            for g in range(FCH // 4):
                for part in range(2):
                    def step(s, g=g, part=part):
                        banks = [part * 4 + j for j in range(4)]

                        def mm(e):
                            for k in range(KC):
                                for j in range(4):
                                    ins = e.matmul(ps[:, banks[j], :], lhsT=slots[s][:, k, j * 128:(j + 1) * 128], rhs=hb[:, k, :],
                                                   start=(k == 0), stop=(k == KC - 1))
                            return ins
                        P.op("pe", mm, reads=[bsl[s], bhb], writes=[pb[b] for b in banks])
                        if part == 1:
                            for j in range(4):
                                P.op("act", lambda e, j=j: e.activation(out=sg[:, j, :], in_=ps[:, j, :], func=AF.Silu), reads=[pb[j]], writes=[bsg[j]])
                                P.op("dve", lambda e, j=j: e.scalar_tensor_tensor(
                                    out=act[:, g * 4 + j, :], in0=ps[:, 4 + j, :], scalar=0.5, in1=sg[:, j, :], op0=ALU.mult, op1=ALU.mult),
                                    reads=[pb[4 + j], bsg[j]], writes=[bactc[g * 4 + j]])
                    W.add((wgu, bgu, 0, 16, part * DFF + g * 512, 512), step)
            for cg in range(4):
                for q in range(4):
                    def step(s, cg=cg, q=q, blk=blk):
                        banks = [(cg % 2) * 4 + j for j in range(4)]
                        if cg == 0 and q == 0:
                            P.dma("sp", lambda e: e.dma_start(
                                out=hf[:], in_=C.HT[:, blk * TB:(blk + 1) * TB].rearrange("(k p) t -> p k t", p=128)),
                                reads=[C.hbuf[blk]], writes=[bhf])

                        def mm(e):
                            for kk in range(11):
                                k = q * 11 + kk
                                for j in range(4):
                                    ins = e.matmul(ps[:, banks[j], :], lhsT=slots[s][:, kk, j * 128:(j + 1) * 128], rhs=act[:, k, :],
                                                   start=(k == 0), stop=(k == FCH - 1))
                            return ins
                        P.op("pe", mm, reads=[bsl[s]] + bactc[q * 11:(q + 1) * 11], writes=[pb[b] for b in banks])
                        if cg == 0 and q == 1 and blk + 1 < C.NB:
                            prefetch_hb(C, blk + 1, hb, bhb, stage, bstage, cnt)
                        if q == 3:
                            for j in range(4):
                                c = cg * 4 + j
                                P.op("dve", lambda e, c=c, b=banks[j]: e.scalar_tensor_tensor(
                                    out=hf[:, c, :], in0=hf[:, c, :], scalar=ALPHA, in1=ps[:, b, :], op0=ALU.mult, op1=ALU.add),
                                    reads=[pb[banks[j]], bhf], writes=[bhf])
                            if cg == 3:
                                ln_block(C, hf, bhf, l, idx, blk, (sq, bsq, mean, msq, rstd, bst))
                    W.add((wdn, bdn, q * 11 * 128, 11, cg * 512, 512), step)
        W.run()
        P.barrier()


def _pvec_host(inputs, depth):
    pv = np.zeros((128, depth * PV_PER_LAYER), np.float32)

    def put(l, name, arr):
        arr = np.asarray(arr, np.float32)
        c = pv_col(l, name)
        pv[:arr.shape[0], c:c + arr.shape[1]] = arr
    for l in range(depth):
        put(l, "ln_g", inputs["ln_g"][l].reshape(3, 16, 128).transpose(2, 0, 1).reshape(128, 48))
        put(l, "ln_b", inputs["ln_b"][l].reshape(3, 16, 128).transpose(2, 0, 1).reshape(128, 48))
        put(l, "b_gate", inputs["b_gate"][l].reshape(64, 128).T)
        put(l, "subln_g", inputs["diff_subln_g"][l].reshape(128, 1))
        put(l, "gla_bg", inputs["gla_b_gate"][l].reshape(2, 128).T)
        put(l, "gla_ng", inputs["gla_norm_g"][l].reshape(128, 1))
        put(l, "mla_qg", inputs["mla_q_norm_g"][l].reshape(3, 128).T)
        put(l, "mla_kvg", inputs["mla_kv_norm_g"][l].reshape(128, 1))
        put(l, "conv_w", inputs["ssd_conv_w"][l].reshape(4, 8, 128).transpose(2, 0, 1).reshape(128, 32))
        put(l, "conv_b", inputs["ssd_conv_b"][l].reshape(8, 128).T)
        put(l, "dt_bias", inputs["ssd_dt_bias"][l].reshape(8, 1))
        put(l, "a_log", inputs["ssd_a_log"][l].reshape(8, 1))
        put(l, "ssd_d", np.repeat(inputs["ssd_d"][l], 64).reshape(4, 128).T)
        put(l, "ssd_ng", inputs["ssd_norm_g"][l].reshape(4, 128).T)
        put(l, "lam", np.broadcast_to(inputs["diff_lambda"][l].reshape(1, 256), (128, 256)))
    return pv


_NC_CACHE = {}


def run(inputs, S, NSEQ, depth, n_cores, phases=("ffn1", "mix", "ffn2"), trace=False, dbg=None,
        mix_sub=("proj", "diff", "gla", "mla", "ssd", "merge")):
    key = (S, NSEQ, depth, tuple(phases), dbg, tuple(mix_sub))
    if key not in _NC_CACHE:
        _NC_CACHE[key] = build_program(S, NSEQ, depth, phases=phases, dbg=dbg, mix_sub=mix_sub)
    nc = _NC_CACHE[key]
    pv = _pvec_host(inputs, depth)
    x = np.ascontiguousarray(inputs["x"], dtype=np.float32)
    pos = np.ascontiguousarray(inputs["positions"], dtype=np.int32)
    shared = {
        "ffn1_w_gu": inputs["ffn1_w_gu"], "ffn2_w_gu": inputs["ffn2_w_gu"],
        "ffn1_w_down": inputs["ffn1_w_down"], "ffn2_w_down": inputs["ffn2_w_down"],
        "w_in": inputs["w_in"], "w_branch": np.asarray(inputs["w_branch"]).reshape(depth, 2048, D), "w_out": inputs["w_out"],
        "mla_w_uq": inputs["mla_w_uq"], "mla_w_ukv": inputs["mla_w_ukv"], "gla_w_gate2": inputs["gla_w_gate2"],
        "pvec": pv, "cst": CST,
    }
    shared = {k: np.ascontiguousarray(v, dtype=np.float32) for k, v in shared.items()}
    in_maps = []
    for c in range(n_cores):
        m = dict(shared)
        m["x"] = x[c * NSEQ:(c + 1) * NSEQ].reshape(NSEQ * S, D)
        m["positions"] = pos[c * NSEQ:(c + 1) * NSEQ]
        in_maps.append(m)
    res = run_bass_kernel_spmd(nc, in_maps, core_ids=list(range(n_cores)), trace=trace)
    out = np.concatenate([r["y"].reshape(NSEQ, S, D) for r in res.results], axis=0)
    return out, res


def kernel(**inputs):
    out, _ = run(inputs, 2048, 2, DEPTH, 8)
    return out.astype(np.float32)


import os
GLA_STOP = int(os.environ.get('GLA_STOP', '9'))
AUX_N = 4
TWO_PI = 2.0 * math.pi
CW1 = 6.28125
CW2 = TWO_PI - CW1


def setup_phase(C, pos_in, depth):
    nc, P = C.nc, C.P
    S = C.S
    aux, baux = C.aux, C.b_aux
    pv = C.pvec
    with contextlib.ExitStack() as st:
        def sb(name, shape, dt):
            return st.enter_context(nc.sbuf_tensor(uname(name), list(shape), dt))
        t64 = sb("su_t64", [128, 64], F32)
        s1 = sb("su_s1", [128, 2], F32)
        bt = Buf()
        for l in range(depth):
            lc = pv_col(l, "lam")
            lam_init = 0.8 - 0.6 * math.exp(-0.3 * l)
            for i in range(2):
                P.op("dve", lambda e, i=i, lc=lc: e.tensor_tensor(out=t64[:], in0=pv[:, lc + i * 128:lc + i * 128 + 64],
                                                               in1=pv[:, lc + i * 128 + 64:lc + i * 128 + 128], op=ALU.mult),
                     reads=[C.b_pvec], writes=[bt])
                P.op("dve", lambda e, i=i: e.tensor_reduce(out=s1[:, i:i + 1], in_=t64[:], axis=mybir.AxisListType.X, op=ALU.add),
                     reads=[bt], writes=[bt])
            P.op("act", lambda e: e.activation(out=s1[:], in_=s1[:], func=AF.Exp), reads=[bt], writes=[bt])
            a0 = l * AUX_N
            P.op("dve", lambda e, a0=a0: e.tensor_tensor(out=aux[:, a0:a0 + 1], in0=s1[:, 1:2], in1=s1[:, 0:1], op=ALU.subtract),
                 reads=[bt], writes=[baux])
            P.op("dve", lambda e, a0=a0, li=lam_init: e.tensor_scalar(out=aux[:, a0:a0 + 1], in0=aux[:, a0:a0 + 1], scalar1=-li, scalar2=None, op0=ALU.add),
                 reads=[baux], writes=[baux])
            gc = pv_col(l, "gla_bg")
            P.op("dve", lambda e, a0=a0, gc=gc: e.tensor_scalar(out=aux[:, a0 + 1:a0 + 3], in0=pv[:, gc:gc + 2], scalar1=-1.0, scalar2=None, op0=ALU.mult),
                 reads=[C.b_pvec], writes=[baux])
            ac = pv_col(l, "a_log")
            P.op("act", lambda e, a0=a0, ac=ac: e.activation(out=aux[:, a0 + 3:a0 + 4], in_=pv[:, ac:ac + 1], func=AF.Exp),
                 reads=[C.b_pvec, baux], writes=[baux])
            P.op("dve", lambda e, a0=a0: e.tensor_scalar(out=aux[:, a0 + 3:a0 + 4], in0=aux[:, a0 + 3:a0 + 4], scalar1=-1.0, scalar2=None, op0=ALU.mult),
                 reads=[baux], writes=[baux])
        pi = sb("su_pi", [128, 1, S], I32)
        pf = sb("su_pf", [128, S], F32)
        ang = sb("su_ang", [128, S], F32)
        a = sb("su_a", [128, S], F32)
        ni = sb("su_ni", [128, S], I32)
        nf = sb("su_nf", [128, S], F32)
        m = sb("su_m", [128, S], F32)
        bw = Buf()
        for s in range(C.NSEQ):
            P.dma("sp", lambda e, s=s: e.dma_start(out=pi[:], in_=pos_in[s:s + 1, :].partition_broadcast(128)), writes=[bw])
            P.op("dve", lambda e: e.tensor_copy(out=pf[:], in_=pi[:, 0, :]), reads=[bw], writes=[bw])
            for ti, (inv, sgn) in enumerate((("invd", "sgnd"), ("invm", "sgnm"))):
                P.op("dve", lambda e, inv=inv: e.tensor_scalar(out=ang[:], in0=pf[:], scalar1=C.cst[:, CO[inv]:CO[inv] + 1], scalar2=None, op0=ALU.mult),
                     reads=[bw, C.b_cst], writes=[bw])
                for ki in range(2):
                    V = lambda fn: P.op("dve", fn, reads=[bw], writes=[bw])
                    V(lambda e, ki=ki: e.tensor_scalar(out=a[:], in0=ang[:], scalar1=(math.pi / 2 if ki == 0 else 0.0), scalar2=None, op0=ALU.add))
                    V(lambda e: e.tensor_scalar(out=ni[:], in0=a[:], scalar1=1.0 / TWO_PI, scalar2=None, op0=ALU.mult))
                    V(lambda e: e.tensor_copy(out=nf[:], in_=ni[:]))
                    V(lambda e: e.scalar_tensor_tensor(out=a[:], in0=nf[:], scalar=-CW1, in1=a[:], op0=ALU.mult, op1=ALU.add))
                    V(lambda e: e.scalar_tensor_tensor(out=a[:], in0=nf[:], scalar=-CW2, in1=a[:], op0=ALU.mult, op1=ALU.add))
                    V(lambda e: e.tensor_scalar(out=m[:], in0=a[:], scalar1=math.pi, scalar2=TWO_PI, op0=ALU.is_gt, op1=ALU.mult))
                    V(lambda e: e.tensor_tensor(out=a[:], in0=a[:], in1=m[:], op=ALU.subtract))
                    V(lambda e: e.tensor_scalar(out=m[:], in0=a[:], scalar1=-math.pi, scalar2=TWO_PI, op0=ALU.is_lt, op1=ALU.mult))
                    V(lambda e: e.tensor_tensor(out=a[:], in0=a[:], in1=m[:], op=ALU.add))
                    V(lambda e: e.tensor_scalar(out=a[:], in0=a[:], scalar1=3.1415925, scalar2=-3.1415925, op0=ALU.min, op1=ALU.max))
                    P.op("act", lambda e: e.activation(out=a[:], in_=a[:], func=AF.Sin), reads=[bw], writes=[bw])
                    if ki == 1:
                        V(lambda e, sgn=sgn: e.tensor_scalar(out=a[:], in0=a[:], scalar1=C.cst[:, CO[sgn]:CO[sgn] + 1], scalar2=None, op0=ALU.mult))
                    P.dma("sp", lambda e, s=s, ti=ti, ki=ki: e.dma_start(out=C.ROPE[s, ti * 2 + ki], in_=a[:]), reads=[bw], writes=[C.ropeb[s]])
        P.barrier()


def seq_blocks(C, s):
    n = C.S // TB
    return list(range(s * n, (s + 1) * n))


def proj_phase(C, l, wbin, bwin):
    nc, P, ps, pb = C.nc, C.P, C.ps, C.pb
    with contextlib.ExitStack() as st:
        def sb(name, shape, dt):
            return st.enter_context(nc.sbuf_tensor(uname(name), list(shape), dt))
        hb = [sb("p_hb%d" % i, [128, KC, TB], BF16) for i in range(2)]
        stage = [sb("p_stg%d" % i, [128, 2, TB], F32) for i in range(2)]
        slots = [sb("p_w%d" % i, [128, 16, 512], BF16) for i in range(3)]
        bsl = [Buf() for _ in range(3)]
        stg = [sb("p_st%d" % i, [128, 4, TB], F32) for i in range(3)]
        bstg = [Buf() for _ in range(3)]
        bhb, bstage = [Buf(), Buf()], [Buf(), Buf()]
        cnt = [0]
        gcnt = [0]
        W = WStream(C, slots, bsl)
        prefetch_hb(C, 0, hb[0], bhb[0], stage, bstage, cnt)
        for blk in range(C.NB):
            cur = blk % 2
            for gi, grp in enumerate(W_IN_GROUPS):
                chs = [W_IN_CHUNKS[CH_IDX[n]] for n in grp]
                c0 = chs[0][1]
                ncols = chs[-1][1] + chs[-1][2] - c0

                def step(s, gi=gi, chs=chs, c0=c0, blk=blk, cur=cur):
                    if gi == 2 and blk + 1 < C.NB:
                        prefetch_hb(C, blk + 1, hb[1 - cur], bhb[1 - cur], stage, bstage, cnt)
                    banks = [(gi % 2) * 4 + j for j in range(len(chs))]

                    def mm(e):
                        for k in range(KC):
                            for j, ch in enumerate(chs):
                                o = ch[1] - c0
                                ins = e.matmul(ps[0:ch[2], banks[j], :], lhsT=slots[s][:, k, o:o + ch[2]], rhs=hb[cur][:, k, :],
                                               start=(k == 0), stop=(k == KC - 1))
                        return ins
                    P.op("pe", mm, reads=[bsl[s], bhb[cur]], writes=[pb[b] for b in banks])
                    sg = gcnt[0] % 3
                    gcnt[0] += 1
                    for j, ch in enumerate(chs):
                        w = ch[2]
                        if j % 2 == 0:
                            P.op("dve", lambda e, j=j, w=w, b=banks[j]: e.tensor_copy(out=stg[sg][0:w, j, :], in_=ps[0:w, b, :]),
                                 reads=[pb[banks[j]]], writes=[bstg[sg]])
                        else:
                            P.op("act", lambda e, j=j, w=w, b=banks[j]: e.copy(out=stg[sg][0:w, j, :], in_=ps[0:w, b, :]),
                                 reads=[pb[banks[j]]], writes=[bstg[sg]])
                    full = all(ch[2] == 128 for ch in chs) and len(chs) == 4
                    if full:
                        ci0 = CH_IDX[chs[0][0]]
                        P.dma("sp", lambda e: e.dma_start(
                            out=C.PJ[ci0 * 128:(ci0 + 4) * 128, blk * TB:(blk + 1) * TB].rearrange("(j p) t -> p j t", p=128), in_=stg[sg][:]),
                            reads=[bstg[sg]], writes=[C.pjb[ci0 + j][blk] for j in range(4)])
                    else:
                        for j, ch in enumerate(chs):
                            w = ch[2]
                            ci = CH_IDX[ch[0]]
                            P.dma("sp", lambda e, j=j, w=w, ci=ci: e.dma_start(
                                out=C.PJ[ci * 128:ci * 128 + w, blk * TB:(blk + 1) * TB], in_=stg[sg][0:w, j, :]),
                                reads=[bstg[sg]], writes=[C.pjb[ci][blk]])
                W.add((wbin, bwin, 0, 16, c0, ncols), step)
        W.run()
        P.barrier()


def pj_load(C, dst, bdst, name, s, rows=128, t0=0, nt=None):
    S = C.S
    nt = S if nt is None else nt
    ci = CH_IDX[name]
    g0 = s * S + t0
    blks = sorted(set(range(g0 // TB, (g0 + nt - 1) // TB + 1)))
    C.P.dma("sp", lambda e: e.dma_start(out=dst, in_=C.PJ[ci * 128:ci * 128 + rows, g0:g0 + nt]),
            reads=[C.pjb[ci][b] for b in blks], writes=[bdst])


def om_store(C, src, bsrc, row0, s, t0, nt, rows=128):
    g0 = s * C.S + t0
    blk = g0 // TB
    C.P.dma("sp", lambda e: e.dma_start(out=C.OM[row0:row0 + rows, g0:g0 + nt], in_=src),
            reads=[bsrc], writes=[C.omb[row0 // 128][blk]])


def attn_core(C, T_, qk_parts, v_tok, bv, scale, qb, rl, brl, obank, lbank, reads):
    P, ps, pb = C.P, C.ps, C.pb
    QB = min(512, C.S)
    nkc = (qb + 1) * QB // 128
    ndiag = QB // 128
    pT, bpT = T_["pT"], T_["bpT"]
    NPT = len(pT)
    LOOK = 2

    def front(kc):
        sb_ = kc % 3
        sl = kc % NPT

        def qk(e):
            for i, (qT, kT) in enumerate(qk_parts):
                ins = e.matmul(ps[:, sb_, 0:QB], lhsT=kT[:, kc * 128:(kc + 1) * 128], rhs=qT[:, qb * QB:(qb + 1) * QB],
                               start=(i == 0), stop=(i == len(qk_parts) - 1))
            return ins
        P.op("pe", qk, reads=reads, writes=[pb[sb_]])
        P.op("act", lambda e: e.activation(out=pT[sl][:, 0:QB], in_=ps[:, sb_, 0:QB], func=AF.Exp, scale=scale),
             reads=[pb[sb_]], writes=[bpT[sl]])
        dj = kc - (nkc - ndiag)
        if dj >= 0:
            mo = CO["maskA"] + dj * 512
            P.op("dve", lambda e: e.tensor_tensor(out=pT[sl][:, 0:QB], in0=pT[sl][:, 0:QB], in1=C.cstb[:, mo:mo + QB], op=ALU.mult),
                 reads=[bpT[sl], C.b_cstb], writes=[bpT[sl]])

    def back(kc):
        sl = kc % NPT

        def pv_(e):
            e.matmul(ps[:, obank, 0:QB], lhsT=v_tok[:, kc, :], rhs=pT[sl][:, 0:QB], start=(kc == 0), stop=(kc == nkc - 1))
            return e.matmul(ps[:, lbank, 0:QB], lhsT=C.onesb, rhs=pT[sl][:, 0:QB], start=(kc == 0), stop=(kc == nkc - 1))
        P.op("pe", pv_, reads=[bpT[sl], bv, C.b_cstb], writes=[pb[obank], pb[lbank]])

    for i in range(nkc + LOOK):
        if i < nkc:
            front(i)
        if i - LOOK >= 0:
            back(i - LOOK)
    P.op("dve", lambda e: e.reciprocal(out=rl[:, 0:QB], in_=ps[:, lbank, 0:QB]), reads=[pb[lbank]], writes=[brl])


def transpose_to_tok(C, srcT, bsrc, dst, bdst, S, bank0=2):
    P, ps, pb = C.P, C.ps, C.pb
    n = S // 128
    for g in range(0, n, 4):
        bank = bank0 + (g // 4) % 2
        cnt = min(4, n - g)
        def tr(e, g=g, bank=bank, cnt=cnt):
            for j in range(cnt):
                ins = e.transpose(out=ps[:, bank, j * 128:(j + 1) * 128], in_=srcT[:, (g + j) * 128:(g + j + 1) * 128], identity=C.ident)
            return ins
        P.op("pe", tr, reads=[bsrc, C.b_cst], writes=[pb[bank]])
        P.op("act", lambda e, g=g, bank=bank, cnt=cnt: e.copy(out=dst[:, g:g + cnt, :], in_=ps[:, bank, 0:cnt * 128].rearrange("p (j t) -> p j t", j=cnt)),
             reads=[pb[bank]], writes=[bdst])


def rope_fm(C, x, bx, xb, bxb, cosT, sinT, brope, perm, kp, out, bout, tmp, btmp, S, bank=3):
    P, ps, pb = C.P, C.ps, C.pb
    P.op("act", lambda e: e.copy(out=xb[0:kp, :], in_=x[0:kp, :]), reads=[bx], writes=[bxb])
    for t0 in range(0, S, 512):
        n = min(512, S - t0)
        P.op("pe", lambda e, t0=t0, n=n: e.matmul(ps[0:kp, bank, 0:n], lhsT=perm[0:kp, 0:kp], rhs=xb[0:kp, t0:t0 + n], start=True, stop=True),
             reads=[bxb, C.b_cstb], writes=[pb[bank]])
        P.op("dve", lambda e, t0=t0, n=n: e.tensor_tensor(out=tmp[0:kp, 0:n], in0=ps[0:kp, bank, 0:n], in1=sinT[0:kp, t0:t0 + n], op=ALU.mult),
             reads=[pb[bank], brope], writes=[btmp])
        P.op("dve", lambda e, t0=t0, n=n: e.tensor_tensor(out=x[0:kp, t0:t0 + n], in0=x[0:kp, t0:t0 + n], in1=cosT[0:kp, t0:t0 + n], op=ALU.mult),
             reads=[bx, brope], writes=[bx])
        P.op("dve", lambda e, t0=t0, n=n: e.tensor_tensor(out=out[0:kp, t0:t0 + n], in0=x[0:kp, t0:t0 + n], in1=tmp[0:kp, 0:n], op=ALU.add),
             reads=[bx, btmp], writes=[bout])


def colnorm_rstd(C, sq_aps, bsq, n_feat, rstd, brstd, n, bank=7):
    P, ps, pb = C.P, C.ps, C.pb
    def mm(e):
        for i, a in enumerate(sq_aps):
            kp = a.shape[0]
            ins = e.matmul(ps[:, bank, 0:n], lhsT=C.ones[0:kp, :], rhs=a, start=(i == 0), stop=(i == len(sq_aps) - 1))
        return ins
    P.op("pe", mm, reads=[bsq, C.b_cst], writes=[pb[bank]])
    P.op("dve", lambda e: e.tensor_scalar(out=rstd[:, 0:n], in0=ps[:, bank, 0:n], scalar1=1.0 / n_feat, scalar2=EPS, op0=ALU.mult, op1=ALU.add),
         reads=[pb[bank]], writes=[brstd])
    P.op("act", lambda e: e.activation(out=rstd[:, 0:n], in_=rstd[:, 0:n], func=AF.Sqrt), reads=[brstd], writes=[brstd])
    P.op("dve", lambda e: e.reciprocal(out=rstd[:, 0:n], in_=rstd[:, 0:n]), reads=[brstd], writes=[brstd])


def diff_phase(C, l):
    nc, P, ps, pb = C.nc, C.P, C.ps, C.pb
    S = C.S
    QB = min(512, S)
    lam_init = 0.8 - 0.6 * math.exp(-0.3 * l)
    with contextlib.ExitStack() as st:
        def sb(name, shape, dt):
            return st.enter_context(nc.sbuf_tensor(uname(name), list(shape), dt))
        cosT = sb("a_cos", [128, S], F32)
        sinT = sb("a_sin", [128, S], F32)
        brope = Buf()
        qf, kf, vf = sb("a_qf", [128, S], F32), sb("a_kf", [128, S], F32), sb("a_vf", [128, S], F32)
        xb = sb("a_xb", [128, S], BF16)
        qb_, kb_ = sb("a_qb", [128, S], BF16), sb("a_kb", [128, S], BF16)
        kbm = [sb("a_kbm%d" % i, [128, S], BF16) for i in range(2)]
        bkbm = [Buf(), Buf()]
        for i in range(2):
            P.op("dve", lambda e, i=i: e.memset(kbm[i][:], 0.0), writes=[bkbm[i]])
        v_tok = sb("a_vt", [128, S // 128, 128], BF16)
        tmp = sb("a_tmp", [128, 512], F32)
        T_ = {"pT": [sb("a_pT%d" % i, [128, 512], BF16) for i in range(4)], "bpT": [Buf() for _ in range(4)]}
        rl = [sb("a_rl%d" % i, [128, 512], F32) for i in range(2)]
        t0_, t1_ = sb("a_t0", [128, 512], F32), sb("a_t1", [128, 512], F32)
        sq = sb("a_sq", [128, 512], F32)
        rstd = sb("a_rstd", [128, 512], F32)
        ob = sb("a_ob", [128, 512], BF16)
        bq, bk, bvf, bxb, bqb, bkb, bvt, btmp = [Buf() for _ in range(8)]
        brl, bt0, bt1, bsq, brstd, bob = [Buf(), Buf()], Buf(), Buf(), Buf(), Buf(), Buf()
        for s in range(C.NSEQ):
            P.dma("sp", lambda e, s=s: e.dma_start(out=cosT[:], in_=C.ROPE[s, 0]), reads=[C.ropeb[s]], writes=[brope])
            P.dma("sp", lambda e, s=s: e.dma_start(out=sinT[:], in_=C.ROPE[s, 1]), reads=[C.ropeb[s]], writes=[brope])
            for h in range(4):
                pj_load(C, qf[:], bq, "aq%d" % h, s)
                pj_load(C, kf[:], bk, "ak%d" % h, s)
                pj_load(C, vf[:], bvf, "av%d" % h, s)
                rope_fm(C, qf, bq, xb, bxb, cosT, sinT, brope, C.cstb[:, CO["permD"]:CO["permD"] + 128], 128, qb_, bqb, tmp, btmp, S)
                rope_fm(C, kf, bk, xb, bxb, cosT, sinT, brope, C.cstb[:, CO["permD"]:CO["permD"] + 128], 128, kb_, bkb, tmp, btmp, S)
                transpose_to_tok(C, vf, bvf, v_tok, bvt, S)
                P.op("act", lambda e: e.copy(out=kbm[0][0:64, :], in_=kb_[0:64, :]), reads=[bkb], writes=[bkbm[0]])
                P.op("dve", lambda e: e.tensor_copy(out=kbm[1][64:128, :], in_=kb_[64:128, :]), reads=[bkb], writes=[bkbm[1]])
                if getattr(C, "dump", False) and s == 0 and h == 0:
                    P.dma("sp", lambda e: e.dma_start(out=C.OM[512:640, 0:S], in_=qb_[:]), reads=[bqb], writes=[C.omb[4][0]])
                    P.dma("sp", lambda e: e.dma_start(out=C.OM[640:768, 0:S], in_=kb_[:]), reads=[bkb], writes=[C.omb[5][0]])
                    P.dma("sp", lambda e: e.dma_start(out=C.OM[768:896, 0:S].rearrange("p (c f) -> p c f", f=128), in_=v_tok[:]), reads=[bvt], writes=[C.omb[6][0]])
                for qb in range(S // QB):
                    for c in range(2):
                        attn_core(C, T_, [(qb_[:, :], kbm[c][:, :])], v_tok, bvt, 0.125, qb,
                                  rl[c], brl[c], 4 + c, 6 + c, [bqb, bkbm[c]])
                    P.op("dve", lambda e: e.tensor_tensor(out=t0_[:, 0:QB], in0=ps[:, 4, 0:QB], in1=rl[0][:, 0:QB], op=ALU.mult),
                         reads=[pb[4], brl[0]], writes=[bt0])
                    P.op("dve", lambda e: e.tensor_tensor(out=t1_[:, 0:QB], in0=ps[:, 5, 0:QB], in1=rl[1][:, 0:QB], op=ALU.mult),
                         reads=[pb[5], brl[1]], writes=[bt1])
                    if getattr(C, "dump", False) and s == 0 and h == 0 and qb == 0:
                        dbt = [st.enter_context(nc.sbuf_tensor("a_dbt%d" % i, [128, 512], BF16)) for i in range(4)]
                        bdb = Buf()
                        P.op("dve", lambda e: e.tensor_copy(out=dbt[0][:, 0:QB], in_=t0_[:, 0:QB]), reads=[bt0], writes=[bdb])
                        P.op("dve", lambda e: e.tensor_copy(out=dbt[1][:, 0:QB], in_=t1_[:, 0:QB]), reads=[bt1], writes=[bdb])
                        P.op("dve", lambda e: e.tensor_copy(out=dbt[2][:, 0:QB], in_=rl[0][:, 0:QB]), reads=[brl[0]], writes=[bdb])
                        P.op("dve", lambda e: e.tensor_copy(out=dbt[3][:, 0:QB], in_=ps[:, 4, 0:QB]), reads=[pb[4]], writes=[bdb])
                        for i in range(4):
                            P.dma("sp", lambda e, i=i: e.dma_start(out=C.OM[1024 + i * 128:1152 + i * 128, 0:QB], in_=dbt[i][:, 0:QB]), reads=[bdb], writes=[C.omb[8 + i][0]])
                    nl = l * AUX_N
                    P.op("dve", lambda e, nl=nl: e.scalar_tensor_tensor(out=t0_[:, 0:QB], in0=t1_[:, 0:QB], scalar=C.aux[:, nl:nl + 1], in1=t0_[:, 0:QB],
                                                                      op0=ALU.mult, op1=ALU.add), reads=[bt0, bt1, C.b_aux], writes=[bt0])
                    P.op("act", lambda e: e.activation(out=sq[:, 0:QB], in_=t0_[:, 0:QB], func=AF.Square), reads=[bt0], writes=[bsq])
                    colnorm_rstd(C, [sq[:, 0:QB]], bsq, 128.0, rstd, brstd, QB)
                    P.op("dve", lambda e: e.tensor_scalar(out=rstd[:, 0:QB], in0=rstd[:, 0:QB], scalar1=1.0 - lam_init, scalar2=None, op0=ALU.mult), reads=[brstd], writes=[brstd])
                    gc = pv_col(l, "subln_g")
                    P.op("dve", lambda e, gc=gc: e.scalar_tensor_tensor(out=ob[:, 0:QB], in0=t0_[:, 0:QB], scalar=C.pvec[:, gc:gc + 1], in1=rstd[:, 0:QB],
                                                                      op0=ALU.mult, op1=ALU.mult), reads=[bt0, brstd, C.b_pvec], writes=[bob])
                    om_store(C, ob[:, 0:QB], bob, h * 128, s, qb * QB, QB)
        P.barrier()


def mla_phase(C, l, wbuq, bwuq, wbukv, bwukv):
    nc, P, ps, pb = C.nc, C.P, C.ps, C.pb
    S = C.S
    QB = min(512, S)
    NT = S // 128
    with contextlib.ExitStack() as st:
        def sb(name, shape, dt):
            return st.enter_context(nc.sbuf_tensor(uname(name), list(shape), dt))
        cosT, sinT = sb("c_cos", [128, S], F32), sb("c_sin", [128, S], F32)
        brope = Buf()
        wq = sb("c_wq", [128, 3, 768], BF16)
        wkv = sb("c_wkv", [128, 1024], BF16)
        bwq, bwkv = Buf(), Buf()
        cq = sb("c_cq", [128, 3, S], F32)
        ckv = sb("c_ckv", [128, S], F32)
        ckr = sb("c_ckr", [128, S], F32)
        cqn = sb("c_cqn", [128, 3, S], BF16)
        ckvn = sb("c_ckvn", [128, S], BF16)
        krb = sb("c_krb", [128, S], BF16)
        xb = sb("c_xb", [128, S], BF16)
        sq3 = sb("c_sq3", [128, 3, 512], F32)
        rstd = sb("c_rstd", [128, 512], F32)
        tmp = sb("c_tmp", [128, 512], F32)
        vt = sb("c_vt", [128, NT, 4, 128], BF16)
        knb, qnb, qrb = sb("c_knb", [128, S], BF16), sb("c_qnb", [128, S], BF16), sb("c_qrb", [128, S], BF16)
        qrf = sb("c_qrf", [128, S], F32)
        T_ = {"pT": [sb("c_pT%d" % i, [128, 512], BF16) for i in range(4)], "bpT": [Buf() for _ in range(4)]}
        rl = sb("c_rl", [128, 512], F32)
        ob = sb("c_ob", [128, 512], BF16)
        bcq, bckv, bckr, bcqn, bckvn, bkrb, bxb, bsq, brstd, btmp, bvt, bknb, bqnb, bqrb, bqrf, brl, bob = [Buf() for _ in range(17)]
        P.dma("sp", lambda e: e.dma_start(out=wq[:], in_=wbuq.rearrange("(k p) c -> p k c", p=128)), reads=bwuq.b, writes=[bwq])
        P.dma("sp", lambda e: e.dma_start(out=wkv[:], in_=wbukv), reads=bwukv.b, writes=[bwkv])
        permM = C.cstb[:, CO["permM"]:CO["permM"] + 128]
        P.op("dve", lambda e: e.memset(krb[:], 0.0), writes=[bkrb])
        P.op("dve", lambda e: e.memset(qrb[:], 0.0), writes=[bqrb])
        for s in range(C.NSEQ):
            P.dma("sp", lambda e, s=s: e.dma_start(out=cosT[:], in_=C.ROPE[s, 2]), reads=[C.ropeb[s]], writes=[brope])
            P.dma("sp", lambda e, s=s: e.dma_start(out=sinT[:], in_=C.ROPE[s, 3]), reads=[C.ropeb[s]], writes=[brope])
            for c in range(3):
                pj_load(C, cq[:, c, :], bcq, "cq%d" % c, s)
            pj_load(C, ckv[:], bckv, "ckv0", s)
            pj_load(C, ckr[0:64, :], bckr, "ckr0", s, rows=64)
            for t0 in range(0, S, 512):
                n = min(512, S - t0)
                for c in range(3):
                    P.op("act", lambda e, c=c, t0=t0, n=n: e.activation(out=sq3[:, c, 0:n], in_=cq[:, c, t0:t0 + n], func=AF.Square), reads=[bcq], writes=[bsq])
                colnorm_rstd(C, [sq3[:, c, 0:n] for c in range(3)], bsq, 384.0, rstd, brstd, n)
                for c in range(3):
                    gc = pv_col(l, "mla_qg", c)
                    P.op("dve", lambda e, c=c, t0=t0, n=n: e.tensor_tensor(out=tmp[:, 0:n], in0=cq[:, c, t0:t0 + n], in1=rstd[:, 0:n], op=ALU.mult),
                         reads=[bcq, brstd], writes=[btmp])
                    P.op("dve", lambda e, c=c, t0=t0, n=n, gc=gc: e.tensor_scalar(out=cqn[:, c, t0:t0 + n], in0=tmp[:, 0:n], scalar1=C.pvec[:, gc:gc + 1], scalar2=None, op0=ALU.mult),
                         reads=[btmp, C.b_pvec], writes=[bcqn])
                P.op("act", lambda e, t0=t0, n=n: e.activation(out=sq3[:, 0, 0:n], in_=ckv[:, t0:t0 + n], func=AF.Square), reads=[bckv], writes=[bsq])
                colnorm_rstd(C, [sq3[:, 0, 0:n]], bsq, 128.0, rstd, brstd, n)
                gc = pv_col(l, "mla_kvg")
                P.op("dve", lambda e, t0=t0, n=n: e.tensor_tensor(out=tmp[:, 0:n], in0=ckv[:, t0:t0 + n], in1=rstd[:, 0:n], op=ALU.mult), reads=[bckv, brstd], writes=[btmp])
                P.op("dve", lambda e, t0=t0, n=n, gc=gc: e.tensor_scalar(out=ckvn[:, t0:t0 + n], in0=tmp[:, 0:n], scalar1=C.pvec[:, gc:gc + 1], scalar2=None, op0=ALU.mult),
                     reads=[btmp, C.b_pvec], writes=[bckvn])
            rope_fm(C, ckr, bckr, xb, bxb, cosT, sinT, brope, permM, 64, krb, bkrb, tmp, btmp, S)
            for tc in range(NT):
                bank = 2 + tc % 2
                def vm(e, tc=tc, bank=bank):
                    for h in range(4):
                        ins = e.matmul(ps[:, bank, h * 128:(h + 1) * 128], lhsT=ckvn[:, tc * 128:(tc + 1) * 128], rhs=wkv[:, h * 256 + 128:h * 256 + 256], start=True, stop=True)
                    return ins
                P.op("pe", vm, reads=[bckvn, bwkv], writes=[pb[bank]])
                P.op("act", lambda e, tc=tc, bank=bank: e.copy(out=vt[:, tc, :, :], in_=ps[:, bank, :].rearrange("p (h d) -> p h d", h=4)), reads=[pb[bank]], writes=[bvt])
            for h in range(4):
                for t0 in range(0, S, 512):
                    n = min(512, S - t0)
                    P.op("pe", lambda e, h=h, t0=t0, n=n: e.matmul(ps[:, 2, 0:n], lhsT=wkv[:, h * 256:h * 256 + 128], rhs=ckvn[:, t0:t0 + n], start=True, stop=True),
                         reads=[bckvn, bwkv], writes=[pb[2]])
                    P.op("act", lambda e, t0=t0, n=n: e.copy(out=knb[:, t0:t0 + n], in_=ps[:, 2, 0:n]), reads=[pb[2]], writes=[bknb])
                    def qn(e, h=h, t0=t0, n=n):
                        for c in range(3):
                            ins = e.matmul(ps[:, 3, 0:n], lhsT=wq[:, c, h * 192:h * 192 + 128], rhs=cqn[:, c, t0:t0 + n], start=(c == 0), stop=(c == 2))
                        return ins
                    P.op("pe", qn, reads=[bcqn, bwq], writes=[pb[3]])
                    P.op("dve", lambda e, t0=t0, n=n: e.tensor_copy(out=qnb[:, t0:t0 + n], in_=ps[:, 3, 0:n]), reads=[pb[3]], writes=[bqnb])
                    def qr(e, h=h, t0=t0, n=n):
                        for c in range(3):
                            ins = e.matmul(ps[0:64, 2, 0:n], lhsT=wq[:, c, h * 192 + 128:h * 192 + 192], rhs=cqn[:, c, t0:t0 + n], start=(c == 0), stop=(c == 2))
                        return ins
                    P.op("pe", qr, reads=[bcqn, bwq], writes=[pb[2]])
                    P.op("act", lambda e, t0=t0, n=n: e.copy(out=qrf[0:64, t0:t0 + n], in_=ps[0:64, 2, 0:n]), reads=[pb[2]], writes=[bqrf])
                rope_fm(C, qrf, bqrf, xb, bxb, cosT, sinT, brope, permM, 64, qrb, bqrb, tmp, btmp, S)
                for qb in range(S // QB):
                    attn_core(C, T_, [(qnb[:, :], knb[:, :]), (qrb[:, :], krb[:, :])], vt[:, :, h, :], bvt, 192.0 ** -0.5, qb,
                              rl, brl, 4, 6, [bqnb, bknb, bqrb, bkrb])
                    P.op("dve", lambda e: e.tensor_tensor(out=ob[:, 0:QB], in0=ps[:, 4, 0:QB], in1=rl[:, 0:QB], op=ALU.mult), reads=[pb[4], brl], writes=[bob])
                    om_store(C, ob[:, 0:QB], bob, 1024 + h * 128, s, qb * QB, QB)
        P.barrier()


def gla_phase(C, l, w_g2l):
    nc, P, ps, pb = C.nc, C.P, C.ps, C.pb
    S = C.S
    NCk = S // 64
    NBk = S // 128
    with contextlib.ExitStack() as st:
        def sb(name, shape, dt):
            return st.enter_context(nc.sbuf_tensor(uname(name), list(shape), dt))
        rm = sb("g_rm", [128, S], F32)
        wg2f, wg2 = sb("g_w2f", [16, 256], F32), sb("g_w2", [16, 256], BF16)
        glf, glb = sb("g_glf", [16, S], F32), sb("g_glb", [16, S], BF16)
        qf, kf = sb("g_qf", [128, S], F32), sb("g_kf", [128, S], F32)
        g_, b_ = sb("g_g", [128, S], F32), sb("g_b", [128, S], F32)
        eb, enb = sb("g_eb", [128, S], F32), sb("g_enb", [128, S], F32)
        kd = sb("g_kd", [128, S], F32)
        qt, kt = sb("g_qt", [128, S], BF16), sb("g_kt", [128, S], BF16)
        kd_tok = sb("g_kdt", [128, NBk, 128], BF16)
        vf = sb("g_vf", [128, S], F32)
        vtok = [sb("g_vt%d" % i, [128, NBk, 128], BF16) for i in range(2)]
        vpar = [[sb("g_vp%d%d" % (i, j), [128, NBk, 128], BF16) for j in range(2)] for i in range(2)]
        qtm = [sb("g_qtm%d" % i, [128, S], BF16) for i in range(2)]
        bqtm = [Buf(), Buf()]
        bvpar = [Buf(), Buf()]
        rf = sb("g_rf", [128, S], F32)
        Sall = sb("g_Sall", [128, NCk, 128], F32)
        Sbf = sb("g_Sbf", [128, NCk, 128], BF16)
        att = [sb("g_att%d" % i, [128, 4, 128], BF16) for i in range(2)]
        osb, sq, rstd, sr = sb("g_osb", [128, 512], F32), sb("g_sq", [128, 512], F32), sb("g_rstd", [128, 512], F32), sb("g_sr", [128, 512], F32)
        ob = sb("g_ob", [128, 512], BF16)
        brm, bw2, bgl, bq, bk, bg, bb, beb, benb, bkd, bqt, bkt, bkdt, bvf, brf, bS, bSb, bosb, bsq, brstd, bsr, bob = [Buf() for _ in range(22)]
        bvt = [Buf(), Buf()]
        batt = [Buf(), Buf()]
        P.op("dve", lambda e: e.memset(rm[:], 1.0), writes=[brm])
        for i in range(2):
            P.op("dve", lambda e, i=i: e.memset(qtm[i][:], 0.0), writes=[bqtm[i]])
            for j in range(2):
                P.op("dve", lambda e, i=i, j=j: e.memset(vpar[i][j][:], 0.0), writes=[bvpar[i]])
        P.op("dve", lambda e: e.memset(rm[:].rearrange("p (n c) -> p n c", c=64)[:, :, 0:1], 0.0), writes=[brm])
        P.dma("sp", lambda e: e.dma_start(out=wg2f[:], in_=w_g2l), writes=[bw2])
        P.op("dve", lambda e: e.tensor_copy(out=wg2[:], in_=wg2f[:]), reads=[bw2], writes=[bw2])
        onecol = C.cst[:, CO["ones"]:CO["ones"] + 1]
        maskG = C.cstb[:, CO["maskG"]:CO["maskG"] + 128]
        ai = 0
        for s in range(C.NSEQ):
            pj_load(C, glf[0:16, :], bgl, "bg0", s, rows=16)
            P.op("dve", lambda e: e.tensor_copy(out=glb[:], in_=glf[:]), reads=[bgl], writes=[bgl])
            for hp in range(2):
                pj_load(C, qf[:], bq, "bq%d" % hp, s)
                pj_load(C, kf[:], bk, "bk%d" % hp, s)
                nb = l * AUX_N + 1 + hp
                for t0 in range(0, S, 512):
                    n = min(512, S - t0)
                    P.op("pe", lambda e, hp=hp, t0=t0, n=n: e.matmul(ps[:, 0, 0:n], lhsT=wg2[0:16, hp * 128:(hp + 1) * 128], rhs=glb[0:16, t0:t0 + n], start=True, stop=True),
                         reads=[bw2, bgl], writes=[pb[0]])
                    P.op("act", lambda e, t0=t0, n=n, nb=nb: e.activation(out=g_[:, t0:t0 + n], in_=ps[:, 0, 0:n], func=AF.Exp, scale=-1.0, bias=C.aux[:, nb:nb + 1]),
                         reads=[pb[0], C.b_aux], writes=[bg])
                P.op("act", lambda e: e.activation(out=g_[:], in_=g_[:], func=AF.Ln, bias=onecol, scale=1.0), reads=[bg, C.b_cst], writes=[bg])
                P.op("dve", lambda e: e.tensor_scalar(out=g_[:], in0=g_[:], scalar1=-1.0 / 16.0, scalar2=None, op0=ALU.mult), reads=[bg], writes=[bg])
                P.op("dve", lambda e: e.tensor_tensor_scan(out=b_[:], data0=rm[:], data1=g_[:], initial=0.0, op0=ALU.mult, op1=ALU.add), reads=[bg, brm], writes=[bb])
                P.op("act", lambda e: e.activation(out=eb[:], in_=b_[:], func=AF.Exp), reads=[bb], writes=[beb])
                P.op("act", lambda e: e.activation(out=enb[:], in_=b_[:], func=AF.Exp, scale=-1.0), reads=[bb], writes=[benb])
                P.op("dve", lambda e: e.scalar_tensor_tensor(out=qt[:], in0=qf[:], scalar=0.125, in1=eb[:], op0=ALU.mult, op1=ALU.mult), reads=[bq, beb], writes=[bqt])
                P.op("dve", lambda e: e.tensor_tensor(out=kt[:], in0=kf[:], in1=enb[:], op=ALU.mult), reads=[bk, benb], writes=[bkt])
                P.op("act", lambda e: e.copy(out=qtm[0][0:64, :], in_=qt[0:64, :]), reads=[bqt], writes=[bqtm[0]])
                P.op("act", lambda e: e.copy(out=qtm[1][64:128, :], in_=qt[64:128, :]), reads=[bqt], writes=[bqtm[1]])
                b3 = b_[:].rearrange("p (n c) -> p n c", c=64)
                P.op("dve", lambda e, b3=b3: e.tensor_tensor(out=kd[:].rearrange("p (n c) -> p n c", c=64), in0=b3[:, :, 63:64].broadcast_to([128, NCk, 64]), in1=b3, op=ALU.subtract),
                     reads=[bb], writes=[bkd])
                P.op("act", lambda e: e.activation(out=kd[:], in_=kd[:], func=AF.Exp), reads=[bkd], writes=[bkd])
                P.op("dve", lambda e: e.tensor_tensor(out=kd[:], in0=kd[:], in1=kf[:], op=ALU.mult), reads=[bkd, bk], writes=[bkd])
                if GLA_STOP <= 1:
                    continue
                transpose_to_tok(C, kd, bkd, kd_tok, bkdt, S)
                for hh in range(2):
                    pj_load(C, vf[:], bvf, "bv%d" % (hp * 2 + hh), s)
                    transpose_to_tok(C, vf, bvf, vtok[hh], bvt[hh], S)
                    P.op("dve", lambda e, hh=hh: e.tensor_copy(out=vpar[hh][0][0:64, :, :], in_=vtok[hh][0:64, :, :]), reads=[bvt[hh]], writes=[bvpar[hh]])
                    P.op("dve", lambda e, hh=hh: e.tensor_copy(out=vpar[hh][1][64:128, :, :], in_=vtok[hh][64:128, :, :]), reads=[bvt[hh]], writes=[bvpar[hh]])
                if GLA_STOP <= 2:
                    continue
                for cg in range(NCk // 4):
                    bank = 4 + cg % 2
                    def inc(e, cg=cg, bank=bank):
                        for c in range(4):
                            n = cg * 4 + c
                            m, par = n // 2, n % 2
                            for hh in range(2):
                                ins = e.matmul(ps[hh * 64:(hh + 1) * 64, bank, c * 128:(c + 1) * 128], lhsT=kd_tok[:, m, hh * 64:(hh + 1) * 64],
                                               rhs=vpar[hh][par][:, m, :], start=True, stop=True)
                        return ins
                    P.op("pe", inc, reads=[bkdt, bvpar[0], bvpar[1]], writes=[pb[bank]])
                    for c in range(4):
                        n = cg * 4 + c
                        if n == 0:
                            P.op("dve", lambda e, bank=bank: e.tensor_copy(out=Sall[:, 0, :], in_=ps[:, bank, 0:128]), reads=[pb[bank]], writes=[bS])
                        else:
                            P.op("dve", lambda e, n=n, c=c, bank=bank: e.scalar_tensor_tensor(
                                out=Sall[:, n, :], in0=Sall[:, n - 1, :], scalar=eb[:, n * 64 + 63:n * 64 + 64], in1=ps[:, bank, c * 128:(c + 1) * 128],
                                op0=ALU.mult, op1=ALU.add), reads=[pb[bank], bS, beb], writes=[bS])
                    P.op("act", lambda e, cg=cg: e.copy(out=Sbf[:, cg * 4:(cg + 1) * 4, :], in_=Sall[:, cg * 4:(cg + 1) * 4, :]), reads=[bS], writes=[bSb])
                if GLA_STOP <= 3:
                    continue
                for hh in range(2):
                    A = hp * 2 + hh
                    base = hh * 64
                    pj_load(C, rf[:], brf, "br%d" % A, s)
                    for g4 in range(0, NBk, 4):
                        cnt = min(4, NBk - g4)
                        W = cnt * 128
                        sl = ai % 2
                        ai += 1
                        def attm(e, g4=g4, cnt=cnt, hh=hh):
                            for j in range(cnt):
                                m = g4 + j
                                ins = e.matmul(ps[:, 0, j * 128:(j + 1) * 128], lhsT=kt[:, m * 128:(m + 1) * 128], rhs=qtm[hh][:, m * 128:(m + 1) * 128],
                                               start=True, stop=True)
                            return ins
                        P.op("pe", attm, reads=[bkt, bqtm[hh]], writes=[pb[0]])
                        P.op("dve", lambda e, sl=sl, cnt=cnt: e.tensor_tensor(out=att[sl][:, 0:cnt, :], in0=ps[:, 0, 0:cnt * 128].rearrange("p (j t) -> p j t", j=cnt),
                                                                           in1=maskG.unsqueeze(1).broadcast_to([128, cnt, 128]), op=ALU.mult),
                             reads=[pb[0], C.b_cstb], writes=[batt[sl]])
                        def om(e, g4=g4, cnt=cnt, base=base, hh=hh, sl=sl):
                            for j in range(cnt):
                                m = g4 + j
                                ins = e.matmul(ps[:, 1, j * 128:(j + 1) * 128], lhsT=vtok[hh][:, m, :], rhs=att[sl][:, j, :], start=True, stop=(m == 0))
                                for par in range(2):
                                    n = 2 * m + par
                                    if n > 0:
                                        ins = e.matmul(ps[:, 1, j * 128 + par * 64:j * 128 + par * 64 + 64], lhsT=Sbf[:, n - 1, :],
                                                       rhs=qtm[hh][:, n * 64:(n + 1) * 64], start=False, stop=True)
                            return ins
                        P.op("pe", om, reads=[bvt[hh], batt[sl], bSb, bqtm[hh]], writes=[pb[1]])
                        t0 = g4 * 128
                        P.op("act", lambda e, W=W: e.copy(out=osb[:, 0:W], in_=ps[:, 1, 0:W]), reads=[pb[1]], writes=[bosb])
                        P.op("act", lambda e, W=W: e.activation(out=sq[:, 0:W], in_=ps[:, 1, 0:W], func=AF.Square), reads=[pb[1]], writes=[bsq])
                        colnorm_rstd(C, [sq[:, 0:W]], bsq, 128.0, rstd, brstd, W)
                        P.op("dve", lambda e, W=W: e.tensor_tensor(out=osb[:, 0:W], in0=osb[:, 0:W], in1=rstd[:, 0:W], op=ALU.mult), reads=[bosb, brstd], writes=[bosb])
                        P.op("act", lambda e, W=W, t0=t0: e.activation(out=sr[:, 0:W], in_=rf[:, t0:t0 + W], func=AF.Silu), reads=[brf], writes=[bsr])
                        gc = pv_col(l, "gla_ng")
                        P.op("dve", lambda e, W=W, gc=gc: e.scalar_tensor_tensor(out=ob[:, 0:W], in0=osb[:, 0:W], scalar=C.pvec[:, gc:gc + 1], in1=sr[:, 0:W],
                                                                              op0=ALU.mult, op1=ALU.mult), reads=[bosb, bsr, C.b_pvec], writes=[bob])
                        om_store(C, ob[:, 0:W], bob, 512 + A * 128, s, t0, W)
        P.barrier()


def ssd_phase(C, l):
    nc, P, ps, pb = C.nc, C.P, C.ps, C.pb
    S = C.S
    NCk = S // 128
    cst = C.cst
    onecol = cst[:, CO["ones"]:CO["ones"] + 1]
    maskS = cst[:, CO["maskS"]:CO["maskS"] + 128]
    with contextlib.ExitStack() as st:
        def sb(name, shape, dt):
            return st.enter_context(nc.sbuf_tensor(uname(name), list(shape), dt))
        xs = [sb("d_xs%d" % i, [128, S], F32) for i in range(4)]
        Bb = [sb("d_Bb%d" % i, [128, S], BF16) for i in range(2)]
        Cb = [sb("d_Cb%d" % i, [128, S], BF16) for i in range(2)]
        Btok = sb("d_Btok", [128, NCk, 256], BF16)
        dt8, dA8, ac8, ec8, rm8 = [sb("d_s%d" % i, [8, S], F32) for i in range(5)]
        raw = [sb("d_raw%d" % i, [128, S + 3], F32) for i in range(2)]
        acc = sb("d_acc", [128, S], F32)
        xdt_tok = sb("d_xdtt", [128, NCk, 512], BF16)
        xdd = [sb("d_xdd%d" % i, [128, 512], BF16) for i in range(2)]
        acol, dcol = sb("d_acol", [128, NCk, 8], F32), sb("d_dcol", [128, NCk, 8], F32)
        R8 = sb("d_R8", [8, NCk, 8], F32)
        cdrow = sb("d_cdrow", [128, NCk, 8], F32)
        stt = [sb("d_st%d" % i, [128, 512], F32) for i in range(2)]
        prevb = sb("d_prevb", [128, NCk, 512], BF16)
        zt = [sb("d_zt%d" % i, [128, 4, 128], F32) for i in range(2)]
        cbm = [sb("d_cbm%d" % i, [128, 128], F32) for i in range(2)]
        Dm = [sb("d_Dm%d" % i, [128, 128], F32) for i in range(2)]
        Mh = [sb("d_Mh%d" % i, [128, 128], BF16) for i in range(2)]
        ecb = [sb("d_ecb%d" % i, [128, 128], F32) for i in range(2)]
        yo = sb("d_yo", [128, 4, 128], F32)
        ysq = sb("d_ysq", [128, 4, 128], F32)
        rstd = sb("d_rstd", [128, 128], F32)
        ob = sb("d_ob", [128, 4, 128], BF16)
        bxs = [Buf() for _ in range(4)]
        bBb, bCb = [Buf(), Buf()], [Buf(), Buf()]
        bBtok, b8, braw, bacc, bxdtt, bcol, bR8, bcdrow, bprev, byo, bysq, brstd, bob = [Buf() for _ in range(13)]
        braw = [Buf(), Buf()]
        bxdd, bst, bzt, bcbm, bDm, bMh, becb = [[Buf(), Buf()] for _ in range(7)]
        P.op("dve", lambda e: e.memset(rm8[:], 1.0), writes=[b8])
        P.op("dve", lambda e: e.memset(rm8[:].rearrange("p (n c) -> p n c", c=128)[:, :, 0:1], 0.0), writes=[b8])
        for i in range(2):
            P.op("dve", lambda e, i=i: e.memset(raw[i][:, 0:3], 0.0), writes=[braw[i]])
        nega = C.aux[0:8, l * AUX_N + 3:l * AUX_N + 4]
        dtb = C.pvec[0:8, pv_col(l, "dt_bias"):pv_col(l, "dt_bias") + 1]
        ri = 0
        for s in range(C.NSEQ):
            for ti in range(8):
                r = ri % 2
                ri += 1
                pj_load(C, raw[r][:, 3:3 + S], braw[r], "dx%d" % ti, s)
                cw = pv_col(l, "conv_w")
                P.op("dve", lambda e, r=r, ti=ti, cw=cw: e.tensor_scalar(out=acc[:], in0=raw[r][:, 0:S], scalar1=C.pvec[:, cw + ti:cw + ti + 1], scalar2=None, op0=ALU.mult),
                     reads=[braw[r], C.b_pvec], writes=[bacc])
                for k in range(1, 4):
                    P.op("dve", lambda e, r=r, ti=ti, cw=cw, k=k: e.scalar_tensor_tensor(out=acc[:], in0=raw[r][:, k:k + S], scalar=C.pvec[:, cw + k * 8 + ti:cw + k * 8 + ti + 1],
                                                                                   in1=acc[:], op0=ALU.mult, op1=ALU.add), reads=[braw[r], bacc, C.b_pvec], writes=[bacc])
                cb_ = pv_col(l, "conv_b", ti)
                if ti < 4:
                    P.op("act", lambda e, ti=ti, cb_=cb_: e.activation(out=xs[ti][:], in_=acc[:], func=AF.Silu, bias=C.pvec[:, cb_:cb_ + 1], scale=1.0),
                         reads=[bacc, C.b_pvec], writes=[bxs[ti]])
                else:
                    P.op("act", lambda e, cb_=cb_: e.activation(out=acc[:], in_=acc[:], func=AF.Silu, bias=C.pvec[:, cb_:cb_ + 1], scale=1.0),
                         reads=[bacc, C.b_pvec], writes=[bacc])
                    g = (ti - 4) % 2
                    if ti < 6:
                        P.op("dve", lambda e, g=g: e.tensor_copy(out=Bb[g][:], in_=acc[:]), reads=[bacc], writes=[bBb[g]])
                        for c4 in range(0, NCk, 4):
                            cnt = min(4, NCk - c4)
                            bank = 2 + (c4 // 4) % 2
                            def tr(e, c4=c4, cnt=cnt, bank=bank):
                                for j in range(cnt):
                                    ins = e.transpose(out=ps[:, bank, j * 128:(j + 1) * 128], in_=acc[:, (c4 + j) * 128:(c4 + j + 1) * 128], identity=C.ident)
                                return ins
                            P.op("pe", tr, reads=[bacc, C.b_cst], writes=[pb[bank]])
                            P.op("act", lambda e, c4=c4, cnt=cnt, bank=bank, g=g: e.copy(out=Btok[:, c4:c4 + cnt, g * 128:(g + 1) * 128],
                                                                                     in_=ps[:, bank, 0:cnt * 128].rearrange("p (j t) -> p j t", j=cnt)),
                                 reads=[pb[bank]], writes=[bBtok])
                    else:
                        P.op("dve", lambda e, g=g: e.tensor_copy(out=Cb[g][:], in_=acc[:]), reads=[bacc], writes=[bCb[g]])
            pj_load(C, dt8[:], b8, "dt0", s, rows=8)
            P.op("act", lambda e: e.activation(out=dt8[:], in_=dt8[:], func=AF.Exp, bias=dtb, scale=1.0), reads=[b8, C.b_pvec], writes=[b8])
            P.op("act", lambda e: e.activation(out=dt8[:], in_=dt8[:], func=AF.Ln, bias=onecol[0:8, :], scale=1.0), reads=[b8, C.b_cst], writes=[b8])
            P.op("dve", lambda e: e.tensor_scalar(out=dA8[:], in0=dt8[:], scalar1=nega, scalar2=None, op0=ALU.mult), reads=[b8, C.b_aux], writes=[b8])
            P.op("dve", lambda e: e.tensor_tensor_scan(out=ac8[:], data0=rm8[:], data1=dA8[:], initial=0.0, op0=ALU.mult, op1=ALU.add), reads=[b8], writes=[b8])
            P.op("act", lambda e: e.activation(out=ec8[:], in_=ac8[:], func=AF.Exp), reads=[b8], writes=[b8])
            a3 = ac8[:].rearrange("p (n c) -> p n c", c=128)
            P.op("dve", lambda e, a3=a3: e.tensor_tensor(out=dA8[:].rearrange("p (n c) -> p n c", c=128), in0=a3[:, :, 127:128].broadcast_to([8, NCk, 128]), in1=a3, op=ALU.subtract),
                 reads=[b8], writes=[b8])
            P.op("act", lambda e: e.activation(out=dA8[:], in_=dA8[:], func=AF.Exp), reads=[b8], writes=[b8])
            for (src, dst) in ((ac8, acol), (dA8, dcol)):
                for c4 in range(0, NCk, 4):
                    cnt = min(4, NCk - c4)
                    def tr(e, src=src, c4=c4, cnt=cnt):
                        for j in range(cnt):
                            ins = e.transpose(out=ps[:, 2, j * 8:(j + 1) * 8], in_=src[0:8, (c4 + j) * 128:(c4 + j + 1) * 128], identity=C.ident[0:8, 0:8])
                        return ins
                    P.op("pe", tr, reads=[b8, C.b_cst], writes=[pb[2]])
                    P.op("dve", lambda e, dst=dst, c4=c4, cnt=cnt: e.tensor_copy(out=dst[:, c4:c4 + cnt, :], in_=ps[:, 2, 0:cnt * 8].rearrange("p (j h) -> p j h", j=cnt)),
                         reads=[pb[2]], writes=[bcol])
            e3 = ec8[:].rearrange("p (n c) -> p n c", c=128)
            P.op("dve", lambda e, e3=e3: e.tensor_tensor(out=R8[:], in0=C.ident[0:8, 0:8].unsqueeze(1).broadcast_to([8, NCk, 8]), in1=e3[:, :, 127:128].broadcast_to([8, NCk, 8]), op=ALU.mult),
                 reads=[b8, C.b_cst], writes=[bR8])
            P.op("pe", lambda e: e.matmul(ps[:, 3, 0:NCk * 8], lhsT=C.ones[0:8, :], rhs=R8[:].rearrange("p n h -> p (n h)"), start=True, stop=True), reads=[bR8, C.b_cst], writes=[pb[3]])
            P.op("dve", lambda e: e.tensor_copy(out=cdrow[:].rearrange("p n h -> p (n h)"), in_=ps[:, 3, 0:NCk * 8]), reads=[pb[3]], writes=[bcdrow])
            for pr in range(4):
                for t0 in range(0, S, 512):
                    n = min(512, S - t0)
                    so = CO["selE"] + pr * 128
                    P.op("pe", lambda e, so=so, t0=t0, n=n: e.matmul(ps[:, 3, 0:n], lhsT=cst[0:8, so:so + 128], rhs=dt8[:, t0:t0 + n], start=True, stop=True),
                         reads=[b8, C.b_cst], writes=[pb[3]])
                    P.op("dve", lambda e, pr=pr, t0=t0, n=n: e.tensor_tensor(out=acc[:, t0:t0 + n], in0=xs[pr][:, t0:t0 + n], in1=ps[:, 3, 0:n], op=ALU.mult),
                         reads=[pb[3], bxs[pr]], writes=[bacc])
                for c4 in range(0, NCk, 4):
                    cnt = min(4, NCk - c4)
                    bank = 2 + (c4 // 4) % 2
                    def tr(e, c4=c4, cnt=cnt, bank=bank):
                        for j in range(cnt):
                            ins = e.transpose(out=ps[:, bank, j * 128:(j + 1) * 128], in_=acc[:, (c4 + j) * 128:(c4 + j + 1) * 128], identity=C.ident)
                        return ins
                    P.op("pe", tr, reads=[bacc, C.b_cst], writes=[pb[bank]])
                    P.op("act", lambda e, c4=c4, cnt=cnt, bank=bank, pr=pr: e.copy(out=xdt_tok[:, c4:c4 + cnt, pr * 128:(pr + 1) * 128],
                                                                              in_=ps[:, bank, 0:cnt * 128].rearrange("p (j t) -> p j t", j=cnt)),
                         reads=[pb[bank]], writes=[bxdtt])
            for n in range(NCk - 1):
                sl = n % 2
                P.op("dve", lambda e, n=n, sl=sl: e.tensor_tensor(out=xdd[sl][:].rearrange("p (h q) -> p h q", h=8), in0=xdt_tok[:, n, :].rearrange("p (h q) -> p h q", h=8),
                                                               in1=dcol[:, n, :].unsqueeze(2).broadcast_to([128, 8, 64]), op=ALU.mult), reads=[bxdtt, bcol], writes=[bxdd[sl]])
                bank = 4 + n % 2
                def incm(e, n=n, sl=sl, bank=bank):
                    for g in range(2):
                        ins = e.matmul(ps[:, bank, g * 256:(g + 1) * 256], lhsT=Btok[:, n, g * 128:(g + 1) * 128], rhs=xdd[sl][:, g * 256:(g + 1) * 256], start=True, stop=True)
                    return ins
                P.op("pe", incm, reads=[bBtok, bxdd[sl]], writes=[pb[bank]])
                cur, prv = stt[n % 2], stt[(n + 1) % 2]
                if n == 0:
                    P.op("dve", lambda e, cur=cur, bank=bank: e.tensor_copy(out=cur[:], in_=ps[:, bank, :]), reads=[pb[bank]], writes=[bst[n % 2]])
                else:
                    P.op("dve", lambda e, n=n, cur=cur, prv=prv: e.tensor_tensor(out=cur[:].rearrange("p (h q) -> p h q", h=8), in0=prv[:].rearrange("p (h q) -> p h q", h=8),
                                                                              in1=cdrow[:, n, :].unsqueeze(2).broadcast_to([128, 8, 64]), op=ALU.mult),
                         reads=[bst[(n + 1) % 2], bcdrow], writes=[bst[n % 2]])
                    P.op("dve", lambda e, cur=cur, bank=bank: e.tensor_tensor(out=cur[:], in0=cur[:], in1=ps[:, bank, :], op=ALU.add), reads=[pb[bank], bst[n % 2]], writes=[bst[n % 2]])
                P.op("act", lambda e, n=n, cur=cur: e.copy(out=prevb[:, n + 1, :], in_=cur[:]), reads=[bst[n % 2]], writes=[bprev])
            hi = 0
            for n in range(NCk):
                tsl = slice(n * 128, (n + 1) * 128)
                zs = n % 2
                for pr in range(4):
                    pj_load(C, zt[zs][:, pr, :], bzt[zs], "dz%d" % pr, s, t0=n * 128, nt=128)
                for g in range(2):
                    cs = (n * 2 + g) % 2
                    P.op("pe", lambda e, g=g, tsl=tsl: e.matmul(ps[:, 0, 0:128], lhsT=Bb[g][:, tsl], rhs=Cb[g][:, tsl], start=True, stop=True), reads=[bBb[g], bCb[g]], writes=[pb[0]])
                    P.op("dve", lambda e, cs=cs: e.tensor_tensor(out=cbm[cs][:], in0=ps[:, 0, 0:128], in1=maskS, op=ALU.mult), reads=[pb[0], C.b_cst], writes=[bcbm[cs]])
                    for hh in range(4):
                        h = g * 4 + hh
                        pr = h // 2
                        k = hi % 2
                        hi += 1
                        so = CO["selH"] + h * 128
                        P.op("pe", lambda e, so=so, tsl=tsl: e.matmul(ps[:, 1, 0:128], lhsT=cst[0:8, so:so + 128], rhs=ac8[:, tsl], start=True, stop=True), reads=[b8, C.b_cst], writes=[pb[1]])
                        P.op("dve", lambda e, k=k, n=n, h=h: e.tensor_scalar(out=Dm[k][:], in0=ps[:, 1, 0:128], scalar1=acol[:, n, h:h + 1], scalar2=None, op0=ALU.subtract),
                             reads=[pb[1], bcol], writes=[bDm[k]])
                        P.op("dve", lambda e, k=k: e.tensor_scalar(out=Dm[k][:], in0=Dm[k][:], scalar1=0.0, scalar2=None, op0=ALU.min), reads=[bDm[k]], writes=[bDm[k]])
                        P.op("act", lambda e, k=k: e.activation(out=Dm[k][:], in_=Dm[k][:], func=AF.Exp), reads=[bDm[k]], writes=[bDm[k]])
                        P.op("dve", lambda e, k=k, cs=cs: e.tensor_tensor(out=Mh[k][:], in0=Dm[k][:], in1=cbm[cs][:], op=ALU.mult), reads=[bDm[k], bcbm[cs]], writes=[bMh[k]])
                        P.op("pe", lambda e, k=k, n=n, h=h, pr=pr: e.matmul(ps[(h % 2) * 64:(h % 2) * 64 + 64, 6, pr * 128:(pr + 1) * 128], lhsT=xdt_tok[:, n, h * 64:(h + 1) * 64], rhs=Mh[k][:],
                                                                          start=True, stop=True), reads=[bxdtt, bMh[k]], writes=[pb[6]])
                for pr in range(4):
                    g = pr // 2
                    P.op("dve", lambda e, pr=pr: e.tensor_copy(out=yo[:, pr, :], in_=ps[:, 6, pr * 128:(pr + 1) * 128]), reads=[pb[6]], writes=[byo])
                    if n > 0:
                        k = pr % 2
                        so = CO["selE"] + pr * 128
                        P.op("pe", lambda e, so=so, tsl=tsl: e.matmul(ps[:, 1, 0:128], lhsT=cst[0:8, so:so + 128], rhs=ec8[:, tsl], start=True, stop=True), reads=[b8, C.b_cst], writes=[pb[1]])
                        P.op("act", lambda e, k=k: e.copy(out=ecb[k][:], in_=ps[:, 1, 0:128]), reads=[pb[1]], writes=[becb[k]])
                        P.op("pe", lambda e, n=n, pr=pr, g=g, tsl=tsl: e.matmul(ps[:, 7, 0:128], lhsT=prevb[:, n, pr * 128:(pr + 1) * 128], rhs=Cb[g][:, tsl], start=True, stop=True),
                             reads=[bprev, bCb[g]], writes=[pb[7]])
                        P.op("dve", lambda e, k=k: e.tensor_tensor(out=ecb[k][:], in0=ecb[k][:], in1=ps[:, 7, 0:128], op=ALU.mult), reads=[pb[7], becb[k]], writes=[becb[k]])
                        P.op("dve", lambda e, k=k, pr=pr: e.tensor_tensor(out=yo[:, pr, :], in0=yo[:, pr, :], in1=ecb[k][:], op=ALU.add), reads=[byo, becb[k]], writes=[byo])
                    dc = pv_col(l, "ssd_d", pr)
                    P.op("dve", lambda e, pr=pr, dc=dc, tsl=tsl: e.scalar_tensor_tensor(out=yo[:, pr, :], in0=xs[pr][:, tsl], scalar=C.pvec[:, dc:dc + 1], in1=yo[:, pr, :], op0=ALU.mult, op1=ALU.add),
                         reads=[byo, bxs[pr], C.b_pvec], writes=[byo])
                P.op("act", lambda e, zs=zs: e.activation(out=zt[zs][:], in_=zt[zs][:], func=AF.Silu), reads=[bzt[zs]], writes=[bzt[zs]])
                P.op("dve", lambda e, zs=zs: e.tensor_tensor(out=yo[:], in0=yo[:], in1=zt[zs][:], op=ALU.mult), reads=[byo, bzt[zs]], writes=[byo])
                P.op("act", lambda e: e.activation(out=ysq[:], in_=yo[:], func=AF.Square), reads=[byo], writes=[bysq])
                for g in range(2):
                    colnorm_rstd(C, [ysq[:, 2 * g, :], ysq[:, 2 * g + 1, :]], bysq, 256.0, rstd, brstd, 128)
                    for pr in (2 * g, 2 * g + 1):
                        gc = pv_col(l, "ssd_ng", pr)
                        P.op("dve", lambda e, pr=pr, gc=gc: e.scalar_tensor_tensor(out=ob[:, pr, :], in0=yo[:, pr, :], scalar=C.pvec[:, gc:gc + 1], in1=rstd[:, 0:128], op0=ALU.mult, op1=ALU.mult),
                             reads=[byo, brstd, C.b_pvec], writes=[bob])
                for pr in range(4):
                    om_store(C, ob[:, pr, :], bob, 1536 + pr * 128, s, n * 128, 128)
        P.barrier()


def merge_phase(C, l, wbin, bwin, wbbr, bwbr, wbout, bwout):
    nc, P, ps, pb = C.nc, C.P, C.ps, C.pb
    with contextlib.ExitStack() as st:
        def sb(name, shape, dt):
            return st.enter_context(nc.sbuf_tensor(uname(name), list(shape), dt))
        hf = sb("m_hf", [128, KC, TB], F32)
        hb = sb("m_hb", [128, KC, TB], BF16)
        stage = [sb("m_stg%d" % i, [128, 2, TB], F32) for i in range(2)]
        om = sb("m_om", [128, 16, TB], BF16)
        mg = sb("m_mg", [128, 4, TB], F32)
        mgb = sb("m_mgb", [128, KC, TB], BF16)
        sgm = sb("m_sgm", [128, 4, TB], F32)
        slots = [sb("m_w%d" % i, [128, 16, 512], BF16) for i in range(3)]
        bsl = [Buf() for _ in range(3)]
        sq = [sb("m_sq%d" % i, [128, TB], BF16) for i in range(2)]
        bsq = [Buf(), Buf()]
        mean, msq, rstd = sb("m_mean", [128, TB], F32), sb("m_msq", [128, TB], F32), sb("m_rstd", [128, TB], F32)
        bhf, bhb, bom, bmgb, bst = Buf(), Buf(), Buf(), Buf(), Buf()
        bstage = [Buf(), Buf()]
        bmg = [Buf() for _ in range(4)]
        bsg = [Buf() for _ in range(4)]
        cnt = [0]
        W = WStream(C, slots, bsl)
        prefetch_hb(C, 0, hb, bhb, stage, bstage, cnt)
        for blk in range(C.NB):
            for fg in range(4):
                for br in range(4):
                    def gstep(s, fg=fg, br=br, blk=blk):
                        if fg == 0 and br == 0:
                            P.dma("sp", lambda e: e.dma_start(
                                out=om[:], in_=C.OM[:, blk * TB:(blk + 1) * TB].rearrange("(k p) t -> p k t", p=128)), reads=[C.omb[r][blk] for r in range(16)], writes=[bom])
                        if fg == 2 and br == 0:
                            P.dma("sp", lambda e: e.dma_start(
                                out=hf[:], in_=C.HT[:, blk * TB:(blk + 1) * TB].rearrange("(k p) t -> p k t", p=128)), reads=[C.hbuf[blk]], writes=[bhf])

                        def gm(e):
                            for k in range(KC):
                                for j in range(4):
                                    ins = e.matmul(ps[:, j, :], lhsT=slots[s][:, k, j * 128:(j + 1) * 128], rhs=hb[:, k, :], start=(k == 0), stop=(k == KC - 1))
                            return ins
                        P.op("pe", gm, reads=[bsl[s], bhb], writes=[pb[j] for j in range(4)])
                    W.add((wbin, bwin, 0, 16, GATE0 + br * 2048 + fg * 512, 512), gstep)

                    def bstep(s2, fg=fg, br=br):
                        def bm(e):
                            for k in range(4):
                                for j in range(4):
                                    ins = e.matmul(ps[:, 4 + j, :], lhsT=slots[s2][:, k, j * 128:(j + 1) * 128], rhs=om[:, br * 4 + k, :], start=(k == 0), stop=(k == 3))
                            return ins
                        P.op("pe", bm, reads=[bsl[s2], bom], writes=[pb[4 + j] for j in range(4)])
                        for j in range(4):
                            bc = pv_col(l, "b_gate", br * 16 + fg * 4 + j)
                            P.op("act", lambda e, j=j, bc=bc: e.activation(out=sgm[:, j, :], in_=ps[:, j, :], func=AF.Sigmoid, bias=C.pvec[:, bc:bc + 1], scale=1.0),
                                 reads=[pb[j], C.b_pvec], writes=[bsg[j]])
                            if br == 0:
                                P.op("dve", lambda e, j=j: e.tensor_tensor(out=mg[:, j, :], in0=ps[:, 4 + j, :], in1=sgm[:, j, :], op=ALU.mult), reads=[pb[4 + j], bsg[j]], writes=[bmg[j]])
                            else:
                                P.op("dve", lambda e, j=j: e.tensor_tensor(out=sgm[:, j, :], in0=ps[:, 4 + j, :], in1=sgm[:, j, :], op=ALU.mult), reads=[pb[4 + j], bsg[j]], writes=[bsg[j]])
                                if br < 3:
                                    P.op("dve", lambda e, j=j: e.tensor_tensor(out=mg[:, j, :], in0=mg[:, j, :], in1=sgm[:, j, :], op=ALU.add), reads=[bmg[j], bsg[j]], writes=[bmg[j]])
                                else:
                                    P.op("dve", lambda e, j=j: e.tensor_tensor(out=mgb[:, fg * 4 + j, :], in0=mg[:, j, :], in1=sgm[:, j, :], op=ALU.add), reads=[bmg[j], bsg[j]], writes=[bmgb])
                    W.add((wbbr, bwbr, br * 512, 4, fg * 512, 512), bstep)
            for cg in range(4):
                def ostep(s, cg=cg, blk=blk):
                    banks = [(cg % 2) * 4 + j for j in range(4)]

                    def omm(e):
                        for k in range(KC):
                            for j in range(4):
                                ins = e.matmul(ps[:, banks[j], :], lhsT=slots[s][:, k, j * 128:(j + 1) * 128], rhs=mgb[:, k, :], start=(k == 0), stop=(k == KC - 1))
                        return ins
                    P.op("pe", omm, reads=[bsl[s], bmgb], writes=[pb[b] for b in banks])
                    if cg == 0 and blk + 1 < C.NB:
                        prefetch_hb(C, blk + 1, hb, bhb, stage, bstage, cnt)
                    for j in range(4):
                        c = cg * 4 + j
                        P.op("dve", lambda e, c=c, b=banks[j]: e.scalar_tensor_tensor(out=hf[:, c, :], in0=hf[:, c, :], scalar=ALPHA, in1=ps[:, b, :], op0=ALU.mult, op1=ALU.add),
                             reads=[pb[banks[j]], bhf], writes=[bhf])
                    if cg == 3:
                        ln_block(C, hf, bhf, l, 1, blk, (sq, bsq, mean, msq, rstd, bst))
                W.add((wbout, bwout, 0, 16, cg * 512, 512), ostep)
        W.run()
        P.barrier()
```

```python
import math
import contextlib
import numpy as np
import concourse.bass as bass
import concourse.mybir as mybir
from concourse.bass_utils import run_bass_kernel_spmd

F32 = mybir.dt.float32
BF16 = mybir.dt.bfloat16
I32 = mybir.dt.int32
AF = mybir.ActivationFunctionType
ALU = mybir.AluOpType

D = 2048
DFF = 5632
KC = D // 128
FCH = DFF // 128
TB = 512
DEPTH = 4
ALPHA = (2 * DEPTH) ** 0.25
EPS = 1e-5
NIN = 13400
GATE0 = 5208
ROPE_THETA = 500000.0

ENGS = ("pe", "act", "dve", "pool", "sp")
N_DSEM = 12
N_WSEM = 4


class Buf:
    __slots__ = ("name", "w", "r")

    def __init__(self, name=""):
        self.name = name
        self.w = None
        self.r = []


class Prog:
    def __init__(self, nc):
        self.nc = nc
        self.eng = {"pe": nc.tensor, "act": nc.scalar, "dve": nc.vector, "pool": nc.gpsimd, "sp": nc.sync}
        self.seq = {e: 0 for e in ENGS}
        self.waited = {e: {} for e in ENGS}
        self.sems = {}
        for e in ENGS:
            self.sems["c" + e] = nc.alloc_semaphore("c_" + e)
        self.dtot = {}
        for i in range(N_DSEM):
            self.sems["d%d" % i] = nc.alloc_semaphore("dma%d" % i)
            self.dtot["d%d" % i] = 0
        for i in range(N_WSEM):
            self.sems["w%d" % i] = nc.alloc_semaphore("wdma%d" % i)
            self.dtot["w%d" % i] = 0
        self.rr = {"d": 0, "w": 0}
        self.n_ins = 0

    def _need(self, eng, tok, waits):
        if tok is None:
            return
        key, val, src = tok
        if src == eng and eng == "pe":
            return
        if self.waited[eng].get(key, 0) >= val:
            return
        self.waited[eng][key] = val
        waits[key] = max(waits.get(key, 0), val)

    def _deps(self, eng, reads, writes):
        waits = {}
        for b in reads:
            self._need(eng, b.w, waits)
        for b in writes:
            self._need(eng, b.w, waits)
            for t in b.r:
                self._need(eng, t, waits)
        return waits

    def _mark(self, tok, reads, writes):
        for b in reads:
            b.r.append(tok)
            if len(b.r) > 16:
                best = {}
                for t in b.r:
                    if t[0] not in best or best[t[0]][1] < t[1]:
                        best[t[0]] = t
                b.r = list(best.values())
        for b in writes:
            b.w = tok
            b.r = []

    def _emit_waits(self, eng, waits):
        e = self.eng[eng]
        for k, v in waits.items():
            e.wait_ge(self.sems[k], v)

    def op(self, eng, fn, reads=(), writes=()):
        waits = self._deps(eng, reads, writes)
        self._emit_waits(eng, waits)
        ins = fn(self.eng[eng])
        self.seq[eng] += 1
        ins.then_inc(self.sems["c" + eng], 1)
        tok = ("c" + eng, self.seq[eng], eng)
        self._mark(tok, reads, writes)
        return tok

    def dma(self, eng, fn, reads=(), writes=(), grp="d"):
        n = N_DSEM if grp == "d" else N_WSEM
        s = self.rr[grp]
        self.rr[grp] = (s + 1) % n
        key = "%s%d" % (grp, s)
        waits = self._deps(eng, reads, writes)
        if self.dtot[key] > 0:
            self._need(eng, (key, self.dtot[key], None), waits)
        self._emit_waits(eng, waits)
        ins = fn(self.eng[eng])
        self.dtot[key] += 16
        ins.then_inc(self.sems[key], 16)
        tok = (key, self.dtot[key], None)
        self._mark(tok, reads, writes)
        return tok

    def barrier(self, engs=("pe", "act", "dve", "sp")):
        for e in engs:
            waits = {}
            for x in engs:
                if x != e and self.seq[x] > 0:
                    self._need(e, ("c" + x, self.seq[x], x), waits)
            for i in range(N_DSEM):
                k = "d%d" % i
                if self.dtot[k] > 0:
                    self._need(e, (k, self.dtot[k], None), waits)
            self._emit_waits(e, waits)

    def wait_bufs(self, eng, bufs):
        waits = {}
        for b in bufs:
            self._need(eng, b.w, waits)
        self._emit_waits(eng, waits)


W_IN_CHUNKS = []


def _mk_chunks():
    def add(name, c0, n):
        i = 0
        while n > 0:
            m = min(128, n)
            W_IN_CHUNKS.append((name + str(i), c0, m))
            c0 += m
            n -= m
            i += 1
    add("aq", 0, 512)
    add("ak", 512, 512)
    add("av", 1024, 512)
    add("bq", 1536, 256)
    add("bk", 1792, 256)
    add("bv", 2048, 512)
    add("bg", 2560, 16)
    add("br", 2576, 512)
    add("cq", 3088, 384)
    add("ckv", 3472, 128)
    add("ckr", 3600, 64)
    add("dz", 3664, 512)
    add("dx", 4176, 1024)
    add("dt", 5200, 8)


_mk_chunks()
CH_IDX = {c[0]: i for i, c in enumerate(W_IN_CHUNKS)}
NCH = len(W_IN_CHUNKS)
W_IN_GROUPS = [["aq0", "aq1", "aq2", "aq3"], ["ak0", "ak1", "ak2", "ak3"], ["av0", "av1", "av2", "av3"],
               ["bq0", "bq1", "bk0", "bk1"], ["bv0", "bv1", "bv2", "bv3"], ["bg0", "br0", "br1", "br2"],
               ["br3", "cq0", "cq1", "cq2"], ["ckv0", "ckr0", "dz0", "dz1"], ["dz2", "dz3", "dx0", "dx1"],
               ["dx2", "dx3", "dx4", "dx5"], ["dx6", "dx7", "dt0"]]


def pj_row(name):
    return CH_IDX[name] * 128


class Ctx:
    pass


_UID = [0]


def uname(name):
    _UID[0] += 1
    return "%s_u%d" % (name, _UID[0])


def build_program(S, NSEQ, depth, dbg=None, phases=("ffn1", "mix", "ffn2"), mix_sub=("proj", "diff", "gla", "mla", "ssd", "merge")):
    T = S * NSEQ
    NB = T // TB
    nc = bass.Bass("TRN2", target_bir_lowering=False)
    P = Prog(nc)
    C = Ctx()
    C.nc, C.P, C.S, C.NSEQ, C.T, C.NB = nc, P, S, NSEQ, T, NB
    C.mix_sub = mix_sub
    C.dump = (dbg == 'dump')

    def din(name, shape, dt=F32):
        return nc.dram_tensor(name, list(shape), dt, kind="ExternalInput").ap()

    def dscr(name, shape, dt=F32):
        return nc.dram_tensor(name, list(shape), dt, kind="Internal").ap()

    x_in = din("x", [T, D])
    pos_in = din("positions", [NSEQ, S], I32)
    w_gu = [din("ffn1_w_gu", [depth, D, 2 * DFF]), din("ffn2_w_gu", [depth, D, 2 * DFF])]
    w_dn = [din("ffn1_w_down", [depth, DFF, D]), din("ffn2_w_down", [depth, DFF, D])]
    w_in = din("w_in", [depth, D, NIN])
    w_br = din("w_branch", [depth, 4 * 512, D])
    w_out = din("w_out", [depth, D, D])
    w_uq = din("mla_w_uq", [depth, 384, 768])
    w_ukv = din("mla_w_ukv", [depth, 128, 1024])
    w_g2 = din("gla_w_gate2", [depth, 16, 256])
    NPV = depth * PV_PER_LAYER
    pvec_in = din("pvec", [128, NPV])
    cst_in = din("cst", [128, CST_N])
    y_out = nc.dram_tensor("y", [T, D], F32, kind="ExternalOutput").ap()

    HT = dscr("HT", [D, T])
    C.HT = HT
    if dbg:
        C.PJ = nc.dram_tensor("PJ", [NCH * 128, T], F32, kind="ExternalOutput").ap()
        C.OM = nc.dram_tensor("OM", [D, T], BF16, kind="ExternalOutput").ap()
    else:
        C.PJ = dscr("PJ", [NCH * 128, T])
        C.OM = dscr("OM", [D, T], BF16)
    C.ROPE = dscr("ROPE", [NSEQ, 4, 128, S])
    C.pjb = [[Buf() for _ in range(NB)] for _ in range(NCH)]
    C.omb = [[Buf() for _ in range(NB)] for _ in range(16)]
    C.ropeb = [Buf() for _ in range(NSEQ)]
    hbuf = [Buf("HT%d" % i) for i in range(NB)]
    C.hbuf = hbuf
    wb_gu = [[dscr("wgu%d_%d" % (i, l), [D, 2 * DFF], BF16) for l in range(depth)] for i in range(2)]
    wb_dn = [[dscr("wdn%d_%d" % (i, l), [DFF, D], BF16) for l in range(depth)] for i in range(2)]
    wb_in = [dscr("win_%d" % l, [D, NIN], BF16) for l in range(depth)]
    wb_br = [dscr("wbr_%d" % l, [4 * 512, D], BF16) for l in range(depth)]
    wb_out = [dscr("wout_%d" % l, [D, D], BF16) for l in range(depth)]
    wb_uq = [dscr("wuq_%d" % l, [384, 768], BF16) for l in range(depth)]
    wb_ukv = [dscr("wukv_%d" % l, [128, 1024], BF16) for l in range(depth)]
    wbuf = {}

    class WB:
        def __init__(self, ncols):
            self.nb = (ncols + 2047) // 2048
            self.b = [Buf() for _ in range(self.nb)]

        def bufs(self, c0, ncols):
            return self.b[c0 // 2048:(c0 + ncols - 1) // 2048 + 1]

    def prep(dst, src, rows, cols, key, order=None):
        wb = wbuf.setdefault(key, WB(cols))
        cbs = order if order is not None else list(range(wb.nb))
        for cb in cbs:
            c0 = cb * 2048
            c1 = min(cols, c0 + 2048)
            for r0 in range(0, rows, 2048):
                r1 = min(rows, r0 + 2048)
                P.dma("pool", lambda e, r0=r0, r1=r1, c0=c0, c1=c1: e.dma_start(out=dst[r0:r1, c0:c1], in_=src[r0:r1, c0:c1]),
                      writes=[wb.b[cb]], grp="w")
        return wb

    def prep_layer(l, which):
        if which == "ffn1":
            prep(wb_gu[0][l], w_gu[0][l], D, 2 * DFF, "gu0_%d" % l, order=[0, 2, 3, 1, 4, 5])
            prep(wb_dn[0][l], w_dn[0][l], DFF, D, "dn0_%d" % l)
        elif which == "mix":
            prep(wb_in[l], w_in[l], D, NIN, "in_%d" % l)
            prep(wb_br[l], w_br[l], 2048, D, "br_%d" % l)
            prep(wb_out[l], w_out[l], D, D, "out_%d" % l)
            prep(wb_uq[l], w_uq[l], 384, 768, "uq_%d" % l)
            prep(wb_ukv[l], w_ukv[l], 128, 1024, "ukv_%d" % l)
        else:
            prep(wb_gu[1][l], w_gu[1][l], D, 2 * DFF, "gu1_%d" % l, order=[0, 2, 3, 1, 4, 5])
            prep(wb_dn[1][l], w_dn[1][l], DFF, D, "dn1_%d" % l)

    with contextlib.ExitStack() as root:
        def sbp(name, shape, dt):
            return root.enter_context(nc.sbuf_tensor(uname(name), list(shape), dt))

        cst = sbp("cst_sb", [128, CST_N], F32)
        pvec = sbp("pvec_sb", [128, NPV], F32)
        cstb = sbp("cstb", [128, CSTB_N], BF16)
        ps = root.enter_context(nc.psum_tensor("ps", [128, 8, 512], F32))
        C.ps = ps
        C.pb = [Buf("bank%d" % i) for i in range(8)]
        b_cst, b_pvec, b_cstb = Buf("cst"), Buf("pvec"), Buf("cstb")
        C.cst, C.pvec, C.cstb, C.b_cst, C.b_pvec, C.b_cstb = cst, pvec, cstb, b_cst, b_pvec, b_cstb
        P.dma("sp", lambda e: e.dma_start(out=cst[:], in_=cst_in[:]), writes=[b_cst])
        P.dma("sp", lambda e: e.dma_start(out=pvec[:], in_=pvec_in[:]), writes=[b_pvec])
        P.op("dve", lambda e: e.tensor_copy(out=cstb[:], in_=cst[:, 0:CSTB_N]), reads=[b_cst], writes=[b_cstb])
        C.ident = cst[:, CO["ident"]:CO["ident"] + 128]
        C.ones = cst[:, CO["ones"]:CO["ones"] + 128]
        C.identb = cstb[:, CO["ident"]:CO["ident"] + 128]
        C.onesb = cstb[:, CO["ones"]:CO["ones"] + 128]

        C.aux = sbp("aux_sb", [128, depth * AUX_N], F32)
        C.b_aux = Buf("aux")
        order = [(l, ph) for l in range(depth) for ph in ("ffn1", "mix", "ffn2") if ph in phases]
        fast = None
        if order and order[0][1] == "ffn1":
            l0 = order[0][0]
            wbuf["gu0_%d" % l0] = WB(2 * DFF)
            fast = (wb_gu[0][l0], w_gu[0][l0], D, 2 * DFF, wbuf["gu0_%d" % l0], [0, 2, 3, 1, 4, 5])
            prep(wb_dn[0][l0], w_dn[0][l0], DFF, D, "dn0_%d" % l0)
            if len(order) > 1:
                prep_layer(*order[1])
        else:
            for (l, ph) in order[:2]:
                prep_layer(l, ph)
        nxt = 2

        if "mix" in phases:
            setup_phase(C, pos_in, depth)
        prologue(C, x_in, fast)
        for oi, (l, ph) in enumerate(order):
            if ph == "ffn1":
                ffn_phase(C, wb_gu[0][l], wb_dn[0][l], wbuf["gu0_%d" % l], wbuf["dn0_%d" % l], l, 0)
            elif ph == "ffn2":
                ffn_phase(C, wb_gu[1][l], wb_dn[1][l], wbuf["gu1_%d" % l], wbuf["dn1_%d" % l], l, 2)
            else:
                sub = C.mix_sub
                if "proj" in sub:
                    proj_phase(C, l, wb_in[l], wbuf["in_%d" % l])
                if "diff" in sub:
                    diff_phase(C, l)
                if "gla" in sub:
                    gla_phase(C, l, w_g2[l])
                if "mla" in sub:
                    mla_phase(C, l, wb_uq[l], wbuf["uq_%d" % l], wb_ukv[l], wbuf["ukv_%d" % l])
                if "ssd" in sub:
                    ssd_phase(C, l)
                if "merge" in sub:
                    merge_phase(C, l, wb_in[l], wbuf["in_%d" % l], wb_br[l], wbuf["br_%d" % l], wb_out[l], wbuf["out_%d" % l])
            if nxt < len(order):
                prep_layer(*order[nxt])
                nxt += 1
        epilogue(C, y_out)
    return nc


def _build_consts():
    tabs = []
    off = {}

    def add(name, arr):
        arr = np.asarray(arr, np.float32)
        assert arr.shape[0] == 128
        off[name] = sum(t.shape[1] for t in tabs)
        tabs.append(arr)
    p = np.arange(128)[:, None]
    f = np.arange(512)[None, :]
    i128 = np.arange(128)[None, :]
    add("ident", np.eye(128))
    add("ones", np.ones((128, 128)))
    add("maskA", np.concatenate([(f - j * 128 - p >= 0) for j in range(4)], axis=1))
    add("maskG", ((p // 64) == (i128 // 64)) & (p <= i128))
    add("maskS", (p <= i128))
    pd = np.zeros((128, 128))
    for m in range(128):
        d = m % 64
        if d < 16:
            k = m + 8 if d < 8 else m - 8
            pd[k, m] = 1
    add("permD", pd)
    pm = np.zeros((128, 128))
    for m in range(64):
        k = m + 32 if m < 32 else m - 32
        pm[k, m] = 1
    add("permM", pm)
    nb = sum(t.shape[1] for t in tabs)
    invd = np.zeros((128, 1), np.float32)
    sgnd = np.zeros((128, 1), np.float32)
    fr_d = (np.float32(ROPE_THETA) ** (-np.arange(0, 16, 2, dtype=np.float32) / np.float32(16))).astype(np.float32)
    for m in range(128):
        d = m % 64
        if d < 16:
            invd[m, 0] = fr_d[d % 8]
            sgnd[m, 0] = -1.0 if d < 8 else 1.0
    invm = np.zeros((128, 1), np.float32)
    sgnm = np.zeros((128, 1), np.float32)
    fr_m = (np.float32(ROPE_THETA) ** (-np.arange(0, 64, 2, dtype=np.float32) / np.float32(64))).astype(np.float32)
    for m in range(64):
        invm[m, 0] = fr_m[m % 32]
        sgnm[m, 0] = -1.0 if m < 32 else 1.0
    add("invd", invd)
    add("sgnd", sgnd)
    add("invm", invm)
    add("sgnm", sgnm)
    selE = np.zeros((128, 4, 128))
    for pr in range(4):
        for m in range(128):
            selE[2 * pr + m // 64, pr, m] = 1
    add("selE", selE.reshape(128, 512))
    selH = np.zeros((128, 8, 128))
    for h in range(8):
        selH[h, h, :] = 1
    add("selH", selH.reshape(128, 1024))
    return np.concatenate(tabs, axis=1), off, nb


CST, CO, CSTB_N = _build_consts()
CST_N = CST.shape[1]

PV = {}
_o = 0
for _n, _w in [("ln_g", 48), ("ln_b", 48), ("b_gate", 64), ("subln_g", 1), ("gla_bg", 2), ("gla_ng", 1), ("mla_qg", 3),
               ("mla_kvg", 1), ("conv_w", 32), ("conv_b", 8), ("dt_bias", 1), ("a_log", 1), ("ssd_d", 4), ("ssd_ng", 4),
               ("lam", 256)]:
    PV[_n] = _o
    _o += _w
PV_PER_LAYER = _o


def pv_col(l, name, i=0):
    return l * PV_PER_LAYER + PV[name] + i


def prologue(C, x_in, fast=None):
    nc, P, ps, pb = C.nc, C.P, C.ps, C.pb
    with contextlib.ExitStack() as st:
        xt = [st.enter_context(nc.sbuf_tensor("pro_x%d" % i, [128, D], F32)) for i in range(2)]
        bx = [Buf(), Buf()]
        stg = st.enter_context(nc.sbuf_tensor("pro_stage", [128, KC, TB], F32))
        bstg = Buf()
        n = 0
        for blk in range(C.NB):
            for sub in range(TB // 128):
                t0 = blk * TB + sub * 128
                s = n % 2
                P.dma("sp", lambda e, s=s, t0=t0: e.dma_start(out=xt[s][:], in_=x_in[t0:t0 + 128, :]), writes=[bx[s]])
                for g in range(4):
                    bank = (n * 4 + g) % 8

                    def tr(e, s=s, g=g, bank=bank):
                        for j in range(4):
                            k = g * 4 + j
                            ins = e.transpose(out=ps[:, bank, j * 128:(j + 1) * 128], in_=xt[s][:, k * 128:(k + 1) * 128],
                                              identity=C.ident)
                        return ins
                    P.op("pe", tr, reads=[bx[s], C.b_cst], writes=[pb[bank]])
                    eng = "dve" if g % 2 == 0 else "act"
                    src = ps[:, bank, :].rearrange("p (j t) -> p j t", j=4)
                    dst = stg[:, g * 4:(g + 1) * 4, sub * 128:(sub + 1) * 128]
                    if eng == "dve":
                        P.op("dve", lambda e, src=src, dst=dst: e.tensor_copy(out=dst, in_=src), reads=[pb[bank]], writes=[bstg])
                    else:
                        P.op("act", lambda e, src=src, dst=dst: e.copy(out=dst, in_=src), reads=[pb[bank]], writes=[bstg])
                n += 1
            P.dma("sp", lambda e, blk=blk: e.dma_start(
                out=C.HT[:, blk * TB:(blk + 1) * TB].rearrange("(k p) t -> p k t", p=128), in_=stg[:]),
                reads=[bstg], writes=[C.hbuf[blk]])
        if fast is not None:
            dst, src, rows, cols, wb, order = fast
            s32 = [st.enter_context(nc.sbuf_tensor("fp_s32_%d" % i, [128, 2048], F32)) for i in range(3)]
            s16 = [st.enter_context(nc.sbuf_tensor("fp_s16_%d" % i, [128, 2048], BF16)) for i in range(3)]
            b32, b16 = [Buf() for _ in range(3)], [Buf() for _ in range(3)]
            jobs = []
            for cb in order:
                c0 = cb * 2048
                c1 = min(cols, c0 + 2048)
                for r0 in range(0, rows, 128):
                    jobs.append((cb, r0, c0, c1))

            def ld(i):
                cb, r0, c0, c1 = jobs[i]
                k = i % 3
                P.dma("sp", lambda e: e.dma_start(out=s32[k][:, 0:c1 - c0], in_=src[r0:r0 + 128, c0:c1]), writes=[b32[k]])

            def cs(i):
                cb, r0, c0, c1 = jobs[i]
                k = i % 3
                if i % 2 == 0:
                    P.op("dve", lambda e: e.tensor_copy(out=s16[k][:, 0:c1 - c0], in_=s32[k][:, 0:c1 - c0]), reads=[b32[k]], writes=[b16[k]])
                else:
                    P.op("act", lambda e: e.copy(out=s16[k][:, 0:c1 - c0], in_=s32[k][:, 0:c1 - c0]), reads=[b32[k]], writes=[b16[k]])
                P.dma("sp", lambda e: e.dma_start(out=dst[r0:r0 + 128, c0:c1], in_=s16[k][:, 0:c1 - c0]), reads=[b16[k]], writes=[wb.b[cb]])
            for i in range(min(2, len(jobs))):
                ld(i)
            for i in range(len(jobs)):
                if i + 2 < len(jobs):
                    ld(i + 2)
                cs(i)
        P.barrier()


def epilogue(C, y_out):
    nc, P, ps, pb = C.nc, C.P, C.ps, C.pb
    by = Buf("y")
    with contextlib.ExitStack() as st:
        hf = st.enter_context(nc.sbuf_tensor("epi_h", [128, KC, TB], F32))
        bh = Buf()
        yt = [st.enter_context(nc.sbuf_tensor("epi_y%d" % i, [128, D], F32)) for i in range(2)]
        byt = [Buf(), Buf()]
        n = 0
        for blk in range(C.NB):
            P.dma("sp", lambda e, blk=blk: e.dma_start(
                out=hf[:], in_=C.HT[:, blk * TB:(blk + 1) * TB].rearrange("(k p) t -> p k t", p=128)),
                reads=[C.hbuf[blk]], writes=[bh])
            for sub in range(TB // 128):
                t0 = blk * TB + sub * 128
                s = n % 2
                for g in range(4):
                    bank = (n * 4 + g) % 8

                    def tr(e, g=g, bank=bank, sub=sub):
                        for j in range(4):
                            k = g * 4 + j
                            ins = e.transpose(out=ps[:, bank, j * 128:(j + 1) * 128], in_=hf[:, k, sub * 128:(sub + 1) * 128],
                                              identity=C.ident)
                        return ins
                    P.op("pe", tr, reads=[bh, C.b_cst], writes=[pb[bank]])
                    dst = yt[s][:, g * 512:(g + 1) * 512]
                    src = ps[:, bank, :]
                    if g % 2 == 0:
                        P.op("dve", lambda e, src=src, dst=dst: e.tensor_copy(out=dst, in_=src), reads=[pb[bank]], writes=[byt[s]])
                    else:
                        P.op("act", lambda e, src=src, dst=dst: e.copy(out=dst, in_=src), reads=[pb[bank]], writes=[byt[s]])
                P.dma("sp", lambda e, s=s, t0=t0: e.dma_start(out=y_out[t0:t0 + 128, :], in_=yt[s][:]), reads=[byt[s]], writes=[by])
                n += 1
        P.wait_bufs("sp", [by])
        P.barrier()


def load_w(C, slot, bslot, w_bf, bw, r0, nk, c0, ncols):
    rd = bw.bufs(c0, ncols) if hasattr(bw, "bufs") else [bw]
    C.P.dma("sp", lambda e: e.dma_start(out=slot[:, 0:nk, 0:ncols],
                                        in_=w_bf[r0:r0 + nk * 128, c0:c0 + ncols].rearrange("(k p) c -> p k c", p=128)),
            reads=rd, writes=[bslot])


class WStream:
    def __init__(self, C, slots, bsl):
        self.C, self.slots, self.bsl = C, slots, bsl
        self.items = []

    def add(self, load_args, fn):
        self.items.append((load_args, fn))

    def run(self):
        n, nsl = len(self.items), len(self.slots)
        la = nsl - 1
        issued = 0
        for i in range(n):
            while issued < min(n, i + la + 1):
                sl = issued % nsl
                load_w(self.C, self.slots[sl], self.bsl[sl], *self.items[issued][0])
                issued += 1
            self.items[i][1](i % nsl)


def prefetch_hb(C, blk, hb, bhb, stage, bstage, cnt):
    P = C.P
    for h in range(8):
        sg = cnt[0] % 2
        cnt[0] += 1
        P.dma("sp", lambda e, h=h, sg=sg: e.dma_start(
            out=stage[sg][:], in_=C.HT[h * 256:(h + 1) * 256, blk * TB:(blk + 1) * TB].rearrange("(k p) t -> p k t", p=128)),
            reads=[C.hbuf[blk]], writes=[bstage[sg]])
        if h % 2 == 0:
            P.op("dve", lambda e, h=h, sg=sg: e.tensor_copy(out=hb[:, h * 2:(h + 1) * 2, :], in_=stage[sg][:]), reads=[bstage[sg]], writes=[bhb])
        else:
            P.op("act", lambda e, h=h, sg=sg: e.copy(out=hb[:, h * 2:(h + 1) * 2, :], in_=stage[sg][:]), reads=[bstage[sg]], writes=[bhb])


def ln_block(C, r, br, l, idx, blk, tmp):
    nc, P, ps, pb = C.nc, C.P, C.ps, C.pb
    sq, bsq, mean, msq, rstd, bst = tmp

    def msum(e):
        for c in range(KC):
            ins = e.matmul(ps[:, 6, :], lhsT=C.ones, rhs=r[:, c, :], start=(c == 0), stop=(c == KC - 1))
        return ins
    P.op("pe", msum, reads=[br, C.b_cst], writes=[pb[6]])
    for c in range(KC):
        s = c % 2
        P.op("act", lambda e, c=c, s=s: e.activation(out=sq[s][:], in_=r[:, c, :], func=AF.Square), reads=[br], writes=[bsq[s]])
        P.op("pe", lambda e, c=c, s=s: e.matmul(ps[:, 7, :], lhsT=C.onesb, rhs=sq[s][:], start=(c == 0), stop=(c == KC - 1)),
             reads=[bsq[s], C.b_cstb], writes=[pb[7]])
    P.op("dve", lambda e: e.tensor_scalar(out=mean[:], in0=ps[:, 6, :], scalar1=1.0 / D, scalar2=None, op0=ALU.mult),
         reads=[pb[6]], writes=[bst])
    P.op("dve", lambda e: e.tensor_tensor(out=msq[:], in0=mean[:], in1=mean[:], op=ALU.mult), reads=[bst], writes=[bst])
    P.op("dve", lambda e: e.scalar_tensor_tensor(out=msq[:], in0=ps[:, 7, :], scalar=1.0 / D, in1=msq[:], op0=ALU.mult, op1=ALU.subtract),
         reads=[pb[7], bst], writes=[bst])
    P.op("dve", lambda e: e.tensor_scalar(out=msq[:], in0=msq[:], scalar1=EPS, scalar2=None, op0=ALU.add), reads=[bst], writes=[bst])
    P.op("act", lambda e: e.activation(out=msq[:], in_=msq[:], func=AF.Sqrt), reads=[bst], writes=[bst])
    P.op("dve", lambda e: e.reciprocal(out=rstd[:], in_=msq[:]), reads=[bst], writes=[bst])
    for c in range(KC):
        P.op("dve", lambda e, c=c: e.tensor_tensor(out=r[:, c, :], in0=r[:, c, :], in1=mean[:], op=ALU.subtract), reads=[br, bst], writes=[br])
        P.op("dve", lambda e, c=c: e.tensor_tensor(out=r[:, c, :], in0=r[:, c, :], in1=rstd[:], op=ALU.mult), reads=[br, bst], writes=[br])
        gcol = pv_col(l, "ln_g", idx * 16 + c)
        bcol = pv_col(l, "ln_b", idx * 16 + c)
        P.op("act", lambda e, c=c, gcol=gcol, bcol=bcol: e.activation(
            out=r[:, c, :], in_=r[:, c, :], func=AF.Identity, scale=C.pvec[:, gcol:gcol + 1], bias=C.pvec[:, bcol:bcol + 1]),
            reads=[br, C.b_pvec], writes=[br])
    P.dma("sp", lambda e: e.dma_start(out=C.HT[:, blk * TB:(blk + 1) * TB].rearrange("(k p) t -> p k t", p=128), in_=r[:]),
          reads=[br], writes=[C.hbuf[blk]])


def ffn_phase(C, wgu, wdn, bgu, bdn, l, idx):
    nc, P, ps, pb = C.nc, C.P, C.ps, C.pb
    with contextlib.ExitStack() as st:
        def sb(name, shape, dt):
            return st.enter_context(nc.sbuf_tensor(uname(name), list(shape), dt))
        hf = sb("f_hf", [128, KC, TB], F32)
        hb = sb("f_hb", [128, KC, TB], BF16)
        stage = [sb("f_stg%d" % i, [128, 2, TB], F32) for i in range(2)]
        act = sb("f_act", [128, FCH, TB], BF16)
        sg = sb("f_sg", [128, 4, TB], BF16)
        NSL = 3
        slots = [sb("f_w%d" % i, [128, 16, 512], BF16) for i in range(NSL)]
        bsl = [Buf() for _ in range(NSL)]
        sq = [sb("f_sq%d" % i, [128, TB], BF16) for i in range(2)]
        bsq = [Buf(), Buf()]
        mean = sb("f_mean", [128, TB], F32)
        msq = sb("f_msq", [128, TB], F32)
        rstd = sb("f_rstd", [128, TB], F32)
        bhf, bst, bhb = Buf(), Buf(), Buf()
        bstage = [Buf(), Buf()]
        bsg = [Buf() for _ in range(4)]
        bactc = [Buf() for _ in range(FCH)]
        cnt = [0]
        W = WStream(C, slots, bsl)
        prefetch_hb(C, 0, hb, bhb, stage, bstage, cnt)
        for blk in range(C.NB):
            for g in range(FCH // 4):
                for part in range(2):
                    def step(s, g=g, part=part):
                        banks = [part * 4 + j for j in range(4)]

                        def mm(e):
                            for k in range(KC):
                                for j in range(4):
                                    ins = e.matmul(ps[:, banks[j], :], lhsT=slots[s][:, k, j * 128:(j + 1) * 128], rhs=hb[:, k, :],
                                                   start=(k == 0), stop=(k == KC - 1))
                            return ins
                        P.op("pe", mm, reads=[bsl[s], bhb], writes=[pb[b] for b in banks])
                        if part == 1:
                            for j in range(4):
                                P.op("act", lambda e, j=j: e.activation(out=sg[:, j, :], in_=ps[:, j, :], func=AF.Silu), reads=[pb[j]], writes=[bsg[j]])
                                P.op("dve", lambda e, j=j: e.scalar_tensor_tensor(
                                    out=act[:, g * 4 + j, :], in0=ps[:, 4 + j, :], scalar=0.5, in1=sg[:, j, :], op0=ALU.mult, op1=ALU.mult),
                                    reads=[pb[4 + j], bsg[j]], writes=[bactc[g * 4 + j]])
                    W.add((wgu, bgu, 0, 16, part * DFF + g * 512, 512), step)
            for cg in range(4):
                for q in range(4):
                    def step(s, cg=cg, q=q, blk=blk):
                        banks = [(cg % 2) * 4 + j for j in range(4)]
                        if cg == 0 and q == 0:
                            P.dma("sp", lambda e: e.dma_start(
                                out=hf[:], in_=C.HT[:, blk * TB:(blk + 1) * TB].rearrange("(k p) t -> p k t", p=128)),
                                reads=[C.hbuf[blk]], writes=[bhf])

                        def mm(e):
                            for kk in range(11):
                                k = q * 11 + kk
                                for j in range(4):
                                    ins = e.matmul(ps[:, banks[j], :], lhsT=slots[s][:, kk, j * 128:(j + 1) * 128], rhs=act[:, k, :],
                                                   start=(k == 0), stop=(k == FCH - 1))
                            return ins
                        P.op("pe", mm, reads=[bsl[s]] + bactc[q * 11:(q + 1) * 11], writes=[pb[b] for b in banks])
                        if cg == 0 and q == 1 and blk + 1 < C.NB:
                            prefetch_hb(C, blk + 1, hb, bhb, stage, bstage, cnt)
                        if q == 3:
                            for j in range(4):
                                c = cg * 4 + j
                                P.op("dve", lambda e, c=c, b=banks[j]: e.scalar_tensor_tensor(
                                    out=hf[:, c, :], in0=hf[:, c, :], scalar=ALPHA, in1=ps[:, b, :], op0=ALU.mult, op1=ALU.add),
                                    reads=[pb[banks[j]], bhf], writes=[bhf])
                            if cg == 3:
                                ln_block(C, hf, bhf, l, idx, blk, (sq, bsq, mean, msq, rstd, bst))
                    W.add((wdn, bdn, q * 11 * 128, 11, cg * 512, 512), step)
        W.run()
        P.barrier()


def _pvec_host(inputs, depth):
    pv = np.zeros((128, depth * PV_PER_LAYER), np.float32)

    def put(l, name, arr):
        arr = np.asarray(arr, np.float32)
        c = pv_col(l, name)
        pv[:arr.shape[0], c:c + arr.shape[1]] = arr
    for l in range(depth):
        put(l, "ln_g", inputs["ln_g"][l].reshape(3, 16, 128).transpose(2, 0, 1).reshape(128, 48))
        put(l, "ln_b", inputs["ln_b"][l].reshape(3, 16, 128).transpose(2, 0, 1).reshape(128, 48))
        put(l, "b_gate", inputs["b_gate"][l].reshape(64, 128).T)
        put(l, "subln_g", inputs["diff_subln_g"][l].reshape(128, 1))
        put(l, "gla_bg", inputs["gla_b_gate"][l].reshape(2, 128).T)
        put(l, "gla_ng", inputs["gla_norm_g"][l].reshape(128, 1))
        put(l, "mla_qg", inputs["mla_q_norm_g"][l].reshape(3, 128).T)
        put(l, "mla_kvg", inputs["mla_kv_norm_g"][l].reshape(128, 1))
        put(l, "conv_w", inputs["ssd_conv_w"][l].reshape(4, 8, 128).transpose(2, 0, 1).reshape(128, 32))
        put(l, "conv_b", inputs["ssd_conv_b"][l].reshape(8, 128).T)
        put(l, "dt_bias", inputs["ssd_dt_bias"][l].reshape(8, 1))
        put(l, "a_log", inputs["ssd_a_log"][l].reshape(8, 1))
        put(l, "ssd_d", np.repeat(inputs["ssd_d"][l], 64).reshape(4, 128).T)
        put(l, "ssd_ng", inputs["ssd_norm_g"][l].reshape(4, 128).T)
        put(l, "lam", np.broadcast_to(inputs["diff_lambda"][l].reshape(1, 256), (128, 256)))
    return pv


_NC_CACHE = {}


def run(inputs, S, NSEQ, depth, n_cores, phases=("ffn1", "mix", "ffn2"), trace=False, dbg=None,
        mix_sub=("proj", "diff", "gla", "mla", "ssd", "merge")):
    key = (S, NSEQ, depth, tuple(phases), dbg, tuple(mix_sub))
    if key not in _NC_CACHE:
        _NC_CACHE[key] = build_program(S, NSEQ, depth, phases=phases, dbg=dbg, mix_sub=mix_sub)
    nc = _NC_CACHE[key]
    pv = _pvec_host(inputs, depth)
    x = np.ascontiguousarray(inputs["x"], dtype=np.float32)
    pos = np.ascontiguousarray(inputs["positions"], dtype=np.int32)
    shared = {
        "ffn1_w_gu": inputs["ffn1_w_gu"], "ffn2_w_gu": inputs["ffn2_w_gu"],
        "ffn1_w_down": inputs["ffn1_w_down"], "ffn2_w_down": inputs["ffn2_w_down"],
        "w_in": inputs["w_in"], "w_branch": np.asarray(inputs["w_branch"]).reshape(depth, 2048, D), "w_out": inputs["w_out"],
        "mla_w_uq": inputs["mla_w_uq"], "mla_w_ukv": inputs["mla_w_ukv"], "gla_w_gate2": inputs["gla_w_gate2"],
        "pvec": pv, "cst": CST,
    }
    shared = {k: np.ascontiguousarray(v, dtype=np.float32) for k, v in shared.items()}
    in_maps = []
    for c in range(n_cores):
        m = dict(shared)
        m["x"] = x[c * NSEQ:(c + 1) * NSEQ].reshape(NSEQ * S, D)
        m["positions"] = pos[c * NSEQ:(c + 1) * NSEQ]
        in_maps.append(m)
    res = run_bass_kernel_spmd(nc, in_maps, core_ids=list(range(n_cores)), trace=trace)
    out = np.concatenate([r["y"].reshape(NSEQ, S, D) for r in res.results], axis=0)
    return out, res


def kernel(**inputs):
    out, _ = run(inputs, 2048, 2, DEPTH, 8)
    return out.astype(np.float32)


import os
GLA_STOP = int(os.environ.get('GLA_STOP', '9'))
AUX_N = 4
TWO_PI = 2.0 * math.pi
CW1 = 6.28125
CW2 = TWO_PI - CW1


def setup_phase(C, pos_in, depth):
    nc, P = C.nc, C.P
    S = C.S
    aux, baux = C.aux, C.b_aux
    pv = C.pvec
    with contextlib.ExitStack() as st:
        def sb(name, shape, dt):
            return st.enter_context(nc.sbuf_tensor(uname(name), list(shape), dt))
        t64 = sb("su_t64", [128, 64], F32)
        s1 = sb("su_s1", [128, 2], F32)
        bt = Buf()
        for l in range(depth):
            lc = pv_col(l, "lam")
            lam_init = 0.8 - 0.6 * math.exp(-0.3 * l)
            for i in range(2):
                P.op("dve", lambda e, i=i, lc=lc: e.tensor_tensor(out=t64[:], in0=pv[:, lc + i * 128:lc + i * 128 + 64],
                                                               in1=pv[:, lc + i * 128 + 64:lc + i * 128 + 128], op=ALU.mult),
                     reads=[C.b_pvec], writes=[bt])
                P.op("dve", lambda e, i=i: e.tensor_reduce(out=s1[:, i:i + 1], in_=t64[:], axis=mybir.AxisListType.X, op=ALU.add),
                     reads=[bt], writes=[bt])
            P.op("act", lambda e: e.activation(out=s1[:], in_=s1[:], func=AF.Exp), reads=[bt], writes=[bt])
            a0 = l * AUX_N
            P.op("dve", lambda e, a0=a0: e.tensor_tensor(out=aux[:, a0:a0 + 1], in0=s1[:, 1:2], in1=s1[:, 0:1], op=ALU.subtract),
                 reads=[bt], writes=[baux])
            P.op("dve", lambda e, a0=a0, li=lam_init: e.tensor_scalar(out=aux[:, a0:a0 + 1], in0=aux[:, a0:a0 + 1], scalar1=-li, scalar2=None, op0=ALU.add),
                 reads=[baux], writes=[baux])
            gc = pv_col(l, "gla_bg")
            P.op("dve", lambda e, a0=a0, gc=gc: e.tensor_scalar(out=aux[:, a0 + 1:a0 + 3], in0=pv[:, gc:gc + 2], scalar1=-1.0, scalar2=None, op0=ALU.mult),
                 reads=[C.b_pvec], writes=[baux])
            ac = pv_col(l, "a_log")
            P.op("act", lambda e, a0=a0, ac=ac: e.activation(out=aux[:, a0 + 3:a0 + 4], in_=pv[:, ac:ac + 1], func=AF.Exp),
                 reads=[C.b_pvec, baux], writes=[baux])
            P.op("dve", lambda e, a0=a0: e.tensor_scalar(out=aux[:, a0 + 3:a0 + 4], in0=aux[:, a0 + 3:a0 + 4], scalar1=-1.0, scalar2=None, op0=ALU.mult),
                 reads=[baux], writes=[baux])
        pi = sb("su_pi", [128, 1, S], I32)
        pf = sb("su_pf", [128, S], F32)
        ang = sb("su_ang", [128, S], F32)
        a = sb("su_a", [128, S], F32)
        ni = sb("su_ni", [128, S], I32)
        nf = sb("su_nf", [128, S], F32)
        m = sb("su_m", [128, S], F32)
        bw = Buf()
        for s in range(C.NSEQ):
            P.dma("sp", lambda e, s=s: e.dma_start(out=pi[:], in_=pos_in[s:s + 1, :].partition_broadcast(128)), writes=[bw])
            P.op("dve", lambda e: e.tensor_copy(out=pf[:], in_=pi[:, 0, :]), reads=[bw], writes=[bw])
            for ti, (inv, sgn) in enumerate((("invd", "sgnd"), ("invm", "sgnm"))):
                P.op("dve", lambda e, inv=inv: e.tensor_scalar(out=ang[:], in0=pf[:], scalar1=C.cst[:, CO[inv]:CO[inv] + 1], scalar2=None, op0=ALU.mult),
                     reads=[bw, C.b_cst], writes=[bw])
                for ki in range(2):
                    V = lambda fn: P.op("dve", fn, reads=[bw], writes=[bw])
                    V(lambda e, ki=ki: e.tensor_scalar(out=a[:], in0=ang[:], scalar1=(math.pi / 2 if ki == 0 else 0.0), scalar2=None, op0=ALU.add))
                    V(lambda e: e.tensor_scalar(out=ni[:], in0=a[:], scalar1=1.0 / TWO_PI, scalar2=None, op0=ALU.mult))
                    V(lambda e: e.tensor_copy(out=nf[:], in_=ni[:]))
                    V(lambda e: e.scalar_tensor_tensor(out=a[:], in0=nf[:], scalar=-CW1, in1=a[:], op0=ALU.mult, op1=ALU.add))
                    V(lambda e: e.scalar_tensor_tensor(out=a[:], in0=nf[:], scalar=-CW2, in1=a[:], op0=ALU.mult, op1=ALU.add))
                    V(lambda e: e.tensor_scalar(out=m[:], in0=a[:], scalar1=math.pi, scalar2=TWO_PI, op0=ALU.is_gt, op1=ALU.mult))
                    V(lambda e: e.tensor_tensor(out=a[:], in0=a[:], in1=m[:], op=ALU.subtract))
                    V(lambda e: e.tensor_scalar(out=m[:], in0=a[:], scalar1=-math.pi, scalar2=TWO_PI, op0=ALU.is_lt, op1=ALU.mult))
                    V(lambda e: e.tensor_tensor(out=a[:], in0=a[:], in1=m[:], op=ALU.add))
                    V(lambda e: e.tensor_scalar(out=a[:], in0=a[:], scalar1=3.1415925, scalar2=-3.1415925, op0=ALU.min, op1=ALU.max))
                    P.op("act", lambda e: e.activation(out=a[:], in_=a[:], func=AF.Sin), reads=[bw], writes=[bw])
                    if ki == 1:
                        V(lambda e, sgn=sgn: e.tensor_scalar(out=a[:], in0=a[:], scalar1=C.cst[:, CO[sgn]:CO[sgn] + 1], scalar2=None, op0=ALU.mult))
                    P.dma("sp", lambda e, s=s, ti=ti, ki=ki: e.dma_start(out=C.ROPE[s, ti * 2 + ki], in_=a[:]), reads=[bw], writes=[C.ropeb[s]])
        P.barrier()


def seq_blocks(C, s):
    n = C.S // TB
    return list(range(s * n, (s + 1) * n))


def proj_phase(C, l, wbin, bwin):
    nc, P, ps, pb = C.nc, C.P, C.ps, C.pb
    with contextlib.ExitStack() as st:
        def sb(name, shape, dt):
            return st.enter_context(nc.sbuf_tensor(uname(name), list(shape), dt))
        hb = [sb("p_hb%d" % i, [128, KC, TB], BF16) for i in range(2)]
        stage = [sb("p_stg%d" % i, [128, 2, TB], F32) for i in range(2)]
        slots = [sb("p_w%d" % i, [128, 16, 512], BF16) for i in range(3)]
        bsl = [Buf() for _ in range(3)]
        stg = [sb("p_st%d" % i, [128, 4, TB], F32) for i in range(3)]
        bstg = [Buf() for _ in range(3)]
        bhb, bstage = [Buf(), Buf()], [Buf(), Buf()]
        cnt = [0]
        gcnt = [0]
        W = WStream(C, slots, bsl)
        prefetch_hb(C, 0, hb[0], bhb[0], stage, bstage, cnt)
        for blk in range(C.NB):
            cur = blk % 2
            for gi, grp in enumerate(W_IN_GROUPS):
                chs = [W_IN_CHUNKS[CH_IDX[n]] for n in grp]
                c0 = chs[0][1]
                ncols = chs[-1][1] + chs[-1][2] - c0

                def step(s, gi=gi, chs=chs, c0=c0, blk=blk, cur=cur):
                    if gi == 2 and blk + 1 < C.NB:
                        prefetch_hb(C, blk + 1, hb[1 - cur], bhb[1 - cur], stage, bstage, cnt)
                    banks = [(gi % 2) * 4 + j for j in range(len(chs))]

                    def mm(e):
                        for k in range(KC):
                            for j, ch in enumerate(chs):
                                o = ch[1] - c0
                                ins = e.matmul(ps[0:ch[2], banks[j], :], lhsT=slots[s][:, k, o:o + ch[2]], rhs=hb[cur][:, k, :],
                                               start=(k == 0), stop=(k == KC - 1))
                        return ins
                    P.op("pe", mm, reads=[bsl[s], bhb[cur]], writes=[pb[b] for b in banks])
                    sg = gcnt[0] % 3
                    gcnt[0] += 1
                    for j, ch in enumerate(chs):
                        w = ch[2]
                        if j % 2 == 0:
                            P.op("dve", lambda e, j=j, w=w, b=banks[j]: e.tensor_copy(out=stg[sg][0:w, j, :], in_=ps[0:w, b, :]),
                                 reads=[pb[banks[j]]], writes=[bstg[sg]])
                        else:
                            P.op("act", lambda e, j=j, w=w, b=banks[j]: e.copy(out=stg[sg][0:w, j, :], in_=ps[0:w, b, :]),
                                 reads=[pb[banks[j]]], writes=[bstg[sg]])
                    full = all(ch[2] == 128 for ch in chs) and len(chs) == 4
                    if full:
                        ci0 = CH_IDX[chs[0][0]]
                        P.dma("sp", lambda e: e.dma_start(
                            out=C.PJ[ci0 * 128:(ci0 + 4) * 128, blk * TB:(blk + 1) * TB].rearrange("(j p) t -> p j t", p=128), in_=stg[sg][:]),
                            reads=[bstg[sg]], writes=[C.pjb[ci0 + j][blk] for j in range(4)])
                    else:
                        for j, ch in enumerate(chs):
                            w = ch[2]
                            ci = CH_IDX[ch[0]]
                            P.dma("sp", lambda e, j=j, w=w, ci=ci: e.dma_start(
                                out=C.PJ[ci * 128:ci * 128 + w, blk * TB:(blk + 1) * TB], in_=stg[sg][0:w, j, :]),
                                reads=[bstg[sg]], writes=[C.pjb[ci][blk]])
                W.add((wbin, bwin, 0, 16, c0, ncols), step)
        W.run()
        P.barrier()


def pj_load(C, dst, bdst, name, s, rows=128, t0=0, nt=None):
    S = C.S
    nt = S if nt is None else nt
    ci = CH_IDX[name]
    g0 = s * S + t0
    blks = sorted(set(range(g0 // TB, (g0 + nt - 1) // TB + 1)))
    C.P.dma("sp", lambda e: e.dma_start(out=dst, in_=C.PJ[ci * 128:ci * 128 + rows, g0:g0 + nt]),
            reads=[C.pjb[ci][b] for b in blks], writes=[bdst])


def om_store(C, src, bsrc, row0, s, t0, nt, rows=128):
    g0 = s * C.S + t0
    blk = g0 // TB
    C.P.dma("sp", lambda e: e.dma_start(out=C.OM[row0:row0 + rows, g0:g0 + nt], in_=src),
            reads=[bsrc], writes=[C.omb[row0 // 128][blk]])


def attn_core(C, T_, qk_parts, v_tok, bv, scale, qb, rl, brl, obank, lbank, reads):
    P, ps, pb = C.P, C.ps, C.pb
    QB = min(512, C.S)
    nkc = (qb + 1) * QB // 128
    ndiag = QB // 128
    pT, bpT = T_["pT"], T_["bpT"]
    NPT = len(pT)
    LOOK = 2

    def front(kc):
        sb_ = kc % 3
        sl = kc % NPT

        def qk(e):
            for i, (qT, kT) in enumerate(qk_parts):
                ins = e.matmul(ps[:, sb_, 0:QB], lhsT=kT[:, kc * 128:(kc + 1) * 128], rhs=qT[:, qb * QB:(qb + 1) * QB],
                               start=(i == 0), stop=(i == len(qk_parts) - 1))
            return ins
        P.op("pe", qk, reads=reads, writes=[pb[sb_]])
        P.op("act", lambda e: e.activation(out=pT[sl][:, 0:QB], in_=ps[:, sb_, 0:QB], func=AF.Exp, scale=scale),
             reads=[pb[sb_]], writes=[bpT[sl]])
        dj = kc - (nkc - ndiag)
        if dj >= 0:
            mo = CO["maskA"] + dj * 512
            P.op("dve", lambda e: e.tensor_tensor(out=pT[sl][:, 0:QB], in0=pT[sl][:, 0:QB], in1=C.cstb[:, mo:mo + QB], op=ALU.mult),
                 reads=[bpT[sl], C.b_cstb], writes=[bpT[sl]])

    def back(kc):
        sl = kc % NPT

        def pv_(e):
            e.matmul(ps[:, obank, 0:QB], lhsT=v_tok[:, kc, :], rhs=pT[sl][:, 0:QB], start=(kc == 0), stop=(kc == nkc - 1))
            return e.matmul(ps[:, lbank, 0:QB], lhsT=C.onesb, rhs=pT[sl][:, 0:QB], start=(kc == 0), stop=(kc == nkc - 1))
        P.op("pe", pv_, reads=[bpT[sl], bv, C.b_cstb], writes=[pb[obank], pb[lbank]])

    for i in range(nkc + LOOK):
        if i < nkc:
            front(i)
        if i - LOOK >= 0:
            back(i - LOOK)
    P.op("dve", lambda e: e.reciprocal(out=rl[:, 0:QB], in_=ps[:, lbank, 0:QB]), reads=[pb[lbank]], writes=[brl])


def transpose_to_tok(C, srcT, bsrc, dst, bdst, S, bank0=2):
    P, ps, pb = C.P, C.ps, C.pb
    n = S // 128
    for g in range(0, n, 4):
        bank = bank0 + (g // 4) % 2
        cnt = min(4, n - g)
        def tr(e, g=g, bank=bank, cnt=cnt):
            for j in range(cnt):
                ins = e.transpose(out=ps[:, bank, j * 128:(j + 1) * 128], in_=srcT[:, (g + j) * 128:(g + j + 1) * 128], identity=C.ident)
            return ins
        P.op("pe", tr, reads=[bsrc, C.b_cst], writes=[pb[bank]])
        P.op("act", lambda e, g=g, bank=bank, cnt=cnt: e.copy(out=dst[:, g:g + cnt, :], in_=ps[:, bank, 0:cnt * 128].rearrange("p (j t) -> p j t", j=cnt)),
             reads=[pb[bank]], writes=[bdst])


def rope_fm(C, x, bx, xb, bxb, cosT, sinT, brope, perm, kp, out, bout, tmp, btmp, S, bank=3):
    P, ps, pb = C.P, C.ps, C.pb
    P.op("act", lambda e: e.copy(out=xb[0:kp, :], in_=x[0:kp, :]), reads=[bx], writes=[bxb])
    for t0 in range(0, S, 512):
        n = min(512, S - t0)
        P.op("pe", lambda e, t0=t0, n=n: e.matmul(ps[0:kp, bank, 0:n], lhsT=perm[0:kp, 0:kp], rhs=xb[0:kp, t0:t0 + n], start=True, stop=True),
             reads=[bxb, C.b_cstb], writes=[pb[bank]])
        P.op("dve", lambda e, t0=t0, n=n: e.tensor_tensor(out=tmp[0:kp, 0:n], in0=ps[0:kp, bank, 0:n], in1=sinT[0:kp, t0:t0 + n], op=ALU.mult),
             reads=[pb[bank], brope], writes=[btmp])
        P.op("dve", lambda e, t0=t0, n=n: e.tensor_tensor(out=x[0:kp, t0:t0 + n], in0=x[0:kp, t0:t0 + n], in1=cosT[0:kp, t0:t0 + n], op=ALU.mult),
             reads=[bx, brope], writes=[bx])
        P.op("dve", lambda e, t0=t0, n=n: e.tensor_tensor(out=out[0:kp, t0:t0 + n], in0=x[0:kp, t0:t0 + n], in1=tmp[0:kp, 0:n], op=ALU.add),
             reads=[bx, btmp], writes=[bout])


def colnorm_rstd(C, sq_aps, bsq, n_feat, rstd, brstd, n, bank=7):
    P, ps, pb = C.P, C.ps, C.pb
    def mm(e):
        for i, a in enumerate(sq_aps):
            kp = a.shape[0]
            ins = e.matmul(ps[:, bank, 0:n], lhsT=C.ones[0:kp, :], rhs=a, start=(i == 0), stop=(i == len(sq_aps) - 1))
        return ins
    P.op("pe", mm, reads=[bsq, C.b_cst], writes=[pb[bank]])
    P.op("dve", lambda e: e.tensor_scalar(out=rstd[:, 0:n], in0=ps[:, bank, 0:n], scalar1=1.0 / n_feat, scalar2=EPS, op0=ALU.mult, op1=ALU.add),
         reads=[pb[bank]], writes=[brstd])
    P.op("act", lambda e: e.activation(out=rstd[:, 0:n], in_=rstd[:, 0:n], func=AF.Sqrt), reads=[brstd], writes=[brstd])
    P.op("dve", lambda e: e.reciprocal(out=rstd[:, 0:n], in_=rstd[:, 0:n]), reads=[brstd], writes=[brstd])


def diff_phase(C, l):
    nc, P, ps, pb = C.nc, C.P, C.ps, C.pb
    S = C.S
    QB = min(512, S)
    lam_init = 0.8 - 0.6 * math.exp(-0.3 * l)
    with contextlib.ExitStack() as st:
        def sb(name, shape, dt):
            return st.enter_context(nc.sbuf_tensor(uname(name), list(shape), dt))
        cosT = sb("a_cos", [128, S], F32)
        sinT = sb("a_sin", [128, S], F32)
        brope = Buf()
        sets = []
        for i in range(2):
            d = dict(qf=sb("a_qf", [128, S], F32), kf=sb("a_kf", [128, S], F32), vf=sb("a_vf", [128, S], F32), xb=sb("a_xb", [128, S], BF16),
                     qb=sb("a_qb", [128, S], BF16), kb=sb("a_kb", [128, S], BF16), kbm=[sb("a_kbm", [128, S], BF16) for _ in range(2)],
                     vt=sb("a_vt", [128, S // 128, 128], BF16), tmp=sb("a_tmp", [128, 512], F32))
            for k in ("bq", "bk", "bvf", "bxb", "bqb", "bkb", "bvt", "btmp"):
                d[k] = Buf()
            d["bkbm"] = [Buf(), Buf()]
            for j in range(2):
                P.op("dve", lambda e, d=d, j=j: e.memset(d["kbm"][j][:], 0.0), writes=[d["bkbm"][j]])
            sets.append(d)
        T_ = {"pT": [sb("a_pT%d" % i, [128, 512], BF16) for i in range(4)], "bpT": [Buf() for _ in range(4)]}
        rl = [sb("a_rl%d" % i, [128, 512], F32) for i in range(2)]
        t0_, t1_ = sb("a_t0", [128, 512], F32), sb("a_t1", [128, 512], F32)
        sq = sb("a_sq", [128, 512], F32)
        rstd = sb("a_rstd", [128, 512], F32)
        ob = sb("a_ob", [128, 512], BF16)
        brl, bt0, bt1, bsq, brstd, bob = [Buf(), Buf()], Buf(), Buf(), Buf(), Buf(), Buf()
        permD = C.cstb[:, CO["permD"]:CO["permD"] + 128]
        hc = 0

        def prep(s, h, d):
            pj_load(C, d["qf"][:], d["bq"], "aq%d" % h, s)
            pj_load(C, d["kf"][:], d["bk"], "ak%d" % h, s)
            pj_load(C, d["vf"][:], d["bvf"], "av%d" % h, s)
            rope_fm(C, d["qf"], d["bq"], d["xb"], d["bxb"], cosT, sinT, brope, permD, 128, d["qb"], d["bqb"], d["tmp"], d["btmp"], S)
            rope_fm(C, d["kf"], d["bk"], d["xb"], d["bxb"], cosT, sinT, brope, permD, 128, d["kb"], d["bkb"], d["tmp"], d["btmp"], S)
            transpose_to_tok(C, d["vf"], d["bvf"], d["vt"], d["bvt"], S)
            P.op("act", lambda e: e.copy(out=d["kbm"][0][0:64, :], in_=d["kb"][0:64, :]), reads=[d["bkb"]], writes=[d["bkbm"][0]])
            P.op("dve", lambda e: e.tensor_copy(out=d["kbm"][1][64:128, :], in_=d["kb"][64:128, :]), reads=[d["bkb"]], writes=[d["bkbm"][1]])

        work = [(s, h) for s in range(C.NSEQ) for h in range(4)]
        for wi, (s, h) in enumerate(work):
            d = sets[wi % 2]
            if h == 0:
                P.dma("sp", lambda e, s=s: e.dma_start(out=cosT[:], in_=C.ROPE[s, 0]), reads=[C.ropeb[s]], writes=[brope])
                P.dma("sp", lambda e, s=s: e.dma_start(out=sinT[:], in_=C.ROPE[s, 1]), reads=[C.ropeb[s]], writes=[brope])
            if wi == 0 or h == 0:
                prep(s, h, d)
            if wi + 1 < len(work) and work[wi + 1][0] == s:
                prep(s, work[wi + 1][1], sets[(wi + 1) % 2])
            for qb in range(S // QB):
                for c in range(2):
                    attn_core(C, T_, [(d["qb"][:, :], d["kbm"][c][:, :])], d["vt"], d["bvt"], 0.125, qb,
                              rl[c], brl[c], 4 + c, 6 + c, [d["bqb"], d["bkbm"][c]])
                P.op("dve", lambda e: e.tensor_tensor(out=t0_[:, 0:QB], in0=ps[:, 4, 0:QB], in1=rl[0][:, 0:QB], op=ALU.mult),
                     reads=[pb[4], brl[0]], writes=[bt0])
                P.op("dve", lambda e: e.tensor_tensor(out=t1_[:, 0:QB], in0=ps[:, 5, 0:QB], in1=rl[1][:, 0:QB], op=ALU.mult),
                     reads=[pb[5], brl[1]], writes=[bt1])
                nl = l * AUX_N
                P.op("dve", lambda e, nl=nl: e.scalar_tensor_tensor(out=t0_[:, 0:QB], in0=t1_[:, 0:QB], scalar=C.aux[:, nl:nl + 1], in1=t0_[:, 0:QB],
                                                                  op0=ALU.mult, op1=ALU.add), reads=[bt0, bt1, C.b_aux], writes=[bt0])
                P.op("act", lambda e: e.activation(out=sq[:, 0:QB], in_=t0_[:, 0:QB], func=AF.Square), reads=[bt0], writes=[bsq])
                colnorm_rstd(C, [sq[:, 0:QB]], bsq, 128.0, rstd, brstd, QB, bank=3)
                P.op("dve", lambda e: e.tensor_scalar(out=rstd[:, 0:QB], in0=rstd[:, 0:QB], scalar1=1.0 - lam_init, scalar2=None, op0=ALU.mult), reads=[brstd], writes=[brstd])
                gc = pv_col(l, "subln_g")
                P.op("dve", lambda e, gc=gc: e.scalar_tensor_tensor(out=ob[:, 0:QB], in0=t0_[:, 0:QB], scalar=C.pvec[:, gc:gc + 1], in1=rstd[:, 0:QB],
                                                                  op0=ALU.mult, op1=ALU.mult), reads=[bt0, brstd, C.b_pvec], writes=[bob])
                om_store(C, ob[:, 0:QB], bob, h * 128, s, qb * QB, QB)
        P.barrier()


def mla_phase(C, l, wbuq, bwuq, wbukv, bwukv):
    nc, P, ps, pb = C.nc, C.P, C.ps, C.pb
    S = C.S
    QB = min(512, S)
    NT = S // 128
    with contextlib.ExitStack() as st:
        def sb(name, shape, dt):
            return st.enter_context(nc.sbuf_tensor(uname(name), list(shape), dt))
        cosT, sinT = sb("c_cos", [128, S], F32), sb("c_sin", [128, S], F32)
        brope = Buf()
        wq = sb("c_wq", [128, 3, 768], BF16)
        wkv = sb("c_wkv", [128, 1024], BF16)
        bwq, bwkv = Buf(), Buf()
        cq = sb("c_cq", [128, 3, S], F32)
        ckv = sb("c_ckv", [128, S], F32)
        ckr = sb("c_ckr", [128, S], F32)
        cqn = sb("c_cqn", [128, 3, S], BF16)
        ckvn = sb("c_ckvn", [128, S], BF16)
        krb = sb("c_krb", [128, S], BF16)
        xb = sb("c_xb", [128, S], BF16)
        sq3 = sb("c_sq3", [128, 3, 512], F32)
        rstd = sb("c_rstd", [128, 512], F32)
        tmp = sb("c_tmp", [128, 512], F32)
        vt = sb("c_vt", [128, NT, 4, 128], BF16)
        T_ = {"pT": [sb("c_pT%d" % i, [128, 512], BF16) for i in range(4)], "bpT": [Buf() for _ in range(4)]}
        rl = sb("c_rl", [128, 512], F32)
        ob = sb("c_ob", [128, 512], BF16)
        bcq, bckv, bckr, bcqn, bckvn, bkrb, bxb, bsq, brstd, btmp, bvt, bknb, bqnb, bqrb, bqrf, brl, bob = [Buf() for _ in range(17)]
        P.dma("sp", lambda e: e.dma_start(out=wq[:], in_=wbuq.rearrange("(k p) c -> p k c", p=128)), reads=bwuq.b, writes=[bwq])
        P.dma("sp", lambda e: e.dma_start(out=wkv[:], in_=wbukv), reads=bwukv.b, writes=[bwkv])
        permM = C.cstb[:, CO["permM"]:CO["permM"] + 128]
        P.op("dve", lambda e: e.memset(krb[:], 0.0), writes=[bkrb])
        hsets = []
        for i in range(2):
            d = dict(knb=sb("c_knb2", [128, S], BF16), qnb=sb("c_qnb2", [128, S], BF16), qrb=sb("c_qrb2", [128, S], BF16), qrf=sb("c_qrf2", [128, S], F32),
                     bknb=Buf(), bqnb=Buf(), bqrb=Buf(), bqrf=Buf())
            P.op("dve", lambda e, d=d: e.memset(d["qrb"][:], 0.0), writes=[d["bqrb"]])
            hsets.append(d)
        for s in range(C.NSEQ):
            P.dma("sp", lambda e, s=s: e.dma_start(out=cosT[:], in_=C.ROPE[s, 2]), reads=[C.ropeb[s]], writes=[brope])
            P.dma("sp", lambda e, s=s: e.dma_start(out=sinT[:], in_=C.ROPE[s, 3]), reads=[C.ropeb[s]], writes=[brope])
            for c in range(3):
                pj_load(C, cq[:, c, :], bcq, "cq%d" % c, s)
            pj_load(C, ckv[:], bckv, "ckv0", s)
            pj_load(C, ckr[0:64, :], bckr, "ckr0", s, rows=64)
            for t0 in range(0, S, 512):
                n = min(512, S - t0)
                for c in range(3):
                    P.op("act", lambda e, c=c, t0=t0, n=n: e.activation(out=sq3[:, c, 0:n], in_=cq[:, c, t0:t0 + n], func=AF.Square), reads=[bcq], writes=[bsq])
                colnorm_rstd(C, [sq3[:, c, 0:n] for c in range(3)], bsq, 384.0, rstd, brstd, n)
                for c in range(3):
                    gc = pv_col(l, "mla_qg", c)
                    P.op("dve", lambda e, c=c, t0=t0, n=n: e.tensor_tensor(out=tmp[:, 0:n], in0=cq[:, c, t0:t0 + n], in1=rstd[:, 0:n], op=ALU.mult),
                         reads=[bcq, brstd], writes=[btmp])
                    P.op("dve", lambda e, c=c, t0=t0, n=n, gc=gc: e.tensor_scalar(out=cqn[:, c, t0:t0 + n], in0=tmp[:, 0:n], scalar1=C.pvec[:, gc:gc + 1], scalar2=None, op0=ALU.mult),
                         reads=[btmp, C.b_pvec], writes=[bcqn])
                P.op("act", lambda e, t0=t0, n=n: e.activation(out=sq3[:, 0, 0:n], in_=ckv[:, t0:t0 + n], func=AF.Square), reads=[bckv], writes=[bsq])
                colnorm_rstd(C, [sq3[:, 0, 0:n]], bsq, 128.0, rstd, brstd, n)
                gc = pv_col(l, "mla_kvg")
                P.op("dve", lambda e, t0=t0, n=n: e.tensor_tensor(out=tmp[:, 0:n], in0=ckv[:, t0:t0 + n], in1=rstd[:, 0:n], op=ALU.mult), reads=[bckv, brstd], writes=[btmp])
                P.op("dve", lambda e, t0=t0, n=n, gc=gc: e.tensor_scalar(out=ckvn[:, t0:t0 + n], in0=tmp[:, 0:n], scalar1=C.pvec[:, gc:gc + 1], scalar2=None, op0=ALU.mult),
                     reads=[btmp, C.b_pvec], writes=[bckvn])
            rope_fm(C, ckr, bckr, xb, bxb, cosT, sinT, brope, permM, 64, krb, bkrb, tmp, btmp, S)
            for tc in range(NT):
                bank = 2 + tc % 2
                def vm(e, tc=tc, bank=bank):
                    for h in range(4):
                        ins = e.matmul(ps[:, bank, h * 128:(h + 1) * 128], lhsT=ckvn[:, tc * 128:(tc + 1) * 128], rhs=wkv[:, h * 256 + 128:h * 256 + 256], start=True, stop=True)
                    return ins
                P.op("pe", vm, reads=[bckvn, bwkv], writes=[pb[bank]])
                P.op("act", lambda e, tc=tc, bank=bank: e.copy(out=vt[:, tc, :, :], in_=ps[:, bank, :].rearrange("p (h d) -> p h d", h=4)), reads=[pb[bank]], writes=[bvt])
            def prep(h, d):
                for t0 in range(0, S, 512):
                    n = min(512, S - t0)
                    P.op("pe", lambda e, t0=t0, n=n: e.matmul(ps[:, 2, 0:n], lhsT=wkv[:, h * 256:h * 256 + 128], rhs=ckvn[:, t0:t0 + n], start=True, stop=True),
                         reads=[bckvn, bwkv], writes=[pb[2]])
                    P.op("act", lambda e, t0=t0, n=n: e.copy(out=d["knb"][:, t0:t0 + n], in_=ps[:, 2, 0:n]), reads=[pb[2]], writes=[d["bknb"]])

                    def qn(e, t0=t0, n=n):
                        for c in range(3):
                            ins = e.matmul(ps[:, 3, 0:n], lhsT=wq[:, c, h * 192:h * 192 + 128], rhs=cqn[:, c, t0:t0 + n], start=(c == 0), stop=(c == 2))
                        return ins
                    P.op("pe", qn, reads=[bcqn, bwq], writes=[pb[3]])
                    P.op("dve", lambda e, t0=t0, n=n: e.tensor_copy(out=d["qnb"][:, t0:t0 + n], in_=ps[:, 3, 0:n]), reads=[pb[3]], writes=[d["bqnb"]])

                    def qr(e, t0=t0, n=n):
                        for c in range(3):
                            ins = e.matmul(ps[0:64, 2, 0:n], lhsT=wq[:, c, h * 192 + 128:h * 192 + 192], rhs=cqn[:, c, t0:t0 + n], start=(c == 0), stop=(c == 2))
                        return ins
                    P.op("pe", qr, reads=[bcqn, bwq], writes=[pb[2]])
                    P.op("act", lambda e, t0=t0, n=n: e.copy(out=d["qrf"][0:64, t0:t0 + n], in_=ps[0:64, 2, 0:n]), reads=[pb[2]], writes=[d["bqrf"]])
                rope_fm(C, d["qrf"], d["bqrf"], xb, bxb, cosT, sinT, brope, permM, 64, d["qrb"], d["bqrb"], tmp, btmp, S)

            prep(0, hsets[0])
            for h in range(4):
                d = hsets[h % 2]
                if h + 1 < 4:
                    prep(h + 1, hsets[(h + 1) % 2])
                for qb in range(S // QB):
                    attn_core(C, T_, [(d["qnb"][:, :], d["knb"][:, :]), (d["qrb"][:, :], krb[:, :])], vt[:, :, h, :], bvt, 192.0 ** -0.5, qb,
                              rl, brl, 4, 6, [d["bqnb"], d["bknb"], d["bqrb"], bkrb])
                    P.op("dve", lambda e: e.tensor_tensor(out=ob[:, 0:QB], in0=ps[:, 4, 0:QB], in1=rl[:, 0:QB], op=ALU.mult), reads=[pb[4], brl], writes=[bob])
                    om_store(C, ob[:, 0:QB], bob, 1024 + h * 128, s, qb * QB, QB)
        P.barrier()


def gla_phase(C, l, w_g2l):
    nc, P, ps, pb = C.nc, C.P, C.ps, C.pb
    S = C.S
    NCk = S // 64
    NBk = S // 128
    with contextlib.ExitStack() as st:
        def sb(name, shape, dt):
            return st.enter_context(nc.sbuf_tensor(uname(name), list(shape), dt))
        rm = sb("g_rm", [128, S], F32)
        wg2f, wg2 = sb("g_w2f", [16, 256], F32), sb("g_w2", [16, 256], BF16)
        glf, glb = sb("g_glf", [16, S], F32), sb("g_glb", [16, S], BF16)
        qf, kf = sb("g_qf", [128, S], F32), sb("g_kf", [128, S], F32)
        g_, b_ = sb("g_g", [128, S], F32), sb("g_b", [128, S], F32)
        eb, enb = sb("g_eb", [128, S], F32), sb("g_enb", [128, S], F32)
        kd = sb("g_kd", [128, S], F32)
        qt, kt = sb("g_qt", [128, S], BF16), sb("g_kt", [128, S], BF16)
        kd_tok = sb("g_kdt", [128, NBk, 128], BF16)
        vf = sb("g_vf", [128, S], F32)
        vtok = [sb("g_vt%d" % i, [128, NBk, 128], BF16) for i in range(2)]
        vpar = [[sb("g_vp%d%d" % (i, j), [128, NBk, 128], BF16) for j in range(2)] for i in range(2)]
        qtm = [sb("g_qtm%d" % i, [128, S], BF16) for i in range(2)]
        bqtm = [Buf(), Buf()]
        bvpar = [Buf(), Buf()]
        rf = sb("g_rf", [128, S], F32)
        Sall = sb("g_Sall", [128, NCk, 128], F32)
        Sbf = sb("g_Sbf", [128, NCk, 128], BF16)
        att = [sb("g_att%d" % i, [128, 4, 128], BF16) for i in range(2)]
        osb, sq, rstd, sr = sb("g_osb", [128, 512], F32), sb("g_sq", [128, 512], F32), sb("g_rstd", [128, 512], F32), sb("g_sr", [128, 512], F32)
        ob = sb("g_ob", [128, 512], BF16)
        brm, bw2, bgl, bq, bk, bg, bb, beb, benb, bkd, bqt, bkt, bkdt, bvf, brf, bS, bSb, bosb, bsq, brstd, bsr, bob = [Buf() for _ in range(22)]
        bvt = [Buf(), Buf()]
        batt = [Buf(), Buf()]
        P.op("dve", lambda e: e.memset(rm[:], 1.0), writes=[brm])
        for i in range(2):
            P.op("dve", lambda e, i=i: e.memset(qtm[i][:], 0.0), writes=[bqtm[i]])
            for j in range(2):
                P.op("dve", lambda e, i=i, j=j: e.memset(vpar[i][j][:], 0.0), writes=[bvpar[i]])
        P.op("dve", lambda e: e.memset(rm[:].rearrange("p (n c) -> p n c", c=64)[:, :, 0:1], 0.0), writes=[brm])
        P.dma("sp", lambda e: e.dma_start(out=wg2f[:], in_=w_g2l), writes=[bw2])
        P.op("dve", lambda e: e.tensor_copy(out=wg2[:], in_=wg2f[:]), reads=[bw2], writes=[bw2])
        onecol = C.cst[:, CO["ones"]:CO["ones"] + 1]
        maskG = C.cstb[:, CO["maskG"]:CO["maskG"] + 128]
        ai = 0
        for s in range(C.NSEQ):
            pj_load(C, glf[0:16, :], bgl, "bg0", s, rows=16)
            P.op("dve", lambda e: e.tensor_copy(out=glb[:], in_=glf[:]), reads=[bgl], writes=[bgl])
            for hp in range(2):
                pj_load(C, qf[:], bq, "bq%d" % hp, s)
                pj_load(C, kf[:], bk, "bk%d" % hp, s)
                nb = l * AUX_N + 1 + hp
                for t0 in range(0, S, 512):
                    n = min(512, S - t0)
                    P.op("pe", lambda e, hp=hp, t0=t0, n=n: e.matmul(ps[:, 0, 0:n], lhsT=wg2[0:16, hp * 128:(hp + 1) * 128], rhs=glb[0:16, t0:t0 + n], start=True, stop=True),
                         reads=[bw2, bgl], writes=[pb[0]])
                    P.op("act", lambda e, t0=t0, n=n, nb=nb: e.activation(out=g_[:, t0:t0 + n], in_=ps[:, 0, 0:n], func=AF.Exp, scale=-1.0, bias=C.aux[:, nb:nb + 1]),
                         reads=[pb[0], C.b_aux], writes=[bg])
                P.op("act", lambda e: e.activation(out=g_[:], in_=g_[:], func=AF.Ln, bias=onecol, scale=1.0), reads=[bg, C.b_cst], writes=[bg])
                P.op("dve", lambda e: e.tensor_scalar(out=g_[:], in0=g_[:], scalar1=-1.0 / 16.0, scalar2=None, op0=ALU.mult), reads=[bg], writes=[bg])
                P.op("dve", lambda e: e.tensor_tensor_scan(out=b_[:], data0=rm[:], data1=g_[:], initial=0.0, op0=ALU.mult, op1=ALU.add), reads=[bg, brm], writes=[bb])
                P.op("act", lambda e: e.activation(out=eb[:], in_=b_[:], func=AF.Exp), reads=[bb], writes=[beb])
                P.op("act", lambda e: e.activation(out=enb[:], in_=b_[:], func=AF.Exp, scale=-1.0), reads=[bb], writes=[benb])
                P.op("dve", lambda e: e.scalar_tensor_tensor(out=qt[:], in0=qf[:], scalar=0.125, in1=eb[:], op0=ALU.mult, op1=ALU.mult), reads=[bq, beb], writes=[bqt])
                P.op("dve", lambda e: e.tensor_tensor(out=kt[:], in0=kf[:], in1=enb[:], op=ALU.mult), reads=[bk, benb], writes=[bkt])
                P.op("act", lambda e: e.copy(out=qtm[0][0:64, :], in_=qt[0:64, :]), reads=[bqt], writes=[bqtm[0]])
                P.op("act", lambda e: e.copy(out=qtm[1][64:128, :], in_=qt[64:128, :]), reads=[bqt], writes=[bqtm[1]])
                b3 = b_[:].rearrange("p (n c) -> p n c", c=64)
                P.op("dve", lambda e, b3=b3: e.tensor_tensor(out=kd[:].rearrange("p (n c) -> p n c", c=64), in0=b3[:, :, 63:64].broadcast_to([128, NCk, 64]), in1=b3, op=ALU.subtract),
                     reads=[bb], writes=[bkd])
                P.op("act", lambda e: e.activation(out=kd[:], in_=kd[:], func=AF.Exp), reads=[bkd], writes=[bkd])
                P.op("dve", lambda e: e.tensor_tensor(out=kd[:], in0=kd[:], in1=kf[:], op=ALU.mult), reads=[bkd, bk], writes=[bkd])
                if GLA_STOP <= 1:
                    continue
                transpose_to_tok(C, kd, bkd, kd_tok, bkdt, S)
                for hh in range(2):
                    pj_load(C, vf[:], bvf, "bv%d" % (hp * 2 + hh), s)
                    transpose_to_tok(C, vf, bvf, vtok[hh], bvt[hh], S)
                    P.op("dve", lambda e, hh=hh: e.tensor_copy(out=vpar[hh][0][0:64, :, :], in_=vtok[hh][0:64, :, :]), reads=[bvt[hh]], writes=[bvpar[hh]])
                    P.op("dve", lambda e, hh=hh: e.tensor_copy(out=vpar[hh][1][64:128, :, :], in_=vtok[hh][64:128, :, :]), reads=[bvt[hh]], writes=[bvpar[hh]])
                if GLA_STOP <= 2:
                    continue
                for cg in range(NCk // 4):
                    bank = 4 + cg % 2
                    def inc(e, cg=cg, bank=bank):
                        for c in range(4):
                            n = cg * 4 + c
                            m, par = n // 2, n % 2
                            for hh in range(2):
                                ins = e.matmul(ps[hh * 64:(hh + 1) * 64, bank, c * 128:(c + 1) * 128], lhsT=kd_tok[:, m, hh * 64:(hh + 1) * 64],
                                               rhs=vpar[hh][par][:, m, :], start=True, stop=True)
                        return ins
                    P.op("pe", inc, reads=[bkdt, bvpar[0], bvpar[1]], writes=[pb[bank]])
                    for c in range(4):
                        n = cg * 4 + c
                        if n == 0:
                            P.op("dve", lambda e, bank=bank: e.tensor_copy(out=Sall[:, 0, :], in_=ps[:, bank, 0:128]), reads=[pb[bank]], writes=[bS])
                        else:
                            P.op("dve", lambda e, n=n, c=c, bank=bank: e.scalar_tensor_tensor(
                                out=Sall[:, n, :], in0=Sall[:, n - 1, :], scalar=eb[:, n * 64 + 63:n * 64 + 64], in1=ps[:, bank, c * 128:(c + 1) * 128],
                                op0=ALU.mult, op1=ALU.add), reads=[pb[bank], bS, beb], writes=[bS])
                    P.op("act", lambda e, cg=cg: e.copy(out=Sbf[:, cg * 4:(cg + 1) * 4, :], in_=Sall[:, cg * 4:(cg + 1) * 4, :]), reads=[bS], writes=[bSb])
                if GLA_STOP <= 3:
                    continue
                for hh in range(2):
                    A = hp * 2 + hh
                    base = hh * 64
                    pj_load(C, rf[:], brf, "br%d" % A, s)
                    for g4 in range(0, NBk, 4):
                        cnt = min(4, NBk - g4)
                        W = cnt * 128
                        sl = ai % 2
                        ai += 1
                        def attm(e, g4=g4, cnt=cnt, hh=hh):
                            for j in range(cnt):
                                m = g4 + j
                                ins = e.matmul(ps[:, 0, j * 128:(j + 1) * 128], lhsT=kt[:, m * 128:(m + 1) * 128], rhs=qtm[hh][:, m * 128:(m + 1) * 128],
                                               start=True, stop=True)
                            return ins
                        P.op("pe", attm, reads=[bkt, bqtm[hh]], writes=[pb[0]])
                        P.op("dve", lambda e, sl=sl, cnt=cnt: e.tensor_tensor(out=att[sl][:, 0:cnt, :], in0=ps[:, 0, 0:cnt * 128].rearrange("p (j t) -> p j t", j=cnt),
                                                                           in1=maskG.unsqueeze(1).broadcast_to([128, cnt, 128]), op=ALU.mult),
                             reads=[pb[0], C.b_cstb], writes=[batt[sl]])
                        def om(e, g4=g4, cnt=cnt, base=base, hh=hh, sl=sl):
                            for j in range(cnt):
                                m = g4 + j
                                ins = e.matmul(ps[:, 1, j * 128:(j + 1) * 128], lhsT=vtok[hh][:, m, :], rhs=att[sl][:, j, :], start=True, stop=(m == 0))
                                for par in range(2):
                                    n = 2 * m + par
                                    if n > 0:
                                        ins = e.matmul(ps[:, 1, j * 128 + par * 64:j * 128 + par * 64 + 64], lhsT=Sbf[:, n - 1, :],
                                                       rhs=qtm[hh][:, n * 64:(n + 1) * 64], start=False, stop=True)
                            return ins
                        P.op("pe", om, reads=[bvt[hh], batt[sl], bSb, bqtm[hh]], writes=[pb[1]])
                        t0 = g4 * 128
                        P.op("act", lambda e, W=W: e.copy(out=osb[:, 0:W], in_=ps[:, 1, 0:W]), reads=[pb[1]], writes=[bosb])
                        P.op("act", lambda e, W=W: e.activation(out=sq[:, 0:W], in_=ps[:, 1, 0:W], func=AF.Square), reads=[pb[1]], writes=[bsq])
                        colnorm_rstd(C, [sq[:, 0:W]], bsq, 128.0, rstd, brstd, W)
                        P.op("dve", lambda e, W=W: e.tensor_tensor(out=osb[:, 0:W], in0=osb[:, 0:W], in1=rstd[:, 0:W], op=ALU.mult), reads=[bosb, brstd], writes=[bosb])
                        P.op("act", lambda e, W=W, t0=t0: e.activation(out=sr[:, 0:W], in_=rf[:, t0:t0 + W], func=AF.Silu), reads=[brf], writes=[bsr])
                        gc = pv_col(l, "gla_ng")
                        P.op("dve", lambda e, W=W, gc=gc: e.scalar_tensor_tensor(out=ob[:, 0:W], in0=osb[:, 0:W], scalar=C.pvec[:, gc:gc + 1], in1=sr[:, 0:W],
                                                                              op0=ALU.mult, op1=ALU.mult), reads=[bosb, bsr, C.b_pvec], writes=[bob])
                        om_store(C, ob[:, 0:W], bob, 512 + A * 128, s, t0, W)
        P.barrier()


def ssd_phase(C, l):
    nc, P, ps, pb = C.nc, C.P, C.ps, C.pb
    S = C.S
    NCk = S // 128
    cst = C.cst
    onecol = cst[:, CO["ones"]:CO["ones"] + 1]
    maskS = cst[:, CO["maskS"]:CO["maskS"] + 128]
    with contextlib.ExitStack() as st:
        def sb(name, shape, dt):
            return st.enter_context(nc.sbuf_tensor(uname(name), list(shape), dt))
        xs = [sb("d_xs%d" % i, [128, S], F32) for i in range(4)]
        Bb = [sb("d_Bb%d" % i, [128, S], BF16) for i in range(2)]
        Cb = [sb("d_Cb%d" % i, [128, S], BF16) for i in range(2)]
        Btok = sb("d_Btok", [128, NCk, 256], BF16)
        dt8, dA8, ac8, ec8, rm8 = [sb("d_s%d" % i, [8, S], F32) for i in range(5)]
        raw = [sb("d_raw0", [128, S + 3], F32)] * 2
        acc = sb("d_acc", [128, S], F32)
        xdt_tok = sb("d_xdtt", [128, NCk, 512], BF16)
        xdd = [sb("d_xdd%d" % i, [128, 512], BF16) for i in range(2)]
        acol, dcol = sb("d_acol", [128, NCk, 8], F32), sb("d_dcol", [128, NCk, 8], F32)
        R8 = sb("d_R8", [8, NCk, 8], F32)
        cdrow = sb("d_cdrow", [128, NCk, 8], F32)
        stt = [sb("d_st%d" % i, [128, 512], F32) for i in range(2)]
        prevb = sb("d_prevb", [128, NCk, 512], BF16)
        zt = [sb("d_zt%d" % i, [128, 4, 128], F32) for i in range(2)]
        cbm = [sb("d_cbm%d" % i, [128, 128], F32) for i in range(2)]
        Dm4 = [sb("d_Dm%d" % i, [128, 4, 128], F32) for i in range(2)]
        Mh4 = [sb("d_Mh%d" % i, [128, 4, 128], BF16) for i in range(2)]
        ecb4 = sb("d_ecb", [128, 4, 128], F32)
        yo = sb("d_yo", [128, 4, 128], F32)
        ysq = sb("d_ysq", [128, 4, 128], F32)
        rstd = sb("d_rstd", [128, 128], F32)
        ob = sb("d_ob", [128, 4, 128], BF16)
        bxs = [Buf() for _ in range(4)]
        bBb, bCb = [Buf(), Buf()], [Buf(), Buf()]
        bBtok, b8, braw, bacc, bxdtt, bcol, bR8, bcdrow, bprev, byo, bysq, brstd, bob = [Buf() for _ in range(13)]
        braw = [Buf()] * 2
        bxdd, bst, bzt, bcbm, bDm, bMh, becb = [[Buf(), Buf()] for _ in range(7)]
        P.op("dve", lambda e: e.memset(rm8[:], 1.0), writes=[b8])
        P.op("dve", lambda e: e.memset(rm8[:].rearrange("p (n c) -> p n c", c=128)[:, :, 0:1], 0.0), writes=[b8])
        for i in range(2):
            P.op("dve", lambda e, i=i: e.memset(raw[i][:, 0:3], 0.0), writes=[braw[i]])
        nega = C.aux[0:8, l * AUX_N + 3:l * AUX_N + 4]
        dtb = C.pvec[0:8, pv_col(l, "dt_bias"):pv_col(l, "dt_bias") + 1]
        ri = 0
        for s in range(C.NSEQ):
            for ti in range(8):
                r = ri % 2
                ri += 1
                pj_load(C, raw[r][:, 3:3 + S], braw[r], "dx%d" % ti, s)
                cw = pv_col(l, "conv_w")
                P.op("dve", lambda e, r=r, ti=ti, cw=cw: e.tensor_scalar(out=acc[:], in0=raw[r][:, 0:S], scalar1=C.pvec[:, cw + ti:cw + ti + 1], scalar2=None, op0=ALU.mult),
                     reads=[braw[r], C.b_pvec], writes=[bacc])
                for k in range(1, 4):
                    P.op("dve", lambda e, r=r, ti=ti, cw=cw, k=k: e.scalar_tensor_tensor(out=acc[:], in0=raw[r][:, k:k + S], scalar=C.pvec[:, cw + k * 8 + ti:cw + k * 8 + ti + 1],
                                                                                   in1=acc[:], op0=ALU.mult, op1=ALU.add), reads=[braw[r], bacc, C.b_pvec], writes=[bacc])
                cb_ = pv_col(l, "conv_b", ti)
                if ti < 4:
                    P.op("act", lambda e, ti=ti, cb_=cb_: e.activation(out=xs[ti][:], in_=acc[:], func=AF.Silu, bias=C.pvec[:, cb_:cb_ + 1], scale=1.0),
                         reads=[bacc, C.b_pvec], writes=[bxs[ti]])
                else:
                    P.op("act", lambda e, cb_=cb_: e.activation(out=acc[:], in_=acc[:], func=AF.Silu, bias=C.pvec[:, cb_:cb_ + 1], scale=1.0),
                         reads=[bacc, C.b_pvec], writes=[bacc])
                    g = (ti - 4) % 2
                    if ti < 6:
                        P.op("dve", lambda e, g=g: e.tensor_copy(out=Bb[g][:], in_=acc[:]), reads=[bacc], writes=[bBb[g]])
                        for c4 in range(0, NCk, 4):
                            cnt = min(4, NCk - c4)
                            bank = 2 + (c4 // 4) % 2
                            def tr(e, c4=c4, cnt=cnt, bank=bank):
                                for j in range(cnt):
                                    ins = e.transpose(out=ps[:, bank, j * 128:(j + 1) * 128], in_=acc[:, (c4 + j) * 128:(c4 + j + 1) * 128], identity=C.ident)
                                return ins
                            P.op("pe", tr, reads=[bacc, C.b_cst], writes=[pb[bank]])
                            P.op("act", lambda e, c4=c4, cnt=cnt, bank=bank, g=g: e.copy(out=Btok[:, c4:c4 + cnt, g * 128:(g + 1) * 128],
                                                                                     in_=ps[:, bank, 0:cnt * 128].rearrange("p (j t) -> p j t", j=cnt)),
                                 reads=[pb[bank]], writes=[bBtok])
                    else:
                        P.op("dve", lambda e, g=g: e.tensor_copy(out=Cb[g][:], in_=acc[:]), reads=[bacc], writes=[bCb[g]])
            pj_load(C, dt8[:], b8, "dt0", s, rows=8)
            P.op("act", lambda e: e.activation(out=dt8[:], in_=dt8[:], func=AF.Exp, bias=dtb, scale=1.0), reads=[b8, C.b_pvec], writes=[b8])
            P.op("act", lambda e: e.activation(out=dt8[:], in_=dt8[:], func=AF.Ln, bias=onecol[0:8, :], scale=1.0), reads=[b8, C.b_cst], writes=[b8])
            P.op("dve", lambda e: e.tensor_scalar(out=dA8[:], in0=dt8[:], scalar1=nega, scalar2=None, op0=ALU.mult), reads=[b8, C.b_aux], writes=[b8])
            P.op("dve", lambda e: e.tensor_tensor_scan(out=ac8[:], data0=rm8[:], data1=dA8[:], initial=0.0, op0=ALU.mult, op1=ALU.add), reads=[b8], writes=[b8])
            P.op("act", lambda e: e.activation(out=ec8[:], in_=ac8[:], func=AF.Exp), reads=[b8], writes=[b8])
            a3 = ac8[:].rearrange("p (n c) -> p n c", c=128)
            P.op("dve", lambda e, a3=a3: e.tensor_tensor(out=dA8[:].rearrange("p (n c) -> p n c", c=128), in0=a3[:, :, 127:128].broadcast_to([8, NCk, 128]), in1=a3, op=ALU.subtract),
                 reads=[b8], writes=[b8])
            P.op("act", lambda e: e.activation(out=dA8[:], in_=dA8[:], func=AF.Exp), reads=[b8], writes=[b8])
            for (src, dst) in ((ac8, acol), (dA8, dcol)):
                for c4 in range(0, NCk, 4):
                    cnt = min(4, NCk - c4)
                    def tr(e, src=src, c4=c4, cnt=cnt):
                        for j in range(cnt):
                            ins = e.transpose(out=ps[:, 2, j * 8:(j + 1) * 8], in_=src[0:8, (c4 + j) * 128:(c4 + j + 1) * 128], identity=C.ident[0:8, 0:8])
                        return ins
                    P.op("pe", tr, reads=[b8, C.b_cst], writes=[pb[2]])
                    P.op("dve", lambda e, dst=dst, c4=c4, cnt=cnt: e.tensor_copy(out=dst[:, c4:c4 + cnt, :], in_=ps[:, 2, 0:cnt * 8].rearrange("p (j h) -> p j h", j=cnt)),
                         reads=[pb[2]], writes=[bcol])
            e3 = ec8[:].rearrange("p (n c) -> p n c", c=128)
            P.op("dve", lambda e, e3=e3: e.tensor_tensor(out=R8[:], in0=C.ident[0:8, 0:8].unsqueeze(1).broadcast_to([8, NCk, 8]), in1=e3[:, :, 127:128].broadcast_to([8, NCk, 8]), op=ALU.mult),
                 reads=[b8, C.b_cst], writes=[bR8])
            P.op("pe", lambda e: e.matmul(ps[:, 3, 0:NCk * 8], lhsT=C.ones[0:8, :], rhs=R8[:].rearrange("p n h -> p (n h)"), start=True, stop=True), reads=[bR8, C.b_cst], writes=[pb[3]])
            P.op("dve", lambda e: e.tensor_copy(out=cdrow[:].rearrange("p n h -> p (n h)"), in_=ps[:, 3, 0:NCk * 8]), reads=[pb[3]], writes=[bcdrow])
            for pr in range(4):
                for t0 in range(0, S, 512):
                    n = min(512, S - t0)
                    so = CO["selE"] + pr * 128
                    P.op("pe", lambda e, so=so, t0=t0, n=n: e.matmul(ps[:, 3, 0:n], lhsT=cst[0:8, so:so + 128], rhs=dt8[:, t0:t0 + n], start=True, stop=True),
                         reads=[b8, C.b_cst], writes=[pb[3]])
                    P.op("dve", lambda e, pr=pr, t0=t0, n=n: e.tensor_tensor(out=acc[:, t0:t0 + n], in0=xs[pr][:, t0:t0 + n], in1=ps[:, 3, 0:n], op=ALU.mult),
                         reads=[pb[3], bxs[pr]], writes=[bacc])
                for c4 in range(0, NCk, 4):
                    cnt = min(4, NCk - c4)
                    bank = 2 + (c4 // 4) % 2
                    def tr(e, c4=c4, cnt=cnt, bank=bank):
                        for j in range(cnt):
                            ins = e.transpose(out=ps[:, bank, j * 128:(j + 1) * 128], in_=acc[:, (c4 + j) * 128:(c4 + j + 1) * 128], identity=C.ident)
                        return ins
                    P.op("pe", tr, reads=[bacc, C.b_cst], writes=[pb[bank]])
                    P.op("act", lambda e, c4=c4, cnt=cnt, bank=bank, pr=pr: e.copy(out=xdt_tok[:, c4:c4 + cnt, pr * 128:(pr + 1) * 128],
                                                                              in_=ps[:, bank, 0:cnt * 128].rearrange("p (j t) -> p j t", j=cnt)),
                         reads=[pb[bank]], writes=[bxdtt])
            for n in range(NCk - 1):
                sl = n % 2
                P.op("dve", lambda e, n=n, sl=sl: e.tensor_tensor(out=xdd[sl][:].rearrange("p (h q) -> p h q", h=8), in0=xdt_tok[:, n, :].rearrange("p (h q) -> p h q", h=8),
                                                               in1=dcol[:, n, :].unsqueeze(2).broadcast_to([128, 8, 64]), op=ALU.mult), reads=[bxdtt, bcol], writes=[bxdd[sl]])
                bank = 4 + n % 2
                def incm(e, n=n, sl=sl, bank=bank):
                    for g in range(2):
                        ins = e.matmul(ps[:, bank, g * 256:(g + 1) * 256], lhsT=Btok[:, n, g * 128:(g + 1) * 128], rhs=xdd[sl][:, g * 256:(g + 1) * 256], start=True, stop=True)
                    return ins
                P.op("pe", incm, reads=[bBtok, bxdd[sl]], writes=[pb[bank]])
                cur, prv = stt[n % 2], stt[(n + 1) % 2]
                if n == 0:
                    P.op("dve", lambda e, cur=cur, bank=bank: e.tensor_copy(out=cur[:], in_=ps[:, bank, :]), reads=[pb[bank]], writes=[bst[n % 2]])
                else:
                    P.op("dve", lambda e, n=n, cur=cur, prv=prv: e.tensor_tensor(out=cur[:].rearrange("p (h q) -> p h q", h=8), in0=prv[:].rearrange("p (h q) -> p h q", h=8),
                                                                              in1=cdrow[:, n, :].unsqueeze(2).broadcast_to([128, 8, 64]), op=ALU.mult),
                         reads=[bst[(n + 1) % 2], bcdrow], writes=[bst[n % 2]])
                    P.op("dve", lambda e, cur=cur, bank=bank: e.tensor_tensor(out=cur[:], in0=cur[:], in1=ps[:, bank, :], op=ALU.add), reads=[pb[bank], bst[n % 2]], writes=[bst[n % 2]])
                P.op("act", lambda e, n=n, cur=cur: e.copy(out=prevb[:, n + 1, :], in_=cur[:]), reads=[bst[n % 2]], writes=[bprev])
            for n in range(NCk):
                tsl = slice(n * 128, (n + 1) * 128)
                zs = n % 2
                for pr in range(4):
                    pj_load(C, zt[zs][:, pr, :], bzt[zs], "dz%d" % pr, s, t0=n * 128, nt=128)
                for g in range(2):
                    cs = (n * 2 + g) % 2
                    k = cs
                    P.op("pe", lambda e, g=g, tsl=tsl: e.matmul(ps[:, 0, 0:128], lhsT=Bb[g][:, tsl], rhs=Cb[g][:, tsl], start=True, stop=True), reads=[bBb[g], bCb[g]], writes=[pb[0]])
                    P.op("dve", lambda e, cs=cs: e.tensor_tensor(out=cbm[cs][:], in0=ps[:, 0, 0:128], in1=maskS, op=ALU.mult), reads=[pb[0], C.b_cst], writes=[bcbm[cs]])

                    def lrow(e, g=g, tsl=tsl):
                        for hh in range(4):
                            so = CO["selH"] + (g * 4 + hh) * 128
                            ins = e.matmul(ps[:, 1, hh * 128:(hh + 1) * 128], lhsT=cst[0:8, so:so + 128], rhs=ac8[:, tsl], start=True, stop=True)
                        return ins
                    P.op("pe", lrow, reads=[b8, C.b_cst], writes=[pb[1]])
                    P.op("dve", lambda e, k=k, n=n, g=g: e.tensor_tensor(out=Dm4[k][:], in0=ps[:, 1, :].rearrange("p (h i) -> p h i", h=4),
                                                                      in1=acol[:, n, g * 4:(g + 1) * 4].unsqueeze(2).broadcast_to([128, 4, 128]), op=ALU.subtract),
                         reads=[pb[1], bcol], writes=[bDm[k]])
                    P.op("dve", lambda e, k=k: e.tensor_scalar(out=Dm4[k][:], in0=Dm4[k][:], scalar1=0.0, scalar2=None, op0=ALU.min), reads=[bDm[k]], writes=[bDm[k]])
                    P.op("act", lambda e, k=k: e.activation(out=Dm4[k][:], in_=Dm4[k][:], func=AF.Exp), reads=[bDm[k]], writes=[bDm[k]])
                    P.op("dve", lambda e, k=k, cs=cs: e.tensor_tensor(out=Mh4[k][:], in0=Dm4[k][:], in1=cbm[cs][:].unsqueeze(1).broadcast_to([128, 4, 128]), op=ALU.mult),
                         reads=[bDm[k], bcbm[cs]], writes=[bMh[k]])

                    def ydiag(e, g=g, n=n, k=k):
                        for hh in range(4):
                            h = g * 4 + hh
                            pr = h // 2
                            ins = e.matmul(ps[(h % 2) * 64:(h % 2) * 64 + 64, 6, pr * 128:(pr + 1) * 128], lhsT=xdt_tok[:, n, h * 64:(h + 1) * 64], rhs=Mh4[k][:, hh, :],
                                           start=True, stop=True)
                        return ins
                    P.op("pe", ydiag, reads=[bxdtt, bMh[k]], writes=[pb[6]])
                P.op("dve", lambda e: e.tensor_copy(out=yo[:], in_=ps[:, 6, :].rearrange("p (r i) -> p r i", r=4)), reads=[pb[6]], writes=[byo])
                if n > 0:
                    def ecm(e, tsl=tsl):
                        for pr in range(4):
                            so = CO["selE"] + pr * 128
                            ins = e.matmul(ps[:, 3, pr * 128:(pr + 1) * 128], lhsT=cst[0:8, so:so + 128], rhs=ec8[:, tsl], start=True, stop=True)
                        return ins
                    P.op("pe", ecm, reads=[b8, C.b_cst], writes=[pb[3]])
                    P.op("act", lambda e: e.copy(out=ecb4[:], in_=ps[:, 3, :].rearrange("p (r i) -> p r i", r=4)), reads=[pb[3]], writes=[becb[0]])

                    def yoff(e, n=n, tsl=tsl):
                        for pr in range(4):
                            ins = e.matmul(ps[:, 7, pr * 128:(pr + 1) * 128], lhsT=prevb[:, n, pr * 128:(pr + 1) * 128], rhs=Cb[pr // 2][:, tsl], start=True, stop=True)
                        return ins
                    P.op("pe", yoff, reads=[bprev, bCb[0], bCb[1]], writes=[pb[7]])
                    P.op("dve", lambda e: e.tensor_tensor(out=ecb4[:], in0=ecb4[:], in1=ps[:, 7, :].rearrange("p (r i) -> p r i", r=4), op=ALU.mult), reads=[pb[7], becb[0]], writes=[becb[0]])
                    P.op("dve", lambda e: e.tensor_tensor(out=yo[:], in0=yo[:], in1=ecb4[:], op=ALU.add), reads=[byo, becb[0]], writes=[byo])
                for pr in range(4):
                    dc = pv_col(l, "ssd_d", pr)
                    P.op("dve", lambda e, pr=pr, dc=dc, tsl=tsl: e.scalar_tensor_tensor(out=yo[:, pr, :], in0=xs[pr][:, tsl], scalar=C.pvec[:, dc:dc + 1], in1=yo[:, pr, :], op0=ALU.mult, op1=ALU.add),
                         reads=[byo, bxs[pr], C.b_pvec], writes=[byo])
                P.op("act", lambda e, zs=zs: e.activation(out=zt[zs][:], in_=zt[zs][:], func=AF.Silu), reads=[bzt[zs]], writes=[bzt[zs]])
                P.op("dve", lambda e, zs=zs: e.tensor_tensor(out=yo[:], in0=yo[:], in1=zt[zs][:], op=ALU.mult), reads=[byo, bzt[zs]], writes=[byo])
                P.op("act", lambda e: e.activation(out=ysq[:], in_=yo[:], func=AF.Square), reads=[byo], writes=[bysq])
                for g in range(2):
                    colnorm_rstd(C, [ysq[:, 2 * g, :], ysq[:, 2 * g + 1, :]], bysq, 256.0, rstd, brstd, 128, bank=2)
                    for pr in (2 * g, 2 * g + 1):
                        gc = pv_col(l, "ssd_ng", pr)
                        P.op("dve", lambda e, pr=pr, gc=gc: e.scalar_tensor_tensor(out=ob[:, pr, :], in0=yo[:, pr, :], scalar=C.pvec[:, gc:gc + 1], in1=rstd[:, 0:128], op0=ALU.mult, op1=ALU.mult),
                             reads=[byo, brstd, C.b_pvec], writes=[bob])
                P.dma("sp", lambda e, n=n, s=s: e.dma_start(out=C.OM[1536:2048, s * S + n * 128:s * S + (n + 1) * 128].rearrange("(r p) t -> p r t", p=128), in_=ob[:]),
                      reads=[bob], writes=[C.omb[12 + r][(s * S + n * 128) // TB] for r in range(4)])
        P.barrier()


def merge_phase(C, l, wbin, bwin, wbbr, bwbr, wbout, bwout):
    nc, P, ps, pb = C.nc, C.P, C.ps, C.pb
    with contextlib.ExitStack() as st:
        def sb(name, shape, dt):
            return st.enter_context(nc.sbuf_tensor(uname(name), list(shape), dt))
        hf = sb("m_hf", [128, KC, TB], F32)
        hb = sb("m_hb", [128, KC, TB], BF16)
        stage = [sb("m_stg%d" % i, [128, 2, TB], F32) for i in range(2)]
        om = sb("m_om", [128, 16, TB], BF16)
        mg = sb("m_mg", [128, 4, TB], F32)
        mgb = sb("m_mgb", [128, KC, TB], BF16)
        sgm = sb("m_sgm", [128, 4, TB], F32)
        slots = [sb("m_w%d" % i, [128, 16, 512], BF16) for i in range(3)]
        bsl = [Buf() for _ in range(3)]
        sq = [sb("m_sq%d" % i, [128, TB], BF16) for i in range(2)]
        bsq = [Buf(), Buf()]
        mean, msq, rstd = sb("m_mean", [128, TB], F32), sb("m_msq", [128, TB], F32), sb("m_rstd", [128, TB], F32)
        bhf, bhb, bom, bmgb, bst = Buf(), Buf(), Buf(), Buf(), Buf()
        bstage = [Buf(), Buf()]
        bmg = [Buf() for _ in range(4)]
        bsg = [Buf() for _ in range(4)]
        cnt = [0]
        W = WStream(C, slots, bsl)
        prefetch_hb(C, 0, hb, bhb, stage, bstage, cnt)
        for blk in range(C.NB):
            for fg in range(4):
                for br in range(4):
                    def gstep(s, fg=fg, br=br, blk=blk):
                        if fg == 0 and br == 0:
                            P.dma("sp", lambda e: e.dma_start(
                                out=om[:], in_=C.OM[:, blk * TB:(blk + 1) * TB].rearrange("(k p) t -> p k t", p=128)), reads=[C.omb[r][blk] for r in range(16)], writes=[bom])
                        if fg == 2 and br == 0:
                            P.dma("sp", lambda e: e.dma_start(
                                out=hf[:], in_=C.HT[:, blk * TB:(blk + 1) * TB].rearrange("(k p) t -> p k t", p=128)), reads=[C.hbuf[blk]], writes=[bhf])

                        def gm(e):
                            for k in range(KC):
                                for j in range(4):
                                    ins = e.matmul(ps[:, j, :], lhsT=slots[s][:, k, j * 128:(j + 1) * 128], rhs=hb[:, k, :], start=(k == 0), stop=(k == KC - 1))
                            return ins
                        P.op("pe", gm, reads=[bsl[s], bhb], writes=[pb[j] for j in range(4)])
                    W.add((wbin, bwin, 0, 16, GATE0 + br * 2048 + fg * 512, 512), gstep)

                    def bstep(s2, fg=fg, br=br):
                        def bm(e):
                            for k in range(4):
                                for j in range(4):
                                    ins = e.matmul(ps[:, 4 + j, :], lhsT=slots[s2][:, k, j * 128:(j + 1) * 128], rhs=om[:, br * 4 + k, :], start=(k == 0), stop=(k == 3))
                            return ins
                        P.op("pe", bm, reads=[bsl[s2], bom], writes=[pb[4 + j] for j in range(4)])
                        for j in range(4):
                            bc = pv_col(l, "b_gate", br * 16 + fg * 4 + j)
                            P.op("act", lambda e, j=j, bc=bc: e.activation(out=sgm[:, j, :], in_=ps[:, j, :], func=AF.Sigmoid, bias=C.pvec[:, bc:bc + 1], scale=1.0),
                                 reads=[pb[j], C.b_pvec], writes=[bsg[j]])
                            if br == 0:
                                P.op("dve", lambda e, j=j: e.tensor_tensor(out=mg[:, j, :], in0=ps[:, 4 + j, :], in1=sgm[:, j, :], op=ALU.mult), reads=[pb[4 + j], bsg[j]], writes=[bmg[j]])
                            else:
                                P.op("dve", lambda e, j=j: e.tensor_tensor(out=sgm[:, j, :], in0=ps[:, 4 + j, :], in1=sgm[:, j, :], op=ALU.mult), reads=[pb[4 + j], bsg[j]], writes=[bsg[j]])
                                if br < 3:
                                    P.op("dve", lambda e, j=j: e.tensor_tensor(out=mg[:, j, :], in0=mg[:, j, :], in1=sgm[:, j, :], op=ALU.add), reads=[bmg[j], bsg[j]], writes=[bmg[j]])
                                else:
                                    P.op("dve", lambda e, j=j: e.tensor_tensor(out=mgb[:, fg * 4 + j, :], in0=mg[:, j, :], in1=sgm[:, j, :], op=ALU.add), reads=[bmg[j], bsg[j]], writes=[bmgb])
                    W.add((wbbr, bwbr, br * 512, 4, fg * 512, 512), bstep)
            for cg in range(4):
                def ostep(s, cg=cg, blk=blk):
                    banks = [(cg % 2) * 4 + j for j in range(4)]

                    def omm(e):
                        for k in range(KC):
                            for j in range(4):
                                ins = e.matmul(ps[:, banks[j], :], lhsT=slots[s][:, k, j * 128:(j + 1) * 128], rhs=mgb[:, k, :], start=(k == 0), stop=(k == KC - 1))
                        return ins
                    P.op("pe", omm, reads=[bsl[s], bmgb], writes=[pb[b] for b in banks])
                    if cg == 0 and blk + 1 < C.NB:
                        prefetch_hb(C, blk + 1, hb, bhb, stage, bstage, cnt)
                    for j in range(4):
                        c = cg * 4 + j
                        P.op("dve", lambda e, c=c, b=banks[j]: e.scalar_tensor_tensor(out=hf[:, c, :], in0=hf[:, c, :], scalar=ALPHA, in1=ps[:, b, :], op0=ALU.mult, op1=ALU.add),
                             reads=[pb[banks[j]], bhf], writes=[bhf])
                    if cg == 3:
                        ln_block(C, hf, bhf, l, 1, blk, (sq, bsq, mean, msq, rstd, bst))
                W.add((wbout, bwout, 0, 16, cg * 512, 512), ostep)
        W.run()
        P.barrier()
```
